# Optimizing a Trainium2 kernel written in Bass

```python
import math
import jax, jax.numpy as jnp
from jax import lax
import numpy as np

D_MODEL = 2048
BATCH = 1
SEQ = 16384
DEPTH = 4

CHUNK = 64
Q_BLOCK = 128
ATTN_WIDTH = D_MODEL // 2
SSM_WIDTH = D_MODEL - ATTN_WIDTH
HEAD_DIM = 64
N_HEADS = ATTN_WIDTH // (2 * HEAD_DIM)
SSM_GROUP_CH = 16
N_SSM_GROUPS = SSM_WIDTH // SSM_GROUP_CH
SSM_STATE = 64
D_FF = 4 * D_MODEL
IN_COLS = 3 * ATTN_WIDTH + SSM_WIDTH
DEEPNORM_ALPHA = (2.0 * DEPTH) ** 0.25
DEEPNORM_BETA = (8.0 * DEPTH) ** -0.25
NORM_EPS = 1e-5

kernel_name = 'hymba_diffattn_s5_deepnorm_trunk'


def layer_norm(x, g, b):
    xf = x.astype(jnp.float32)
    mu = jnp.mean(xf, axis=-1, keepdims=True)
    var = jnp.mean(jnp.square(xf - mu), axis=-1, keepdims=True)
    y = (xf - mu) * lax.rsqrt(var + NORM_EPS) * g.astype(jnp.float32) + b.astype(jnp.float32)
    return y.astype(x.dtype)


def group_rms_norm(x, g):
    xf = x.astype(jnp.float32)
    y = xf * lax.rsqrt(jnp.mean(jnp.square(xf), axis=-1, keepdims=True) + NORM_EPS)
    return y * g.astype(jnp.float32)


def diff_attention(q, k, v, lam, lam_init, g_attn):
    bsz, seq_len, _ = q.shape
    n_blk = seq_len // Q_BLOCK
    q = q.reshape(bsz, seq_len, N_HEADS, 2, HEAD_DIM)
    k = k.reshape(bsz, seq_len, N_HEADS, 2, HEAD_DIM)
    v = v.reshape(bsz, seq_len, N_HEADS, 2 * HEAD_DIM)
    k_t = k.transpose(0, 2, 3, 1, 4)
    v_t = v.transpose(0, 2, 1, 3)
    q_blocks = q.reshape(bsz, n_blk, Q_BLOCK, N_HEADS, 2, HEAD_DIM).transpose(1, 0, 3, 4, 2, 5)
    key_chunk = jnp.arange(seq_len) // CHUNK
    scale = HEAD_DIM ** -0.5

    def one_block(args):
        q_blk, blk = args
        s = jnp.einsum('bhsqd,bhskd->bhsqk', q_blk, k_t).astype(jnp.float32) * scale
        q_chunk = (blk * Q_BLOCK + jnp.arange(Q_BLOCK)) // CHUNK
        allowed = key_chunk[None, :] <= q_chunk[:, None]
        s = jnp.where(allowed, s, -jnp.inf)
        p = jax.nn.softmax(s, axis=-1)
        p_diff = p[:, :, 0] - lam.astype(jnp.float32) * p[:, :, 1]
        return jnp.einsum('bhqk,bhkv->bhqv', p_diff.astype(v_t.dtype), v_t)

    o = lax.map(one_block, (q_blocks, jnp.arange(n_blk)))
    o = o.transpose(1, 0, 3, 2, 4).reshape(bsz, seq_len, N_HEADS, 2 * HEAD_DIM)
    o = group_rms_norm(o, g_attn.reshape(N_HEADS, 2 * HEAD_DIM)) * (1.0 - lam_init)
    return o.reshape(bsz, seq_len, ATTN_WIDTH).astype(q.dtype)


def _complex_affine_combine(e1, e2):
    a1r, a1i, b1r, b1i = e1
    a2r, a2i, b2r, b2i = e2
    ar = a2r * a1r - a2i * a1i
    ai = a2r * a1i + a2i * a1r
    br = a2r * b1r - a2i * b1i + b2r
    bi = a2r * b1i + a2i * b1r + b2i
    return (ar, ai, br, bi)


def s5_mixer(u, lam_re, lam_im, log_dt, b_re, b_im, c_re, c_im, d_skip, glu_w, glu_b):
    f32 = jnp.float32
    bsz, seq_len, _ = u.shape
    uf = u.astype(f32).reshape(bsz, seq_len, N_SSM_GROUPS, SSM_GROUP_CH)
    dt = jnp.exp(log_dt.astype(f32))[:, None]
    lr, li = lam_re.astype(f32), lam_im.astype(f32)
    mag = jnp.exp(lr * dt)
    ab_r, ab_i = mag * jnp.cos(li * dt), mag * jnp.sin(li * dt)
    den = lr * lr + li * li
    nr, ni = ab_r - 1.0, ab_i
    fr = (nr * lr + ni * li) / den
    fi = (ni * lr - nr * li) / den
    br_, bi_ = b_re.astype(f32), b_im.astype(f32)
    bb_r = fr[..., None] * br_ - fi[..., None] * bi_
    bb_i = fr[..., None] * bi_ + fi[..., None] * br_
    bu_r = jnp.einsum('blgc,gpc->blgp', uf, bb_r)
    bu_i = jnp.einsum('blgc,gpc->blgp', uf, bb_i)
    a_r = jnp.broadcast_to(ab_r, bu_r.shape)
    a_i = jnp.broadcast_to(ab_i, bu_i.shape)
    _, _, h_r, h_i = lax.associative_scan(_complex_affine_combine, (a_r, a_i, bu_r, bu_i), axis=1)
    y = (jnp.einsum('blgp,gcp->blgc', h_r, c_re.astype(f32))
         - jnp.einsum('blgp,gcp->blgc', h_i, c_im.astype(f32)))
    y = y + d_skip.astype(f32).reshape(N_SSM_GROUPS, SSM_GROUP_CH) * uf
    y = jax.nn.gelu(y.reshape(bsz, seq_len, SSM_WIDTH))
    y = y * jax.nn.sigmoid(y @ glu_w.astype(f32) + glu_b.astype(f32))
    return y.astype(u.dtype)


def setup_inputs(seed: int = 0) -> dict:
    key = jax.random.key(seed)
    ks = jax.random.split(key, 24)
    f32 = jnp.float32
    nrm = lambda k, shape, s: jax.random.normal(k, shape, f32) * s
    x = jax.random.normal(ks[0], (BATCH, SEQ, D_MODEL), f32)
    col_scale = jnp.concatenate([
        jnp.ones((2 * ATTN_WIDTH,), f32),
        jnp.full((ATTN_WIDTH,), DEEPNORM_BETA, f32),
        jnp.ones((SSM_WIDTH,), f32)])
    w_in = nrm(ks[1], (DEPTH, D_MODEL, IN_COLS), D_MODEL ** -0.5) * col_scale
    lambda_q1 = nrm(ks[2], (DEPTH, HEAD_DIM), 0.1)
    lambda_k1 = nrm(ks[3], (DEPTH, HEAD_DIM), 0.1)
    lambda_q2 = nrm(ks[4], (DEPTH, HEAD_DIM), 0.1)
    lambda_k2 = nrm(ks[5], (DEPTH, HEAD_DIM), 0.1)
    attn_norm_g = 1.0 + nrm(ks[6], (DEPTH, ATTN_WIDTH), 0.02)
    n_idx = jnp.arange(SSM_STATE, dtype=f32)
    ssm_lambda_re = -0.5 + nrm(ks[7], (DEPTH, N_SSM_GROUPS, SSM_STATE), 0.01)
    ssm_lambda_im = math.pi * n_idx + nrm(ks[8], (DEPTH, N_SSM_GROUPS, SSM_STATE), 0.01)
    ssm_log_dt = jax.random.uniform(ks[9], (DEPTH, N_SSM_GROUPS), f32, math.log(1e-3), math.log(1e-1))
    b_s = (2.0 * SSM_GROUP_CH) ** -0.5
    ssm_b_re = nrm(ks[10], (DEPTH, N_SSM_GROUPS, SSM_STATE, SSM_GROUP_CH), b_s)
    ssm_b_im = nrm(ks[11], (DEPTH, N_SSM_GROUPS, SSM_STATE, SSM_GROUP_CH), b_s)
    c_s = SSM_STATE ** -0.5
    ssm_c_re = nrm(ks[12], (DEPTH, N_SSM_GROUPS, SSM_GROUP_CH, SSM_STATE), c_s)
    ssm_c_im = nrm(ks[13], (DEPTH, N_SSM_GROUPS, SSM_GROUP_CH, SSM_STATE), c_s)
    ssm_d = nrm(ks[14], (DEPTH, SSM_WIDTH), 1.0)
    glu_w = nrm(ks[15], (DEPTH, SSM_WIDTH, SSM_WIDTH), SSM_WIDTH ** -0.5)
    glu_b = nrm(ks[16], (DEPTH, SSM_WIDTH), 0.02)
    ssm_norm_g = 1.0 + nrm(ks[17], (DEPTH, SSM_WIDTH), 0.02)
    w_out = nrm(ks[18], (DEPTH, D_MODEL, D_MODEL), D_MODEL ** -0.5 * DEEPNORM_BETA)
    ln1_g = 1.0 + nrm(ks[19], (DEPTH, D_MODEL), 0.02)
    ln1_b = nrm(ks[20], (DEPTH, D_MODEL), 0.02)
    w_up = nrm(ks[21], (DEPTH, D_MODEL, D_FF), D_MODEL ** -0.5 * DEEPNORM_BETA)
    w_down = nrm(ks[22], (DEPTH, D_FF, D_MODEL), D_FF ** -0.5 * DEEPNORM_BETA)
    k_ln = jax.random.split(ks[23], 2)
    ln2_g = 1.0 + nrm(k_ln[0], (DEPTH, D_MODEL), 0.02)
    ln2_b = nrm(k_ln[1], (DEPTH, D_MODEL), 0.02)
    return {'x': x, 'w_in': w_in, 'lambda_q1': lambda_q1, 'lambda_k1': lambda_k1,
            'lambda_q2': lambda_q2, 'lambda_k2': lambda_k2, 'attn_norm_g': attn_norm_g,
            'ssm_lambda_re': ssm_lambda_re, 'ssm_lambda_im': ssm_lambda_im, 'ssm_log_dt': ssm_log_dt,
            'ssm_b_re': ssm_b_re, 'ssm_b_im': ssm_b_im, 'ssm_c_re': ssm_c_re, 'ssm_c_im': ssm_c_im,
            'ssm_d': ssm_d, 'glu_w': glu_w, 'glu_b': glu_b, 'ssm_norm_g': ssm_norm_g,
            'w_out': w_out, 'ln1_g': ln1_g, 'ln1_b': ln1_b, 'w_up': w_up, 'w_down': w_down,
            'ln2_g': ln2_g, 'ln2_b': ln2_b}


def reference(x, w_in, lambda_q1, lambda_k1, lambda_q2, lambda_k2, attn_norm_g,
              ssm_lambda_re, ssm_lambda_im, ssm_log_dt, ssm_b_re, ssm_b_im, ssm_c_re, ssm_c_im,
              ssm_d, glu_w, glu_b, ssm_norm_g, w_out, ln1_g, ln1_b, w_up, w_down, ln2_g, ln2_b):
    bsz, seq_len, _ = x.shape
    for l in range(DEPTH):
        lam_init = 0.8 - 0.6 * math.exp(-0.3 * l)
        lam = (jnp.exp(jnp.sum(lambda_q1[l].astype(jnp.float32) * lambda_k1[l].astype(jnp.float32)))
               - jnp.exp(jnp.sum(lambda_q2[l].astype(jnp.float32) * lambda_k2[l].astype(jnp.float32)))
               + lam_init)
        proj = x @ w_in[l]
        q = proj[..., :ATTN_WIDTH]
        k = proj[..., ATTN_WIDTH:2 * ATTN_WIDTH]
        v = proj[..., 2 * ATTN_WIDTH:3 * ATTN_WIDTH]
        u = proj[..., 3 * ATTN_WIDTH:]
        attn_out = diff_attention(q, k, v, lam, lam_init, attn_norm_g[l])
        ssm_out = s5_mixer(u, ssm_lambda_re[l], ssm_lambda_im[l], ssm_log_dt[l], ssm_b_re[l], ssm_b_im[l],
                           ssm_c_re[l], ssm_c_im[l], ssm_d[l], glu_w[l], glu_b[l])
        ssm_out = group_rms_norm(ssm_out.reshape(bsz, seq_len, N_SSM_GROUPS, SSM_GROUP_CH),
                                 ssm_norm_g[l].reshape(N_SSM_GROUPS, SSM_GROUP_CH))
        ssm_out = ssm_out.reshape(bsz, seq_len, SSM_WIDTH).astype(x.dtype)
        mix = jnp.concatenate([attn_out, ssm_out], axis=-1) @ w_out[l]
        x = layer_norm(DEEPNORM_ALPHA * x + mix, ln1_g[l], ln1_b[l])
        hid = jnp.square(jax.nn.relu(x @ w_up[l]))
        x = layer_norm(DEEPNORM_ALPHA * x + hid @ w_down[l], ln2_g[l], ln2_b[l])
    return x
```

```python
import numpy as np
from contextlib import ExitStack
import concourse.bass as bass
import concourse.mybir as mybir

ACT = mybir.ActivationFunctionType
ALU = mybir.AluOpType
AX = mybir.AxisListType
F32 = mybir.dt.float32
BF16 = mybir.dt.bfloat16
I32 = mybir.dt.int32

SEM_ROLL = 30000
DMA_RING = 8


class Buf:
    __slots__ = ("name", "w", "r")

    def __init__(self, name=""):
        self.name = name
        self.w = None
        self.r = {}


class Ctx:
    ENG = ("pe", "act", "dve", "pool", "sp")

    def __init__(self, nc, same_engine_sync=True):
        self.nc = nc
        self.es = ExitStack()
        self.stream = {e: [] for e in self.ENG}
        self.same_engine_sync = same_engine_sync
        self.nsem = 0
        self.cur = {}
        for e in ("pe", "act", "dve", "pool"):
            self.cur[e] = [self._newsem(e), 0]
        self.ring = {}
        self.ringpos = {}
        for q in ("sp", "act", "pool"):
            self.ring[q] = [[self._newsem("dq" + q), 0] for _ in range(DMA_RING)]
            self.ringpos[q] = 0
        self.waited = {}
        self.n_inst = 0
        self.n_wait = 0
        self.scopes = []

    def _newsem(self, name):
        self.nsem += 1
        return self.es.enter_context(self.nc.semaphore(f"s{self.nsem}_{name}"))

    def push_scope(self):
        self.scopes.append(ExitStack())

    def pop_scope(self):
        self.barrier()
        self.scopes.pop().close()

    def barrier(self):
        toks = []
        for e, (sem, cnt) in self.cur.items():
            if cnt > 0:
                toks.append((sem, cnt))
        for q, ring in self.ring.items():
            for sem, cnt in ring:
                if cnt > 0:
                    toks.append((sem, cnt))
        for e in self.ENG:
            own = self.cur[e][0] if e in self.cur else None
            self._need(e, [t for t in toks if t[0] is not own])

    def sbuf(self, name, shape, dt):
        es = self.scopes[-1] if self.scopes else self.es
        return es.enter_context(self.nc.sbuf_tensor(name, list(shape), dt))

    def psum(self, name, shape, dt=F32):
        return self.es.enter_context(self.nc.psum_tensor(name, list(shape), dt))

    def _need(self, eng, deps):
        best = {}
        for tok in deps:
            if tok is None:
                continue
            sem, val = tok
            k = id(sem)
            if k not in best or best[k][1] < val:
                best[k] = (sem, val)
        for k, (sem, val) in best.items():
            if self.waited.get((eng, k), 0) >= val:
                continue
            self.waited[(eng, k)] = val
            self.n_wait += 1
            self.stream[eng].append(lambda E, sem=sem, val=val: E.wait_ge(sem, val))

    def _deps(self, eng, reads, writes):
        own = self.cur[eng][0] if eng in self.cur else None
        deps = []
        for b in reads:
            if b.w is not None:
                deps.append(b.w)
        for b in writes:
            if b.w is not None:
                deps.append(b.w)
            deps.extend(b.r.values())
        if own is not None and (eng == "pe" or not self.same_engine_sync):
            deps = [d for d in deps if d[0] is not own]
        return deps

    def _mark(self, tok, reads, writes):
        for b in writes:
            b.w = tok
            b.r = {}
        for b in reads:
            b.r[id(tok[0])] = tok

    def op(self, eng, fn, reads=(), writes=()):
        self._need(eng, self._deps(eng, reads, writes))
        c = self.cur[eng]
        if c[1] >= SEM_ROLL:
            c[0] = self._newsem(eng)
            c[1] = 0
        c[1] += 1
        sem, val = c[0], c[1]
        self.n_inst += 1
        self.stream[eng].append(lambda E, fn=fn, sem=sem: fn(E).then_inc(sem, 1))
        self._mark((sem, val), reads, writes)

    def dma(self, q, out, in_, reads=(), writes=(), **kw):
        ring = self.ring[q]
        pos = self.ringpos[q]
        self.ringpos[q] = (pos + 1) % len(ring)
        slot = ring[pos]
        deps = self._deps(q, reads, writes)
        if slot[1] > 0:
            deps.append((slot[0], slot[1]))
        self._need(q, deps)
        slot[1] += 16
        sem, val = slot[0], slot[1]
        self.n_inst += 1
        self.stream[q].append(
            lambda E, out=out, in_=in_, sem=sem, kw=kw: E.dma_start(out=out, in_=in_, **kw).then_inc(sem, 16))
        self._mark((sem, val), reads, writes)

    def finish(self, bufs, eng="sp"):
        self._need(eng, [b.w for b in bufs])

    def emit(self):
        nc = self.nc
        st = self.stream
        with nc.Block() as block:
            @block.sync
            def _(E):
                for f in st["sp"]:
                    f(E)

            @block.tensor
            def _(E):
                for f in st["pe"]:
                    f(E)

            @block.scalar
            def _(E):
                for f in st["act"]:
                    f(E)

            @block.vector
            def _(E):
                for f in st["dve"]:
                    f(E)

            @block.gpsimd
            def _(E):
                for f in st["pool"]:
                    f(E)
        self.es.close()


import math
import numpy as np

D = 2048
DFF = 8192
TOK = 2048
T = 512
SEQ = 16384
DEPTH = 4
ALPHA = (2.0 * DEPTH) ** 0.25
EPS = 1e-5
NW = 6


class Shared:
    def __init__(self, C):
        self.C = C
        self.ps = []
        for i in range(8):
            self.ps.append((C.psum(f"ps{i}", [128, 512], F32), Buf(f"ps{i}")))
        self.pset_i = 0
        self.w = [(C.sbuf(f"w{i}", [128, 4, 512], BF16), Buf(f"w{i}")) for i in range(NW)]
        self.w_i = 0

    def next_pset(self):
        s = self.ps[self.pset_i * 4:(self.pset_i + 1) * 4]
        self.pset_i ^= 1
        return s

    def next_w(self):
        r = self.w[self.w_i]
        self.w_i = (self.w_i + 1) % NW
        return r


def _mm(ps, lhsT, rhs, start, stop):
    return lambda E: E.matmul(ps, lhsT, rhs, start=start, stop=stop)


def stream_mm(C, S, W, K, M, rhs, rhs_bufs, evac):
    ncg = M // 512
    nks = K // 512
    nk = K // 128
    for cg in range(ncg):
        pset = S.next_pset()
        for ks in range(nks):
            wt, wb = S.next_w()
            src = W[ks * 512:(ks + 1) * 512, cg * 512:(cg + 1) * 512].rearrange("(kc p) n -> p kc n", p=128)
            C.dma("pool", wt[:], src, writes=[wb])
            for kc in range(4):
                kk = ks * 4 + kc
                for m in range(4):
                    ps, pb = pset[m]
                    C.op("pe", _mm(ps[:, :], wt[:, kc, m * 128:(m + 1) * 128], rhs(kk), kk == 0, kk == nk - 1),
                         reads=[wb] + rhs_bufs(kk), writes=[pb])
        for m in range(4):
            evac(cg * 4 + m, pset[m][0], pset[m][1])


def load_vec(C, name, ap, ncols):
    t = C.sbuf(name, [128, ncols], F32)
    b = Buf(name)
    C.dma("sp", t[:], ap, writes=[b])
    return t, b


def phase_A(C, S, d, out_bufs):
    xr = C.sbuf("a_xr", [128, 16, T], F32); Bxr = Buf()
    xb = C.sbuf("a_xb", [128, 16, T], BF16); Bxb = [Buf() for _ in range(16)]
    stg = [(C.sbuf(f"a_stg{i}", [128, T], BF16), Buf()) for i in range(4)]
    for t in range(TOK // T):
        tok = slice(t * T, (t + 1) * T)
        C.dma("sp", xr[:], d["xT"][:, tok].rearrange("(kc p) n -> p kc n", p=128), writes=[Bxr])
        for m in range(16):
            C.op("act", lambda E, m=m: E.activation(out=xb[:, m, :], in_=xr[:, m, :], func=ACT.Copy),
                 reads=[Bxr], writes=[Bxb[m]])
        proj_stage(C, S, d["w_in"], d["projT"], xb, Bxb, stg, tok, out_bufs)


def proj_stage(C, S, w_in, projT, xb, Bxb, stg, tok, out_bufs):
    cnt = [0]

    def evac(m, ps, pb):
        st, sb = stg[cnt[0] % len(stg)]
        cnt[0] += 1
        C.op("act", lambda E: E.activation(out=st[:], in_=ps[:], func=ACT.Copy), reads=[pb], writes=[sb])
        ob = Buf()
        out_bufs.append(ob)
        C.dma("sp", projT[m * 128:(m + 1) * 128, tok], st[:], reads=[sb], writes=[ob])
    stream_mm(C, S, w_in, D, 4096, lambda kk: xb[:, kk, :], lambda kk: [Bxb[kk]], evac)


def phase_C(C, S, d, do_proj, out_bufs):
    nc = C.nc
    xr = C.sbuf("c_xr", [128, 16, T], F32); Bxr = [Buf() for _ in range(16)]
    xb = C.sbuf("c_xb", [128, 16, T], BF16); Bxb = [Buf() for _ in range(16)]
    sF32 = xb.bitcast(F32)

    def sFv(m):
        return sF32[:, 2 * m:2 * m + 2, :].rearrange("p a c -> p (a c)")
    cat = C.sbuf("c_cat", [128, 16, T], BF16); Bcat = [Buf() for _ in range(16)]
    yb = C.sbuf("c_yb", [128, 8, T], BF16); Byb = Buf()
    hb = C.sbuf("c_hb", [128, 64, T], BF16); Bhb = [Buf() for _ in range(64)]
    hF32 = hb.bitcast(F32)

    def hFv(m):
        return hF32[:, 2 * m:2 * m + 2, :].rearrange("p a c -> p (a c)")
    tmp = [(C.sbuf(f"c_tmp{i}", [128, T], F32), Buf()) for i in range(4)]
    mean_sb = C.sbuf("c_mean", [128, T], F32); Bmean = Buf()
    rstd = C.sbuf("c_rstd", [128, T], F32); Brstd = Buf()
    var_sb = C.sbuf("c_var", [128, T], F32); Bvar = Buf()
    stg = [(C.sbuf(f"c_stg{i}", [128, T], BF16), Buf()) for i in range(4)]
    glu_b, Bglu_b = load_vec(C, "c_glub", d["glu_b"], 8)
    ssm_g, Bssm_g = load_vec(C, "c_ssmg", d["ssm_g"], 8)
    ln1g, Bln1g = load_vec(C, "c_ln1g", d["ln1_g"], 16)
    ln1b, Bln1b = load_vec(C, "c_ln1b", d["ln1_b"], 16)
    ln2g, Bln2g = load_vec(C, "c_ln2g", d["ln2_g"], 16)
    ln2b, Bln2b = load_vec(C, "c_ln2b", d["ln2_b"], 16)
    bd16 = C.sbuf("c_bd16", [128, 128], F32); Bbd16 = Buf()
    C.dma("sp", bd16[:], d["bd16"], writes=[Bbd16])
    onesb = C.sbuf("c_ones", [128, 128], BF16); Bones = Buf()
    C.op("dve", lambda E: E.memset(onesb[:], 1.0 / D), writes=[Bones])
    tcnt = [0]

    def nexttmp():
        r = tmp[tcnt[0] % len(tmp)]
        tcnt[0] += 1
        return r

    def evac_res(sq, Bsq):
        def f(m, ps, pb):
            C.op("dve", lambda E: E.scalar_tensor_tensor(out=xr[:, m, :], in0=xr[:, m, :], scalar=ALPHA, in1=ps[:],
                                                          op0=ALU.mult, op1=ALU.add),
                 reads=[Bxr[m], pb], writes=[Bxr[m]])
            C.op("act", lambda E: E.activation(out=xb[:, m, :], in_=xr[:, m, :], func=ACT.Copy),
                 reads=[Bxr[m]], writes=[Bxb[m]])
            C.op("act", lambda E: E.activation(out=sq[:, m, :], in_=xr[:, m, :], func=ACT.Square),
                 reads=[Bxr[m]], writes=[Bsq[m]])
        return f

    def layer_norm(g, Bg, b, Bb, sq, Bsq):
        pset = S.next_pset()
        (psA, pbA), (psB, pbB) = pset[0], pset[1]
        for m in range(16):
            C.op("pe", _mm(psA[:, :], onesb[:, :], xb[:, m, :], m == 0, m == 15), reads=[Bones, Bxb[m]], writes=[pbA])
        for m in range(16):
            C.op("pe", _mm(psB[:, :], onesb[:, :], sq[:, m, :], m == 0, m == 15), reads=[Bones, Bsq[m]], writes=[pbB])
        C.op("dve", lambda E: E.tensor_copy(out=mean_sb[:], in_=psA[:]), reads=[pbA], writes=[Bmean])
        C.op("dve", lambda E: E.tensor_tensor(out=var_sb[:], in0=mean_sb[:], in1=mean_sb[:], op=ALU.mult),
             reads=[Bmean], writes=[Bvar])
        C.op("dve", lambda E: E.tensor_tensor(out=var_sb[:], in0=psB[:], in1=var_sb[:], op=ALU.subtract),
             reads=[pbB, Bvar], writes=[Bvar])
        C.op("dve", lambda E: E.tensor_scalar(out=var_sb[:], in0=var_sb[:], scalar1=EPS, scalar2=None, op0=ALU.add),
             reads=[Bvar], writes=[Bvar])
        C.op("act", lambda E: E.activation(out=rstd[:], in_=var_sb[:], func=ACT.Sqrt), reads=[Bvar], writes=[Brstd])
        C.op("dve", lambda E: E.reciprocal(out=rstd[:], in_=rstd[:]), reads=[Brstd], writes=[Brstd])
        for m in range(16):
            t1, b1 = nexttmp()
            C.op("dve", lambda E, m=m, t1=t1: E.tensor_tensor(out=t1[:], in0=xr[:, m, :], in1=mean_sb[:], op=ALU.subtract),
                 reads=[Bxr[m], Bmean], writes=[b1])
            C.op("dve", lambda E, m=m, t1=t1: E.tensor_tensor(out=t1[:], in0=t1[:], in1=rstd[:], op=ALU.mult),
                 reads=[b1, Brstd], writes=[b1])
            C.op("act", lambda E, m=m, t1=t1: E.activation(out=xr[:, m, :], in_=t1[:], func=ACT.Identity,
                                                            scale=g[:, m:m + 1], bias=b[:, m:m + 1]),
                 reads=[b1, Bg, Bb], writes=[Bxr[m]])
            C.op("act", lambda E, m=m: E.activation(out=xb[:, m, :], in_=xr[:, m, :], func=ACT.Copy),
                 reads=[Bxr[m]], writes=[Bxb[m]])

    def emit_loads(t):
        tok = slice(t * T, (t + 1) * T)
        C.dma("sp", yb[:], d["yT"][:, tok].rearrange("(kc p) n -> p kc n", p=128), writes=[Byb])
        for m in range(8):
            C.dma("sp", cat[:, m, :], d["attnT"][m * 128:(m + 1) * 128, tok], writes=[Bcat[m]])
        for m in range(16):
            C.dma("sp", xr[:, m, :], d["xres"][m * 128:(m + 1) * 128, tok], writes=[Bxr[m]])

    emit_loads(0)
    for t in range(TOK // T):
        tok = slice(t * T, (t + 1) * T)

        def evac_glu(m, ps, pb):
            t1, b1 = nexttmp()
            C.op("act", lambda E: E.activation(out=t1[:], in_=ps[:], func=ACT.Sigmoid, bias=glu_b[:, m:m + 1]),
                 reads=[pb, Bglu_b], writes=[b1])
            C.op("dve", lambda E: E.tensor_tensor(out=sFv(m), in0=yb[:, m, :], in1=t1[:], op=ALU.mult),
                 reads=[Byb, b1], writes=[Bxb[2 * m], Bxb[2 * m + 1]])
        stream_mm(C, S, d["glu_w"], 1024, 1024, lambda kk: yb[:, kk, :], lambda kk: [Byb], evac_glu)
        psA = S.next_pset()
        psB = S.next_pset()
        ps8 = psA + psB
        for m in range(8):
            sfb = [Bxb[2 * m], Bxb[2 * m + 1]]
            C.op("act", lambda E, m=m: E.activation(out=hFv(m), in_=sFv(m), func=ACT.Square), reads=sfb, writes=[Bhb[2 * m], Bhb[2 * m + 1]])
        for m in range(8):
            ps, pb = ps8[m]
            C.op("pe", _mm(ps[:, :], bd16[:, :], hFv(m), True, True), reads=[Bbd16, Bhb[2 * m], Bhb[2 * m + 1]], writes=[pb])
        for m in range(8):
            ps, pb = ps8[m]
            sfb = [Bxb[2 * m], Bxb[2 * m + 1]]
            t2, b2 = nexttmp()
            C.op("dve", lambda E, ps=ps, t2=t2: E.tensor_scalar(out=t2[:], in0=ps[:], scalar1=EPS, scalar2=None, op0=ALU.add),
                 reads=[pb], writes=[b2])
            C.op("act", lambda E, t2=t2: E.activation(out=t2[:], in_=t2[:], func=ACT.Sqrt), reads=[b2], writes=[b2])
            C.op("dve", lambda E, t2=t2: E.reciprocal(out=t2[:], in_=t2[:]), reads=[b2], writes=[b2])
            C.op("dve", lambda E, m=m, t2=t2: E.scalar_tensor_tensor(out=cat[:, 8 + m, :], in0=sFv(m), scalar=ssm_g[:, m:m + 1],
                                                                    in1=t2[:], op0=ALU.mult, op1=ALU.mult),
                 reads=sfb + [b2, Bssm_g], writes=[Bcat[8 + m]])
        sq1 = hb
        stream_mm(C, S, d["w_out"], D, D, lambda kk: cat[:, kk, :], lambda kk: [Bcat[kk]], evac_res(sq1, Bhb))
        layer_norm(ln1g, Bln1g, ln1b, Bln1b, sq1, Bhb)

        def evac_up(j, ps, pb):
            t1, b1 = nexttmp()
            C.op("act", lambda E: E.activation(out=t1[:], in_=ps[:], func=ACT.Relu), reads=[pb], writes=[b1])
            C.op("dve", lambda E: E.tensor_tensor(out=hb[:, j, :], in0=t1[:], in1=t1[:], op=ALU.mult), reads=[b1], writes=[Bhb[j]])
        stream_mm(C, S, d["w_up"], D, DFF, lambda kk: xb[:, kk, :], lambda kk: [Bxb[kk]], evac_up)
        stream_mm(C, S, d["w_down"], DFF, D, lambda kk: hb[:, kk, :], lambda kk: [Bhb[kk]], evac_res(cat, Bcat))
        layer_norm(ln2g, Bln2g, ln2b, Bln2b, cat, Bcat)
        for m in range(16):
            ob = Buf()
            out_bufs.append(ob)
            C.dma("sp", d["xout"][m * 128:(m + 1) * 128, tok], xr[:, m, :], reads=[Bxr[m]], writes=[ob])
        if t + 1 < TOK // T:
            emit_loads(t + 1)
        if do_proj:
            proj_stage(C, S, d["w_in"], d["projT"], xb, Bxb, stg, tok, out_bufs)


NSEG = SEQ // 512
TWO_PI_1 = 6.28125
TWO_PI_2 = 2.0 * math.pi - 6.28125
PI_LO = 3.1415925


def phase_B(C, d, out_bufs, do_attn=True, do_ssm=True):
    S0 = C.psum("b_S0", [128, 1024], F32); BS0 = [Buf(), Buf()]
    S1 = C.psum("b_S1", [128, 1024], F32); BS1 = [Buf(), Buf()]
    O = [(C.psum(f"b_O{a}", [128, 512], F32), Buf()) for a in range(2)]
    Lb = C.psum("b_L", [128, 512], F32); BL = Buf()
    Fb = C.psum("b_F", [128, 512], F32); BF = [Buf(), Buf()]
    Sset = [(S0, BS0), (S1, BS1)]

    onesb = C.sbuf("b_ones", [128, 128], BF16); Bones = Buf()
    C.op("dve", lambda E: E.memset(onesb[:], 1.0), writes=[Bones])
    onesf = C.sbuf("b_onesf", [128, 128], F32); Bonesf = Buf()
    C.op("dve", lambda E: E.memset(onesf[:], 1.0 / 128.0), writes=[Bonesf])
    identf = C.sbuf("b_identf", [128, 128], F32); Bidf = Buf()
    C.dma("sp", identf[:], d["ident"], writes=[Bidf])
    identb = C.sbuf("b_identb", [128, 128], BF16); Bidb = Buf()
    C.op("dve", lambda E: E.tensor_copy(out=identb[:], in_=identf[:]), reads=[Bidf], writes=[Bidb])
    cvec, Bcvec = load_vec(C, "b_cvec", d["cvec"], 4)

    steps = []
    if do_ssm:
        steps = ssm_setup(C, d, out_bufs, Fb, BF, O[1], identf, Bidf)

    if do_attn:
        qT = C.sbuf("b_qT", [128, SEQ], BF16); Bq = [Buf() for _ in range(4)]
        kT = C.sbuf("b_kT", [128, SEQ], BF16); Bk = [Buf() for _ in range(4)]
        V = C.sbuf("b_V", [128, 128, 128], BF16); BV = [Buf() for _ in range(16)]
        Vf = V[:, :, :].rearrange("p b d -> p (b d)")
        for i in range(4):
            C.dma("sp", qT[:, i * 4096:(i + 1) * 4096], d["qT"][:, i * 4096:(i + 1) * 4096], writes=[Bq[i]])
            C.dma("sp", kT[:, i * 4096:(i + 1) * 4096], d["kT"][:, i * 4096:(i + 1) * 4096], writes=[Bk[i]])
        qm = C.sbuf("b_qm", [128, 4], F32); Bqm = Buf()
        qkmax = C.sbuf("b_qkmax", [128, 64], F32); Bqkmax = Buf()
        C.push_scope()
        vst = [(C.sbuf(f"b_vst{i}", [128, 1024], BF16), Buf()) for i in range(2)]
        ptb = Fb.bitcast(BF16)
        for g in range(16):
            vs, bvs = vst[g % 2]
            C.dma("sp", vs[:], d["vT"][:, g * 1024:(g + 1) * 1024], writes=[bvs])
            for i in range(8):
                C.op("pe", lambda E, vs=vs, i=i: E.transpose(out=ptb[:, i * 128:(i + 1) * 128], in_=vs[:, i * 128:(i + 1) * 128],
                                                              identity=identb[:]),
                     reads=[bvs, Bidb], writes=BF)
            C.op("dve", lambda E, g=g: E.tensor_copy(out=Vf[:, g * 1024:(g + 1) * 1024], in_=ptb[:, :]),
                 reads=BF, writes=[BV[g]])
        sqb = [(C.sbuf(f"b_sqb{i}", [128, 512], BF16), Buf()) for i in range(2)]
        for which, (src, Bsrc) in enumerate(((qT, Bq), (kT, Bk))):
            for ch in range(32):
                sq, bsq = sqb[ch % 2]
                ps, pb = O[ch % 2]
                C.op("act", lambda E, sq=sq, src=src, ch=ch: E.activation(out=sq[:], in_=src[:, ch * 512:(ch + 1) * 512], func=ACT.Square),
                     reads=[Bsrc[ch // 8]], writes=[bsq])
                C.op("pe", _mm(ps[:, :], onesb[:, :], sq[:, :], True, True), reads=[Bones, bsq], writes=[pb])
                col = which * 32 + ch
                C.op("dve", lambda E, ps=ps, col=col: E.tensor_reduce(out=qkmax[:, col:col + 1], in_=ps[:], axis=AX.X, op=ALU.max),
                     reads=[pb], writes=[Bqkmax])
        C.op("dve", lambda E: E.tensor_reduce(out=qm[:, 0:1], in_=qkmax[:, 0:32], axis=AX.X, op=ALU.max), reads=[Bqkmax], writes=[Bqm])
        C.op("dve", lambda E: E.tensor_reduce(out=qm[:, 1:2], in_=qkmax[:, 32:64], axis=AX.X, op=ALU.max), reads=[Bqkmax], writes=[Bqm])
        C.op("dve", lambda E: E.tensor_tensor(out=qm[:, 2:3], in0=qm[:, 0:1], in1=qm[:, 1:2], op=ALU.mult), reads=[Bqm], writes=[Bqm])
        C.op("act", lambda E: E.activation(out=qm[:, 3:4], in_=qm[:, 2:3], func=ACT.Sqrt), reads=[Bqm], writes=[Bqm])
        C.pop_scope()
        negc = C.sbuf("b_negc", [128, 1], F32); Bnegc = Buf()
        C.op("dve", lambda E: E.tensor_scalar(out=negc[:], in0=qm[:, 3:4], scalar1=-1.02 * 0.125, scalar2=None, op0=ALU.mult),
             reads=[Bqm], writes=[Bnegc])
        lamv = C.sbuf("b_lamv", [128, 4, 64], F32); Blamv = Buf()
        C.dma("sp", lamv[:], d["lamv"], writes=[Blamv])
        lt = C.sbuf("b_lt", [128, 2, 64], F32); Blt = Buf()
        ls = C.sbuf("b_ls", [128, 8], F32); Bls = Buf()
        for i in range(2):
            C.op("dve", lambda E, i=i: E.tensor_tensor(out=lt[:, i, :], in0=lamv[:, 2 * i, :], in1=lamv[:, 2 * i + 1, :], op=ALU.mult),
                 reads=[Blamv], writes=[Blt])
            C.op("dve", lambda E, i=i: E.tensor_reduce(out=ls[:, i:i + 1], in_=lt[:, i, :], axis=AX.X, op=ALU.add), reads=[Blt], writes=[Bls])
        C.op("act", lambda E: E.activation(out=ls[:, 2:4], in_=ls[:, 0:2], func=ACT.Exp), reads=[Bls], writes=[Bls])
        C.op("dve", lambda E: E.tensor_tensor(out=ls[:, 4:5], in0=ls[:, 2:3], in1=ls[:, 3:4], op=ALU.subtract), reads=[Bls], writes=[Bls])
        C.op("dve", lambda E: E.tensor_tensor(out=ls[:, 5:6], in0=ls[:, 4:5], in1=cvec[:, 0:1], op=ALU.add), reads=[Bls, Bcvec], writes=[Bls])
        C.op("dve", lambda E: E.tensor_scalar(out=ls[:, 6:7], in0=ls[:, 5:6], scalar1=-1.0, scalar2=None, op0=ALU.mult), reads=[Bls], writes=[Bls])
        neglam = ls[:, 6:7]
        gat, Bgat = load_vec(C, "b_gat", d["gattn"], 1)
        C.op("dve", lambda E: E.tensor_tensor(out=ls[:, 7:8], in0=gat[:, 0:1], in1=cvec[:, 1:2], op=ALU.mult), reads=[Bgat, Bcvec, Bls], writes=[Bls])
        gcoef = ls[:, 7:8]
        selT = C.sbuf("b_sel", [128, 2, 128], F32); Bsel = Buf()
        C.op("dve", lambda E: E.memset(selT[:], 0.0), writes=[Bsel])
        C.op("dve", lambda E: E.memset(selT[0:1, 0, :], 1.0), writes=[Bsel])
        C.op("dve", lambda E: E.memset(selT[64:65, 1, :], 1.0), writes=[Bsel])

        Pset = [(C.sbuf(f"b_P{i}", [128, 1024], BF16), Buf()) for i in range(3)]
        obs = [(C.sbuf(f"b_ob{i}", [128, 512], BF16), Buf()) for i in range(2)]
        fin = [[(C.sbuf(f"b_fin{j}_{i}", [128, 512], F32), Buf()) for i in range(7)] for j in range(1)]
        items = []
        for qb in range(SEQ // 512):
            nkb = 4 * qb + 4
            for kb in range(nkb):
                items.append((qb, kb, nkb))
        pending = []

        def emit_qk(i):
            qb, kb, nkb = items[i]
            S, BS = Sset[i % 2]
            col0 = max(0, 128 * (kb - 4 * qb))
            q0 = qb * 512
            for a in range(2):
                C.op("pe", _mm(S[:, a * 512 + col0:(a + 1) * 512], kT[a * 64:(a + 1) * 64, kb * 128:(kb + 1) * 128],
                               qT[a * 64:(a + 1) * 64, q0 + col0:q0 + 512], True, True),
                     reads=[Bk[(kb * 128) // 4096], Bq[q0 // 4096]], writes=[BS[a]])

        def finalize(qb, i):
            q0 = qb * 512
            (Lsb, bLsb), (Os0, bOs0), (Os1, bOs1), (o, bo), (sq, bsq), (rs, brs), (r_, br_) = fin[0]
            Os = [(Os0, bOs0), (Os1, bOs1)]
            C.op("dve", lambda E: E.tensor_copy(out=Lsb[:, :], in_=Lb[:, :]), reads=[BL], writes=[bLsb])
            C.op("act", lambda E: E.activation(out=Os0[:], in_=O[0][0][:], func=ACT.Copy), reads=[O[0][1]], writes=[bOs0])
            C.op("dve", lambda E: E.tensor_copy(out=Os1[:], in_=O[1][0][:]), reads=[O[1][1]], writes=[bOs1])

            def stage_bcast(a):
                def f():
                    C.op("pe", _mm(Fb[:, :], selT[:, a, :], Lsb[:, :], True, True), reads=[Bsel, bLsb], writes=BF)
                    C.op("dve", lambda E: E.reciprocal(out=r_[:], in_=Fb[:]), reads=BF, writes=[br_])
                    C.op("dve", lambda E: E.tensor_tensor(out=Os[a][0][:], in0=Os[a][0][:], in1=r_[:], op=ALU.mult),
                         reads=[Os[a][1], br_], writes=[Os[a][1]])
                    if a == 1:
                        C.op("dve", lambda E: E.scalar_tensor_tensor(out=o[:], in0=Os1[:], scalar=neglam, in1=Os0[:], op0=ALU.mult, op1=ALU.add),
                             reads=[bOs0, bOs1, Bls], writes=[bo])
                return f

            def stage_sq():
                C.op("act", lambda E: E.activation(out=sq[:], in_=o[:], func=ACT.Square), reads=[bo], writes=[bsq])

            def stage_ms():
                C.op("pe", _mm(Fb[:, :], onesf[:, :], sq[:, :], True, True), reads=[Bonesf, bsq], writes=BF)
                C.op("dve", lambda E: E.tensor_scalar(out=rs[:], in0=Fb[:], scalar1=EPS, scalar2=None, op0=ALU.add), reads=BF, writes=[brs])

            def stage_sqrt():
                C.op("act", lambda E: E.activation(out=rs[:], in_=rs[:], func=ACT.Sqrt), reads=[brs], writes=[brs])

            def stage_out():
                C.op("dve", lambda E: E.reciprocal(out=rs[:], in_=rs[:]), reads=[brs], writes=[brs])
                ob, bob = obs[qb % 2]
                C.op("dve", lambda E: E.scalar_tensor_tensor(out=ob[:], in0=o[:], scalar=gcoef, in1=rs[:], op0=ALU.mult, op1=ALU.mult),
                     reads=[bo, brs, Bls], writes=[bob])
                dst = Buf()
                out_bufs.append(dst)
                C.dma("sp", d["attnT"][:, q0:q0 + 512], ob[:], reads=[bob], writes=[dst])
            dls = (1, 2, 3, 4, 5, 6) if qb == 0 else (3, 6, 9, 11, 13, 15)
            for dl, fn in zip(dls, (stage_bcast(0), stage_bcast(1), stage_sq, stage_ms, stage_sqrt, stage_out)):
                pending.append((i + dl, fn))
            pending.sort(key=lambda x: x[0])

        nsteps = len(steps)
        every = max(1, len(items) // max(1, nsteps + 4))

        def emit_exp(i):
            qb, kb, nkb = items[i]
            S, BS = Sset[i % 2]
            P, BP = Pset[i % 3]
            col0 = max(0, 128 * (kb - 4 * qb))
            Sv = S[:, :].rearrange("p (a c) -> p a c", a=2)[:, :, col0:512]
            Pv = P[:, :].rearrange("p (a c) -> p a c", a=2)[:, :, col0:512]
            C.op("act", lambda E: E.activation(out=Pv, in_=Sv, func=ACT.Exp, bias=negc[:, 0:1], scale=0.125),
                 reads=[BS[0], BS[1], Bnegc], writes=[BP])
            if kb >= 4 * qb:
                Pm = P[64:128, :].rearrange("p (a c) -> p a c", a=2)[:, :, col0:col0 + 64]
                C.op("act", lambda E: E.activation(out=Pm, in_=Pm, func=ACT.Copy, scale=0.0), reads=[BP], writes=[BP])

        def emit_pv(i):
            qb, kb, nkb = items[i]
            P, BP = Pset[i % 3]
            col0 = max(0, 128 * (kb - 4 * qb))
            for a in range(2):
                for j in range(2):
                    C.op("pe", _mm(O[a][0][64 * j:64 * j + 64, col0:512], V[:, kb, 64 * j:64 * j + 64], P[:, a * 512 + col0:(a + 1) * 512],
                                   kb == 0, kb == nkb - 1),
                         reads=[BV[kb // 8], BP], writes=[O[a][1]])
            for a in range(2):
                C.op("pe", _mm(Lb[64 * a:64 * a + 64, col0:512], onesb[:, 0:64], P[:, a * 512 + col0:(a + 1) * 512], kb == 0, kb == nkb - 1),
                     reads=[Bones, BP], writes=[BL])
            if kb == nkb - 1:
                finalize(qb, i)

        emit_qk(0)
        for i in range(len(items)):
            if i + 1 < len(items):
                emit_qk(i + 1)
            while pending and pending[0][0] <= i:
                pending.pop(0)[1]()
            if steps and i % every == every - 1:
                for dl, fn in (steps.pop(0)() or ()):
                    pending.append((i + dl, fn))
                pending.sort(key=lambda x: x[0])
            emit_exp(i)
            if i >= 1:
                emit_pv(i - 1)
        emit_pv(len(items) - 1)
        while pending:
            pending.pop(0)[1]()
    while steps:
        for dl, fn in (steps.pop(0)() or ()):
            fn()


def ssm_setup(C, d, out_bufs, Fb, BF, tpbank, identf, Bidf):
    SG = 256
    tmpf = [(C.sbuf(f"s_tmp{i}", [128, SG], F32), Buf()) for i in range(10)]
    tcnt = [0]

    def nexttmp():
        r = tmpf[tcnt[0] % len(tmpf)]
        tcnt[0] += 1
        return r
    uT = C.sbuf("s_uT", [128, SEQ], BF16); Bu = [Buf() for _ in range(4)]
    for i in range(4):
        C.dma("sp", uT[:, i * 4096:(i + 1) * 4096], d["uT"][:, i * 4096:(i + 1) * 4096], writes=[Bu[i]])
    NP = 24
    pp = C.sbuf("s_pp", [128, NP, 4], F32); Bpp = Buf()
    ppi = C.sbuf("s_ppi", [128, 4], I32); Bppi = Buf()
    LR, LI, LDT, DT, TH, RHO, SIN, COS, ABR, ABI, DEN, NR, FR, FI, T0, T1, T2, KF, R0 = range(19)

    def col(i):
        return pp[:, i, :]
    C.dma("sp", col(LR), d["p_lr"], writes=[Bpp])
    C.dma("sp", col(LI), d["p_li"], writes=[Bpp])
    C.dma("sp", col(LDT), d["p_ldt"], writes=[Bpp])

    def v(fn):
        C.op("dve", fn, reads=[Bpp], writes=[Bpp])

    def a_(fn):
        C.op("act", fn, reads=[Bpp], writes=[Bpp])
    a_(lambda E: E.activation(out=col(DT), in_=col(LDT), func=ACT.Exp))
    v(lambda E: E.tensor_tensor(out=col(T0), in0=col(LR), in1=col(DT), op=ALU.mult))
    a_(lambda E: E.activation(out=col(RHO), in_=col(T0), func=ACT.Exp))
    v(lambda E: E.tensor_tensor(out=col(TH), in0=col(LI), in1=col(DT), op=ALU.mult))

    def sin_of(dst, src, shift):
        v(lambda E: E.tensor_scalar(out=col(T1), in0=col(src), scalar1=shift, scalar2=None, op0=ALU.add))
        v(lambda E: E.tensor_scalar(out=col(T2), in0=col(T1), scalar1=1.0 / (2.0 * math.pi), scalar2=None, op0=ALU.mult))
        C.op("dve", lambda E: E.tensor_copy(out=ppi[:], in_=col(T2)), reads=[Bpp], writes=[Bppi])
        C.op("dve", lambda E: E.tensor_copy(out=col(KF), in_=ppi[:]), reads=[Bppi], writes=[Bpp])
        v(lambda E: E.scalar_tensor_tensor(out=col(R0), in0=col(KF), scalar=-TWO_PI_1, in1=col(T1), op0=ALU.mult, op1=ALU.add))
        v(lambda E: E.scalar_tensor_tensor(out=col(R0), in0=col(KF), scalar=-TWO_PI_2, in1=col(R0), op0=ALU.mult, op1=ALU.add))
        v(lambda E: E.tensor_scalar(out=col(T2), in0=col(R0), scalar1=math.pi, scalar2=None, op0=ALU.is_gt))
        v(lambda E: E.scalar_tensor_tensor(out=col(R0), in0=col(T2), scalar=-2.0 * math.pi, in1=col(R0), op0=ALU.mult, op1=ALU.add))
        v(lambda E: E.tensor_scalar(out=col(T2), in0=col(R0), scalar1=-math.pi, scalar2=None, op0=ALU.is_lt))
        v(lambda E: E.scalar_tensor_tensor(out=col(R0), in0=col(T2), scalar=2.0 * math.pi, in1=col(R0), op0=ALU.mult, op1=ALU.add))
        v(lambda E: E.tensor_scalar(out=col(R0), in0=col(R0), scalar1=PI_LO, scalar2=-PI_LO, op0=ALU.min, op1=ALU.max))
        a_(lambda E: E.activation(out=col(dst), in_=col(R0), func=ACT.Sin))
    sin_of(SIN, TH, 0.0)
    sin_of(COS, TH, math.pi / 2.0)
    v(lambda E: E.tensor_tensor(out=col(ABR), in0=col(RHO), in1=col(COS), op=ALU.mult))
    v(lambda E: E.tensor_tensor(out=col(ABI), in0=col(RHO), in1=col(SIN), op=ALU.mult))
    v(lambda E: E.tensor_tensor(out=col(T0), in0=col(LR), in1=col(LR), op=ALU.mult))
    v(lambda E: E.tensor_tensor(out=col(T1), in0=col(LI), in1=col(LI), op=ALU.mult))
    v(lambda E: E.tensor_tensor(out=col(DEN), in0=col(T0), in1=col(T1), op=ALU.add))
    v(lambda E: E.reciprocal(out=col(DEN), in_=col(DEN)))
    v(lambda E: E.tensor_scalar(out=col(NR), in0=col(ABR), scalar1=-1.0, scalar2=None, op0=ALU.add))
    v(lambda E: E.tensor_tensor(out=col(T0), in0=col(NR), in1=col(LR), op=ALU.mult))
    v(lambda E: E.tensor_tensor(out=col(T1), in0=col(ABI), in1=col(LI), op=ALU.mult))
    v(lambda E: E.tensor_tensor(out=col(T0), in0=col(T0), in1=col(T1), op=ALU.add))
    v(lambda E: E.tensor_tensor(out=col(FR), in0=col(T0), in1=col(DEN), op=ALU.mult))
    v(lambda E: E.tensor_tensor(out=col(T0), in0=col(ABI), in1=col(LR), op=ALU.mult))
    v(lambda E: E.tensor_tensor(out=col(T1), in0=col(NR), in1=col(LI), op=ALU.mult))
    v(lambda E: E.tensor_tensor(out=col(T0), in0=col(T0), in1=col(T1), op=ALU.subtract))
    v(lambda E: E.tensor_tensor(out=col(FI), in0=col(T0), in1=col(DEN), op=ALU.mult))

    cosT = [C.sbuf(f"s_cos{g}", [128, SG], F32) for g in range(4)]
    sinT = [C.sbuf(f"s_sin{g}", [128, SG], F32) for g in range(4)]
    rhoT = [C.sbuf(f"s_rho{g}", [128, SG], F32) for g in range(4)]
    Btab = [Buf() for _ in range(4)]
    stp = C.sbuf("s_stp", [128, 4, 8], F32)
    Bstp = [Buf() for _ in range(4)]
    for g in range(4):
        cg_, sg_ = pp[:, COS, g:g + 1], pp[:, SIN, g:g + 1]
        rd = [Bpp, Btab[g], Bstp[g]]
        C.op("dve", lambda E, g=g: E.memset(cosT[g][:, 0:1], 1.0), writes=[Btab[g]])
        C.op("dve", lambda E, g=g: E.memset(sinT[g][:, 0:1], 0.0), writes=[Btab[g]])
        C.op("dve", lambda E, g=g: E.memset(rhoT[g][:], 1.0), writes=[Btab[g]])
        C.op("dve", lambda E, g=g: E.tensor_scalar(out=rhoT[g][:], in0=rhoT[g][:], scalar1=pp[:, RHO, g:g + 1], scalar2=None, op0=ALU.mult),
             reads=rd, writes=[Btab[g]])
        C.op("dve", lambda E, g=g, cg_=cg_: E.tensor_copy(out=stp[:, g, 0:1], in_=cg_), reads=rd, writes=[Bstp[g]])
        C.op("dve", lambda E, g=g, sg_=sg_: E.tensor_copy(out=stp[:, g, 1:2], in_=sg_), reads=rd, writes=[Bstp[g]])
        Lq = 1
        while True:
            cL, sL = stp[:, g, 0:1], stp[:, g, 1:2]
            if Lq < SG:
                n = min(Lq, SG - Lq)
                t1, b1 = nexttmp()
                C.op("dve", lambda E, g=g, t1=t1, n=n, sL=sL: E.tensor_scalar(out=t1[:, 0:n], in0=sinT[g][:, 0:n], scalar1=sL, scalar2=None, op0=ALU.mult),
                     reads=rd, writes=[b1])
                C.op("dve", lambda E, g=g, t1=t1, n=n, cL=cL, Lq=Lq: E.scalar_tensor_tensor(out=cosT[g][:, Lq:Lq + n], in0=cosT[g][:, 0:n], scalar=cL,
                                                                                         in1=t1[:, 0:n], op0=ALU.mult, op1=ALU.subtract),
                     reads=rd + [b1], writes=[Btab[g]])
                C.op("dve", lambda E, g=g, t1=t1, n=n, sL=sL: E.tensor_scalar(out=t1[:, 0:n], in0=cosT[g][:, 0:n], scalar1=sL, scalar2=None, op0=ALU.mult),
                     reads=rd + [b1], writes=[b1])
                C.op("dve", lambda E, g=g, t1=t1, n=n, cL=cL, Lq=Lq: E.scalar_tensor_tensor(out=sinT[g][:, Lq:Lq + n], in0=sinT[g][:, 0:n], scalar=cL,
                                                                                         in1=t1[:, 0:n], op0=ALU.mult, op1=ALU.add),
                     reads=rd + [b1], writes=[Btab[g]])
            if Lq >= SG:
                break
            C.op("dve", lambda E, g=g, cL=cL: E.tensor_tensor(out=stp[:, g, 2:3], in0=cL, in1=cL, op=ALU.mult), reads=rd, writes=[Bstp[g]])
            C.op("dve", lambda E, g=g, sL=sL: E.tensor_tensor(out=stp[:, g, 3:4], in0=sL, in1=sL, op=ALU.mult), reads=rd, writes=[Bstp[g]])
            C.op("dve", lambda E, g=g, cL=cL, sL=sL: E.tensor_tensor(out=stp[:, g, 4:5], in0=cL, in1=sL, op=ALU.mult), reads=rd, writes=[Bstp[g]])
            C.op("dve", lambda E, g=g: E.tensor_tensor(out=stp[:, g, 0:1], in0=stp[:, g, 2:3], in1=stp[:, g, 3:4], op=ALU.subtract), reads=rd, writes=[Bstp[g]])
            C.op("dve", lambda E, g=g: E.tensor_scalar(out=stp[:, g, 1:2], in0=stp[:, g, 4:5], scalar1=2.0, scalar2=None, op0=ALU.mult), reads=rd, writes=[Bstp[g]])
            Lq *= 2

    BbT = [[C.sbuf(f"s_bbT{g}{ri}", [128, 128], BF16) for ri in range(2)] for g in range(4)]
    CT = [[C.sbuf(f"s_cT{g}{ri}", [128, 128], BF16) for ri in range(2)] for g in range(4)]
    Bmat = [Buf() for _ in range(4)]
    Z = [(C.sbuf(f"s_Z{i}", [128, 128], F32), Buf()) for i in range(4)]
    tp, btp = tpbank
    for g in range(4):
        for i in range(4):
            C.op("dve", lambda E, i=i: E.memset(Z[i][0][:], 0.0), writes=[Z[i][1]])
        for gl in range(2):
            gg = 2 * g + gl
            rs = slice(gl * 64, gl * 64 + 64)
            cs = slice(32 * g + 16 * gl, 32 * g + 16 * gl + 16)
            C.dma("sp", Z[0][0][rs, cs], d["b_re"][gg], writes=[Z[0][1]])
            C.dma("sp", Z[1][0][rs, cs], d["b_im"][gg], writes=[Z[1][1]])
            C.dma("sp", Z[2][0][rs, cs], d["c_reT"][gg], writes=[Z[2][1]])
            C.dma("sp", Z[3][0][rs, cs], d["c_imT"][gg], writes=[Z[3][1]])
        fr, fi = pp[:, FR, g:g + 1], pp[:, FI, g:g + 1]
        t1, b1 = nexttmp()
        t2, b2 = nexttmp()
        C.op("dve", lambda E, t1=t1, fi=fi: E.tensor_scalar(out=t1[:, 0:128], in0=Z[1][0][:], scalar1=fi, scalar2=None, op0=ALU.mult),
             reads=[Z[1][1], Bpp], writes=[b1])
        C.op("dve", lambda E, t1=t1, fr=fr: E.scalar_tensor_tensor(out=t1[:, 0:128], in0=Z[0][0][:], scalar=fr, in1=t1[:, 0:128],
                                                                   op0=ALU.mult, op1=ALU.subtract),
             reads=[Z[0][1], Bpp, b1], writes=[b1])
        C.op("dve", lambda E, t2=t2, fi=fi: E.tensor_scalar(out=t2[:, 0:128], in0=Z[0][0][:], scalar1=fi, scalar2=None, op0=ALU.mult),
             reads=[Z[0][1], Bpp], writes=[b2])
        C.op("dve", lambda E, t2=t2, fr=fr: E.scalar_tensor_tensor(out=t2[:, 0:128], in0=Z[1][0][:], scalar=fr, in1=t2[:, 0:128],
                                                                   op0=ALU.mult, op1=ALU.add),
             reads=[Z[1][1], Bpp, b2], writes=[b2])
        for ri, (tt, bt) in enumerate(((t1, b1), (t2, b2))):
            C.op("pe", lambda E, tt=tt: E.transpose(out=tp[:, 0:128], in_=tt[:, 0:128], identity=identf[:]), reads=[bt, Bidf], writes=[btp])
            C.op("act", lambda E, g=g, ri=ri: E.activation(out=BbT[g][ri][:], in_=tp[:, 0:128], func=ACT.Copy), reads=[btp], writes=[Bmat[g]])
        C.op("act", lambda E, g=g: E.activation(out=CT[g][0][:], in_=Z[2][0][:], func=ACT.Copy), reads=[Z[2][1]], writes=[Bmat[g]])
        C.op("act", lambda E, g=g: E.activation(out=CT[g][1][:], in_=Z[3][0][:], func=ACT.Copy, scale=-1.0), reads=[Z[3][1]], writes=[Bmat[g]])
    dsk, Bdsk = load_vec(C, "s_dsk", d["dsk"], 1)

    init = C.sbuf("s_init", [128, 4, 2, 2], F32)
    Binit = [[Buf(), Buf()] for _ in range(4)]
    C.op("dve", lambda E: E.memset(init[:], 0.0), writes=[b for bb in Binit for b in bb])
    gin = [[(C.sbuf(f"s_gin{i}{ri}", [128, SG], F32), Buf()) for ri in range(2)] for i in range(2)]
    gst = [[(C.sbuf(f"s_g{i}{ri}", [128, SG], F32), Buf()) for ri in range(2)] for i in range(2)]
    hbf = [[[(C.sbuf(f"s_h{i}{g}{ri}", [128, SG], BF16), Buf()) for ri in range(2)] for g in range(4)] for i in range(2)]
    yob = [(C.sbuf(f"s_yo{i}", [128, SG], BF16), Buf()) for i in range(2)]
    ytmp = [[(C.sbuf(f"s_yt{i}{j}", [128, SG], F32), Buf()) for j in range(2)] for i in range(2)]
    psR, psI = Fb[:, 0:SG], Fb[:, SG:2 * SG]
    psY = Fb[:, 0:SG]
    itc = [0]

    def step_bu(seg, g):
        def f():
            t0 = seg * SG
            bu_ = Bu[t0 // 4096]
            par = itc[0] % 2
            itc[0] += 1
            for ri in range(2):
                C.op("pe", _mm(Fb[:, ri * SG:(ri + 1) * SG], BbT[g][ri][:, :], uT[:, t0:t0 + SG], True, True),
                     reads=[Bmat[g], bu_], writes=[BF[ri]])
            m1, bm1 = nexttmp(); m2, bm2 = nexttmp(); m3, bm3 = nexttmp(); m4, bm4 = nexttmp()
            C.op("dve", lambda E: E.tensor_tensor(out=m1[:], in0=psR, in1=cosT[g][:], op=ALU.mult), reads=[BF[0], Btab[g]], writes=[bm1])
            C.op("dve", lambda E: E.tensor_tensor(out=m4[:], in0=psR, in1=sinT[g][:], op=ALU.mult), reads=[BF[0], Btab[g]], writes=[bm4])
            C.op("dve", lambda E: E.tensor_tensor(out=m2[:], in0=psI, in1=sinT[g][:], op=ALU.mult), reads=[BF[1], Btab[g]], writes=[bm2])
            C.op("dve", lambda E: E.tensor_tensor(out=m3[:], in0=psI, in1=cosT[g][:], op=ALU.mult), reads=[BF[1], Btab[g]], writes=[bm3])
            (gir, bgir), (gii, bgii) = gin[par]
            C.op("dve", lambda E: E.tensor_tensor(out=gir[:], in0=m1[:], in1=m2[:], op=ALU.add), reads=[bm1, bm2], writes=[bgir])
            C.op("dve", lambda E: E.tensor_tensor(out=gii[:], in0=m3[:], in1=m4[:], op=ALU.subtract), reads=[bm3, bm4], writes=[bgii])
            (gr, bgr), (gi, bgi) = gst[par]
            ip = seg % 2
            C.op("dve", lambda E: E.tensor_tensor_scan(out=gr[:], data0=rhoT[g][:], data1=gir[:], initial=init[:, g, ip, 0:1],
                                                       op0=ALU.mult, op1=ALU.add),
                 reads=[Btab[g], bgir, Binit[g][ip]], writes=[bgr])
            C.op("dve", lambda E: E.tensor_tensor_scan(out=gi[:], data0=rhoT[g][:], data1=gii[:], initial=init[:, g, ip, 1:2],
                                                       op0=ALU.mult, op1=ALU.add),
                 reads=[Btab[g], bgii, Binit[g][ip]], writes=[bgi])
            cS, sS = stp[:, g, 0:1], stp[:, g, 1:2]
            nx = 1 - ip
            rdn = [bgr, bgi, Bstp[g], Binit[g][nx]]
            C.op("dve", lambda E: E.tensor_scalar(out=stp[:, g, 5:6], in0=gi[:, SG - 1:SG], scalar1=sS, scalar2=None, op0=ALU.mult),
                 reads=rdn, writes=[Bstp[g]])
            C.op("dve", lambda E: E.scalar_tensor_tensor(out=init[:, g, nx, 0:1], in0=gr[:, SG - 1:SG], scalar=cS, in1=stp[:, g, 5:6],
                                                         op0=ALU.mult, op1=ALU.subtract),
                 reads=rdn, writes=[Binit[g][nx]])
            C.op("dve", lambda E: E.tensor_scalar(out=stp[:, g, 6:7], in0=gr[:, SG - 1:SG], scalar1=sS, scalar2=None, op0=ALU.mult),
                 reads=rdn, writes=[Bstp[g]])
            C.op("dve", lambda E: E.scalar_tensor_tensor(out=init[:, g, nx, 1:2], in0=gi[:, SG - 1:SG], scalar=cS, in1=stp[:, g, 6:7],
                                                         op0=ALU.mult, op1=ALU.add),
                 reads=rdn, writes=[Binit[g][nx]])
            n1, bn1 = nexttmp(); n2, bn2 = nexttmp(); n3, bn3 = nexttmp(); n4, bn4 = nexttmp()
            C.op("pool", lambda E: E.tensor_tensor(out=n1[:], in0=gr[:], in1=cosT[g][:], op=ALU.mult), reads=[bgr, Btab[g]], writes=[bn1])
            C.op("pool", lambda E: E.tensor_tensor(out=n2[:], in0=gi[:], in1=sinT[g][:], op=ALU.mult), reads=[bgi, Btab[g]], writes=[bn2])
            C.op("pool", lambda E: E.tensor_tensor(out=n3[:], in0=gr[:], in1=sinT[g][:], op=ALU.mult), reads=[bgr, Btab[g]], writes=[bn3])
            C.op("pool", lambda E: E.tensor_tensor(out=n4[:], in0=gi[:], in1=cosT[g][:], op=ALU.mult), reads=[bgi, Btab[g]], writes=[bn4])
            (hr, bhr), (hi, bhi) = hbf[seg % 2][g]
            C.op("pool", lambda E: E.tensor_tensor(out=hr[:], in0=n1[:], in1=n2[:], op=ALU.subtract), reads=[bn1, bn2], writes=[bhr])
            C.op("pool", lambda E: E.tensor_tensor(out=hi[:], in0=n3[:], in1=n4[:], op=ALU.add), reads=[bn3, bn4], writes=[bhi])
        return f

    def step_y(seg):
        def f():
            t0 = seg * SG
            bu_ = Bu[t0 // 4096]
            for g in range(4):
                (hr, bhr), (hi, bhi) = hbf[seg % 2][g]
                C.op("pe", _mm(psY, CT[g][0][:, :], hr[:, :], g == 0, False), reads=[Bmat[g], bhr], writes=[BF[0]])
                C.op("pe", _mm(psY, CT[g][1][:, :], hi[:, :], False, g == 3), reads=[Bmat[g], bhi], writes=[BF[0]])
            yt, byt = ytmp[seg % 2][0]
            w, bw = ytmp[seg % 2][1]
            C.op("dve", lambda E: E.scalar_tensor_tensor(out=yt[:], in0=uT[:, t0:t0 + SG], scalar=dsk[:, 0:1], in1=psY, op0=ALU.mult, op1=ALU.add),
                 reads=[bu_, Bdsk, BF[0]], writes=[byt])
            C.op("pool", lambda E: E.tensor_tensor(out=w[:], in0=yt[:], in1=yt[:], op=ALU.mult), reads=[byt], writes=[bw])
            C.op("pool", lambda E: E.tensor_scalar(out=w[:], in0=w[:], scalar1=0.044715, scalar2=1.0, op0=ALU.mult, op1=ALU.add), reads=[bw], writes=[bw])
            C.op("pool", lambda E: E.tensor_tensor(out=w[:], in0=w[:], in1=yt[:], op=ALU.mult), reads=[bw, byt], writes=[bw])

            def part2():
                C.op("act", lambda E: E.activation(out=w[:], in_=w[:], func=ACT.Sigmoid, scale=1.5957691216057308), reads=[bw], writes=[bw])
                yo, byo = yob[seg % 2]
                C.op("pool", lambda E: E.tensor_tensor(out=yo[:], in0=yt[:], in1=w[:], op=ALU.mult), reads=[bw, byt], writes=[byo])
                dst = Buf()
                out_bufs.append(dst)
                C.dma("sp", d["yT"][:, t0:t0 + SG], yo[:], reads=[byo], writes=[dst])
            return [(5, part2)]
        return f

    steps = []
    nseg = SEQ // SG
    for seg in range(nseg):
        for g in range(4):
            steps.append(step_bu(seg, g))
            if g == 0 and seg > 0:
                steps.append(step_y(seg - 1))
    steps.append(step_y(nseg - 1))
    return steps


from concourse.bass_utils import run_bass_kernel_spmd
import ml_dtypes

NCORES = 8
_BF = ml_dtypes.bfloat16
_PROGS = {}


def _din(nc, name, shape, dt=F32):
    return nc.dram_tensor(name, list(shape), dt, kind="ExternalInput").ap()


def _dout(nc, name, shape, dt=F32):
    return nc.dram_tensor(name, list(shape), dt, kind="ExternalOutput").ap()


def _vec128(v):
    return np.ascontiguousarray(np.asarray(v, np.float32).reshape(-1, 128).T)


def build_A():
    nc = bass.Bass("TRN2", target_bir_lowering=False)
    d = dict(xT=_din(nc, "xT", [D, TOK]), w_in=_din(nc, "w_in", [D, 4096]), projT=_dout(nc, "projT", [4096, TOK], BF16))
    C = Ctx(nc)
    S = Shared(C)
    outs = []
    phase_A(C, S, d, outs)
    C.finish(outs, "sp")
    C.emit()
    return nc


def build_B():
    nc = bass.Bass("TRN2", target_bir_lowering=False)
    d = dict(qT=_din(nc, "qT", [128, SEQ], BF16), kT=_din(nc, "kT", [128, SEQ], BF16), vT=_din(nc, "vT", [128, SEQ], BF16),
             uT=_din(nc, "uT", [128, SEQ], BF16), ident=_din(nc, "ident", [128, 128]), lamv=_din(nc, "lamv", [128, 4, 64]),
             cvec=_din(nc, "cvec", [128, 4]), gattn=_din(nc, "gattn", [128, 1]),
             p_lr=_din(nc, "p_lr", [128, 4]), p_li=_din(nc, "p_li", [128, 4]), p_ldt=_din(nc, "p_ldt", [128, 4]),
             b_re=_din(nc, "b_re", [8, 64, 16]), b_im=_din(nc, "b_im", [8, 64, 16]), c_reT=_din(nc, "c_reT", [8, 64, 16]),
             c_imT=_din(nc, "c_imT", [8, 64, 16]), dsk=_din(nc, "dsk", [128, 1]),
             attnT=_dout(nc, "attnT", [128, SEQ], BF16), yT=_dout(nc, "yT", [128, SEQ], BF16))
    C = Ctx(nc)
    outs = []
    phase_B(C, d, outs)
    C.finish(outs, "sp")
    C.emit()
    return nc


def build_C(do_proj):
    nc = bass.Bass("TRN2", target_bir_lowering=False)
    d = dict(xres=_din(nc, "xres", [D, TOK]), attnT=_din(nc, "attnT", [1024, TOK], BF16), yT=_din(nc, "yT", [1024, TOK], BF16),
             glu_w=_din(nc, "glu_w", [1024, 1024]), glu_b=_din(nc, "glu_b", [128, 8]), ssm_g=_din(nc, "ssm_g", [128, 8]),
             w_out=_din(nc, "w_out", [D, D]), ln1_g=_din(nc, "ln1_g", [128, 16]), ln1_b=_din(nc, "ln1_b", [128, 16]),
             w_up=_din(nc, "w_up", [D, DFF]), w_down=_din(nc, "w_down", [DFF, D]), ln2_g=_din(nc, "ln2_g", [128, 16]),
             ln2_b=_din(nc, "ln2_b", [128, 16]), bd16=_din(nc, "bd16", [128, 128]), xout=_dout(nc, "xout", [D, TOK]))
    if do_proj:
        d["w_in"] = _din(nc, "w_in", [D, 4096])
        d["projT"] = _dout(nc, "projT", [4096, TOK], BF16)
    C = Ctx(nc)
    S = Shared(C)
    outs = []
    phase_C(C, S, d, do_proj, outs)
    C.finish(outs, "sp")
    C.emit()
    return nc


def _host_B_inputs(inp, l, projT_full, h):
    lam_init = 0.8 - 0.6 * math.exp(-0.3 * l)
    m = {}
    m["qT"] = np.ascontiguousarray(projT_full[h * 128:(h + 1) * 128])
    m["kT"] = np.ascontiguousarray(projT_full[1024 + h * 128:1024 + (h + 1) * 128])
    m["vT"] = np.ascontiguousarray(projT_full[2048 + h * 128:2048 + (h + 1) * 128])
    m["uT"] = np.ascontiguousarray(projT_full[3072 + h * 128:3072 + (h + 1) * 128])
    m["ident"] = np.eye(128, dtype=np.float32)
    lam4 = np.stack([inp["lambda_q1"][l], inp["lambda_k1"][l], inp["lambda_q2"][l], inp["lambda_k2"][l]])
    m["lamv"] = np.ascontiguousarray(np.broadcast_to(lam4[None], (128, 4, 64))).astype(np.float32)
    cv = np.zeros((128, 4), np.float32)
    cv[:, 0] = lam_init
    cv[:, 1] = 1.0 - lam_init
    m["cvec"] = cv
    m["gattn"] = np.ascontiguousarray(inp["attn_norm_g"][l][h * 128:(h + 1) * 128].reshape(128, 1)).astype(np.float32)
    gs = slice(8 * h, 8 * h + 8)

    def pcol(a):
        return np.ascontiguousarray(np.asarray(a).reshape(4, 128).T).astype(np.float32)
    m["p_lr"] = pcol(inp["ssm_lambda_re"][l][gs])
    m["p_li"] = pcol(inp["ssm_lambda_im"][l][gs])
    m["p_ldt"] = pcol(np.broadcast_to(inp["ssm_log_dt"][l][gs][:, None], (8, 64)))
    m["b_re"] = np.ascontiguousarray(inp["ssm_b_re"][l][gs]).astype(np.float32)
    m["b_im"] = np.ascontiguousarray(inp["ssm_b_im"][l][gs]).astype(np.float32)
    m["c_reT"] = np.ascontiguousarray(inp["ssm_c_re"][l][gs].transpose(0, 2, 1)).astype(np.float32)
    m["c_imT"] = np.ascontiguousarray(inp["ssm_c_im"][l][gs].transpose(0, 2, 1)).astype(np.float32)
    m["dsk"] = np.ascontiguousarray(inp["ssm_d"][l][h * 128:(h + 1) * 128].reshape(128, 1)).astype(np.float32)
    return m


def _host_C_inputs(inp, l, do_proj):
    m = dict(glu_w=np.ascontiguousarray(inp["glu_w"][l]), glu_b=_vec128(inp["glu_b"][l]), ssm_g=_vec128(inp["ssm_norm_g"][l]),
             w_out=np.ascontiguousarray(inp["w_out"][l]), ln1_g=_vec128(inp["ln1_g"][l]), ln1_b=_vec128(inp["ln1_b"][l]),
             w_up=np.ascontiguousarray(inp["w_up"][l]), w_down=np.ascontiguousarray(inp["w_down"][l]),
             ln2_g=_vec128(inp["ln2_g"][l]), ln2_b=_vec128(inp["ln2_b"][l]),
             bd16=np.kron(np.eye(8), np.ones((16, 16)) / 16).astype(np.float32))
    if do_proj:
        m["w_in"] = np.ascontiguousarray(inp["w_in"][l + 1])
    return m


def _prog(key, fn):
    if key not in _PROGS:
        _PROGS[key] = fn()
    return _PROGS[key]


def kernel(**inputs):
    inp = {k: np.asarray(v) for k, v in inputs.items()}
    cores = list(range(NCORES))
    x = inp["x"][0]
    xT = [np.ascontiguousarray(x[c * TOK:(c + 1) * TOK].T) for c in cores]
    res = run_bass_kernel_spmd(_prog("A", build_A), [dict(xT=xT[c], w_in=np.ascontiguousarray(inp["w_in"][0])) for c in cores],
                               core_ids=cores)
    projT = np.concatenate([res.results[c]["projT"] for c in cores], axis=1)
    for l in range(DEPTH):
        res = run_bass_kernel_spmd(_prog("B", build_B), [_host_B_inputs(inp, l, projT, h) for h in cores], core_ids=cores)
        attnT = np.concatenate([res.results[h]["attnT"] for h in cores], axis=0)
        yT = np.concatenate([res.results[h]["yT"] for h in cores], axis=0)
        do_proj = l + 1 < DEPTH
        common = _host_C_inputs(inp, l, do_proj)
        maps = []
        for c in cores:
            m = dict(common)
            m["xres"] = xT[c]
            m["attnT"] = np.ascontiguousarray(attnT[:, c * TOK:(c + 1) * TOK])
            m["yT"] = np.ascontiguousarray(yT[:, c * TOK:(c + 1) * TOK])
            maps.append(m)
        res = run_bass_kernel_spmd(_prog(("C", do_proj), lambda: build_C(do_proj)), maps, core_ids=cores)
        xT = [res.results[c]["xout"] for c in cores]
        if do_proj:
            projT = np.concatenate([res.results[c]["projT"] for c in cores], axis=1)
    out = np.concatenate([np.ascontiguousarray(xT[c].T) for c in cores], axis=0)[None]
    return out.astype(np.float32)
```

```python
import numpy as np
from contextlib import ExitStack
import concourse.bass as bass
import concourse.mybir as mybir

ACT = mybir.ActivationFunctionType
ALU = mybir.AluOpType
AX = mybir.AxisListType
F32 = mybir.dt.float32
BF16 = mybir.dt.bfloat16
I32 = mybir.dt.int32

SEM_ROLL = 30000
DMA_RING = 8


class Buf:
    __slots__ = ("name", "w", "r")

    def __init__(self, name=""):
        self.name = name
        self.w = None
        self.r = {}


class Ctx:
    ENG = ("pe", "act", "dve", "pool", "sp")

    def __init__(self, nc, same_engine_sync=True):
        self.nc = nc
        self.es = ExitStack()
        self.stream = {e: [] for e in self.ENG}
        self.same_engine_sync = same_engine_sync
        self.nsem = 0
        self.cur = {}
        for e in ("pe", "act", "dve", "pool"):
            self.cur[e] = [self._newsem(e), 0]
        self.ring = {}
        self.ringpos = {}
        for q in ("sp", "act", "pool"):
            self.ring[q] = [[self._newsem("dq" + q), 0] for _ in range(DMA_RING)]
            self.ringpos[q] = 0
        self.waited = {}
        self.n_inst = 0
        self.n_wait = 0
        self.scopes = []

    def _newsem(self, name):
        self.nsem += 1
        return self.es.enter_context(self.nc.semaphore(f"s{self.nsem}_{name}"))

    def push_scope(self):
        self.scopes.append(ExitStack())

    def pop_scope(self):
        self.barrier()
        self.scopes.pop().close()

    def barrier(self):
        toks = []
        for e, (sem, cnt) in self.cur.items():
            if cnt > 0:
                toks.append((sem, cnt))
        for q, ring in self.ring.items():
            for sem, cnt in ring:
                if cnt > 0:
                    toks.append((sem, cnt))
        for e in self.ENG:
            own = self.cur[e][0] if e in self.cur else None
            self._need(e, [t for t in toks if t[0] is not own])

    def sbuf(self, name, shape, dt):
        es = self.scopes[-1] if self.scopes else self.es
        return es.enter_context(self.nc.sbuf_tensor(name, list(shape), dt))

    def psum(self, name, shape, dt=F32):
        return self.es.enter_context(self.nc.psum_tensor(name, list(shape), dt))

    def _need(self, eng, deps):
        best = {}
        for tok in deps:
            if tok is None:
                continue
            sem, val = tok
            k = id(sem)
            if k not in best or best[k][1] < val:
                best[k] = (sem, val)
        for k, (sem, val) in best.items():
            if self.waited.get((eng, k), 0) >= val:
                continue
            self.waited[(eng, k)] = val
            self.n_wait += 1
            self.stream[eng].append(lambda E, sem=sem, val=val: E.wait_ge(sem, val))

    def _deps(self, eng, reads, writes):
        own = self.cur[eng][0] if eng in self.cur else None
        deps = []
        for b in reads:
            if b.w is not None:
                deps.append(b.w)
        for b in writes:
            if b.w is not None:
                deps.append(b.w)
            deps.extend(b.r.values())
        if own is not None and (eng == "pe" or not self.same_engine_sync):
            deps = [d for d in deps if d[0] is not own]
        return deps

    def _mark(self, tok, reads, writes):
        for b in writes:
            b.w = tok
            b.r = {}
        for b in reads:
            b.r[id(tok[0])] = tok

    def op(self, eng, fn, reads=(), writes=()):
        self._need(eng, self._deps(eng, reads, writes))
        c = self.cur[eng]
        if c[1] >= SEM_ROLL:
            c[0] = self._newsem(eng)
            c[1] = 0
        c[1] += 1
        sem, val = c[0], c[1]
        self.n_inst += 1
        self.stream[eng].append(lambda E, fn=fn, sem=sem: fn(E).then_inc(sem, 1))
        self._mark((sem, val), reads, writes)

    def dma(self, q, out, in_, reads=(), writes=(), **kw):
        ring = self.ring[q]
        pos = self.ringpos[q]
        self.ringpos[q] = (pos + 1) % len(ring)
        slot = ring[pos]
        deps = self._deps(q, reads, writes)
        if slot[1] > 0:
            deps.append((slot[0], slot[1]))
        self._need(q, deps)
        slot[1] += 16
        sem, val = slot[0], slot[1]
        self.n_inst += 1
        self.stream[q].append(
            lambda E, out=out, in_=in_, sem=sem, kw=kw: E.dma_start(out=out, in_=in_, **kw).then_inc(sem, 16))
        self._mark((sem, val), reads, writes)

    def finish(self, bufs, eng="sp"):
        self._need(eng, [b.w for b in bufs])

    def emit(self):
        nc = self.nc
        st = self.stream
        with nc.Block() as block:
            @block.sync
            def _(E):
                for f in st["sp"]:
                    f(E)

            @block.tensor
            def _(E):
                for f in st["pe"]:
                    f(E)

            @block.scalar
            def _(E):
                for f in st["act"]:
                    f(E)

            @block.vector
            def _(E):
                for f in st["dve"]:
                    f(E)

            @block.gpsimd
            def _(E):
                for f in st["pool"]:
                    f(E)
        self.es.close()


import math
import numpy as np

D = 2048
DFF = 8192
TOK = 2048
T = 512
SEQ = 16384
DEPTH = 4
ALPHA = (2.0 * DEPTH) ** 0.25
EPS = 1e-5
NW = 6


class Shared:
    def __init__(self, C):
        self.C = C
        self.ps = []
        for i in range(8):
            self.ps.append((C.psum(f"ps{i}", [128, 512], F32), Buf(f"ps{i}")))
        self.pset_i = 0
        self.w = [(C.sbuf(f"w{i}", [128, 4, 512], BF16), Buf(f"w{i}")) for i in range(NW)]
        self.w_i = 0

    def next_pset(self):
        s = self.ps[self.pset_i * 4:(self.pset_i + 1) * 4]
        self.pset_i ^= 1
        return s

    def next_w(self):
        r = self.w[self.w_i]
        self.w_i = (self.w_i + 1) % NW
        return r


def _mm(ps, lhsT, rhs, start, stop):
    return lambda E: E.matmul(ps, lhsT, rhs, start=start, stop=stop)


def stream_mm(C, S, W, K, M, rhs, rhs_bufs, evac):
    ncg = M // 512
    nks = K // 512
    nk = K // 128
    for cg in range(ncg):
        pset = S.next_pset()
        for ks in range(nks):
            wt, wb = S.next_w()
            src = W[ks * 512:(ks + 1) * 512, cg * 512:(cg + 1) * 512].rearrange("(kc p) n -> p kc n", p=128)
            C.dma("pool", wt[:], src, writes=[wb])
            for kc in range(4):
                kk = ks * 4 + kc
                for m in range(4):
                    ps, pb = pset[m]
                    C.op("pe", _mm(ps[:, :], wt[:, kc, m * 128:(m + 1) * 128], rhs(kk), kk == 0, kk == nk - 1),
                         reads=[wb] + rhs_bufs(kk), writes=[pb])
        for m in range(4):
            evac(cg * 4 + m, pset[m][0], pset[m][1])


def load_vec(C, name, ap, ncols):
    t = C.sbuf(name, [128, ncols], F32)
    b = Buf(name)
    C.dma("sp", t[:], ap, writes=[b])
    return t, b


def phase_A(C, S, d, out_bufs):
    xr = C.sbuf("a_xr", [128, 16, T], F32); Bxr = Buf()
    xb = C.sbuf("a_xb", [128, 16, T], BF16); Bxb = [Buf() for _ in range(16)]
    stg = [(C.sbuf(f"a_stg{i}", [128, T], BF16), Buf()) for i in range(4)]
    for t in range(TOK // T):
        tok = slice(t * T, (t + 1) * T)
        C.dma("sp", xr[:], d["xT"][:, tok].rearrange("(kc p) n -> p kc n", p=128), writes=[Bxr])
        for m in range(16):
            C.op("act", lambda E, m=m: E.activation(out=xb[:, m, :], in_=xr[:, m, :], func=ACT.Copy),
                 reads=[Bxr], writes=[Bxb[m]])
        proj_stage(C, S, d["w_in"], d["projT"], xb, Bxb, stg, tok, out_bufs)


def proj_stage(C, S, w_in, projT, xb, Bxb, stg, tok, out_bufs):
    cnt = [0]

    def evac(m, ps, pb):
        st, sb = stg[cnt[0] % len(stg)]
        cnt[0] += 1
        C.op("act", lambda E: E.activation(out=st[:], in_=ps[:], func=ACT.Copy), reads=[pb], writes=[sb])
        ob = Buf()
        out_bufs.append(ob)
        C.dma("sp", projT[m * 128:(m + 1) * 128, tok], st[:], reads=[sb], writes=[ob])
    stream_mm(C, S, w_in, D, 4096, lambda kk: xb[:, kk, :], lambda kk: [Bxb[kk]], evac)


def phase_C(C, S, d, do_proj, out_bufs):
    nc = C.nc
    xr = C.sbuf("c_xr", [128, 16, T], F32); Bxr = [Buf() for _ in range(16)]
    xb = C.sbuf("c_xb", [128, 16, T], BF16); Bxb = [Buf() for _ in range(16)]
    sF32 = xb.bitcast(F32)

    def sFv(m):
        return sF32[:, 2 * m:2 * m + 2, :].rearrange("p a c -> p (a c)")
    cat = C.sbuf("c_cat", [128, 16, T], BF16); Bcat = [Buf() for _ in range(16)]
    yb = C.sbuf("c_yb", [128, 8, T], BF16); Byb = Buf()
    hb = C.sbuf("c_hb", [128, 64, T], BF16); Bhb = [Buf() for _ in range(64)]
    hF32 = hb.bitcast(F32)

    def hFv(m):
        return hF32[:, 2 * m:2 * m + 2, :].rearrange("p a c -> p (a c)")
    tmp = [(C.sbuf(f"c_tmp{i}", [128, T], F32), Buf()) for i in range(4)]
    mean_sb = C.sbuf("c_mean", [128, T], F32); Bmean = Buf()
    rstd = C.sbuf("c_rstd", [128, T], F32); Brstd = Buf()
    var_sb = C.sbuf("c_var", [128, T], F32); Bvar = Buf()
    stg = [(C.sbuf(f"c_stg{i}", [128, T], BF16), Buf()) for i in range(4)]
    glu_b, Bglu_b = load_vec(C, "c_glub", d["glu_b"], 8)
    ssm_g, Bssm_g = load_vec(C, "c_ssmg", d["ssm_g"], 8)
    ln1g, Bln1g = load_vec(C, "c_ln1g", d["ln1_g"], 16)
    ln1b, Bln1b = load_vec(C, "c_ln1b", d["ln1_b"], 16)
    ln2g, Bln2g = load_vec(C, "c_ln2g", d["ln2_g"], 16)
    ln2b, Bln2b = load_vec(C, "c_ln2b", d["ln2_b"], 16)
    bd16 = C.sbuf("c_bd16", [128, 128], F32); Bbd16 = Buf()
    C.dma("sp", bd16[:], d["bd16"], writes=[Bbd16])
    onesb = C.sbuf("c_ones", [128, 128], BF16); Bones = Buf()
    C.op("dve", lambda E: E.memset(onesb[:], 1.0 / D), writes=[Bones])
    tcnt = [0]

    def nexttmp():
        r = tmp[tcnt[0] % len(tmp)]
        tcnt[0] += 1
        return r

    def evac_res(sq, Bsq):
        def f(m, ps, pb):
            C.op("dve", lambda E: E.scalar_tensor_tensor(out=xr[:, m, :], in0=xr[:, m, :], scalar=ALPHA, in1=ps[:],
                                                          op0=ALU.mult, op1=ALU.add),
                 reads=[Bxr[m], pb], writes=[Bxr[m]])
            C.op("act", lambda E: E.activation(out=xb[:, m, :], in_=xr[:, m, :], func=ACT.Copy),
                 reads=[Bxr[m]], writes=[Bxb[m]])
            C.op("act", lambda E: E.activation(out=sq[:, m, :], in_=xr[:, m, :], func=ACT.Square),
                 reads=[Bxr[m]], writes=[Bsq[m]])
        return f

    def layer_norm(g, Bg, b, Bb, sq, Bsq):
        pset = S.next_pset()
        (psA, pbA), (psB, pbB) = pset[0], pset[1]
        for m in range(16):
            C.op("pe", _mm(psA[:, :], onesb[:, :], xb[:, m, :], m == 0, m == 15), reads=[Bones, Bxb[m]], writes=[pbA])
        for m in range(16):
            C.op("pe", _mm(psB[:, :], onesb[:, :], sq[:, m, :], m == 0, m == 15), reads=[Bones, Bsq[m]], writes=[pbB])
        C.op("dve", lambda E: E.tensor_copy(out=mean_sb[:], in_=psA[:]), reads=[pbA], writes=[Bmean])
        C.op("dve", lambda E: E.tensor_tensor(out=var_sb[:], in0=mean_sb[:], in1=mean_sb[:], op=ALU.mult),
             reads=[Bmean], writes=[Bvar])
        C.op("dve", lambda E: E.tensor_tensor(out=var_sb[:], in0=psB[:], in1=var_sb[:], op=ALU.subtract),
             reads=[pbB, Bvar], writes=[Bvar])
        C.op("dve", lambda E: E.tensor_scalar(out=var_sb[:], in0=var_sb[:], scalar1=EPS, scalar2=None, op0=ALU.add),
             reads=[Bvar], writes=[Bvar])
        C.op("act", lambda E: E.activation(out=rstd[:], in_=var_sb[:], func=ACT.Sqrt), reads=[Bvar], writes=[Brstd])
        C.op("dve", lambda E: E.reciprocal(out=rstd[:], in_=rstd[:]), reads=[Brstd], writes=[Brstd])
        for m in range(16):
            t1, b1 = nexttmp()
            C.op("dve", lambda E, m=m, t1=t1: E.tensor_tensor(out=t1[:], in0=xr[:, m, :], in1=mean_sb[:], op=ALU.subtract),
                 reads=[Bxr[m], Bmean], writes=[b1])
            C.op("dve", lambda E, m=m, t1=t1: E.tensor_tensor(out=t1[:], in0=t1[:], in1=rstd[:], op=ALU.mult),
                 reads=[b1, Brstd], writes=[b1])
            C.op("act", lambda E, m=m, t1=t1: E.activation(out=xr[:, m, :], in_=t1[:], func=ACT.Identity,
                                                            scale=g[:, m:m + 1], bias=b[:, m:m + 1]),
                 reads=[b1, Bg, Bb], writes=[Bxr[m]])
            C.op("act", lambda E, m=m: E.activation(out=xb[:, m, :], in_=xr[:, m, :], func=ACT.Copy),
                 reads=[Bxr[m]], writes=[Bxb[m]])

    def emit_loads(t):
        tok = slice(t * T, (t + 1) * T)
        C.dma("sp", yb[:], d["yT"][:, tok].rearrange("(kc p) n -> p kc n", p=128), writes=[Byb])
        for m in range(8):
            C.dma("sp", cat[:, m, :], d["attnT"][m * 128:(m + 1) * 128, tok], writes=[Bcat[m]])
        for m in range(16):
            C.dma("sp", xr[:, m, :], d["xres"][m * 128:(m + 1) * 128, tok], writes=[Bxr[m]])

    emit_loads(0)
    for t in range(TOK // T):
        tok = slice(t * T, (t + 1) * T)

        def evac_glu(m, ps, pb):
            t1, b1 = nexttmp()
            C.op("act", lambda E: E.activation(out=t1[:], in_=ps[:], func=ACT.Sigmoid, bias=glu_b[:, m:m + 1]),
                 reads=[pb, Bglu_b], writes=[b1])
            C.op("dve", lambda E: E.tensor_tensor(out=sFv(m), in0=yb[:, m, :], in1=t1[:], op=ALU.mult),
                 reads=[Byb, b1], writes=[Bxb[2 * m], Bxb[2 * m + 1]])
        stream_mm(C, S, d["glu_w"], 1024, 1024, lambda kk: yb[:, kk, :], lambda kk: [Byb], evac_glu)
        psA = S.next_pset()
        psB = S.next_pset()
        ps8 = psA + psB
        for m in range(8):
            sfb = [Bxb[2 * m], Bxb[2 * m + 1]]
            C.op("act", lambda E, m=m: E.activation(out=hFv(m), in_=sFv(m), func=ACT.Square), reads=sfb, writes=[Bhb[2 * m], Bhb[2 * m + 1]])
        for m in range(8):
            ps, pb = ps8[m]
            C.op("pe", _mm(ps[:, :], bd16[:, :], hFv(m), True, True), reads=[Bbd16, Bhb[2 * m], Bhb[2 * m + 1]], writes=[pb])
        for m in range(8):
            ps, pb = ps8[m]
            sfb = [Bxb[2 * m], Bxb[2 * m + 1]]
            t2, b2 = nexttmp()
            C.op("dve", lambda E, ps=ps, t2=t2: E.tensor_scalar(out=t2[:], in0=ps[:], scalar1=EPS, scalar2=None, op0=ALU.add),
                 reads=[pb], writes=[b2])
            C.op("act", lambda E, t2=t2: E.activation(out=t2[:], in_=t2[:], func=ACT.Sqrt), reads=[b2], writes=[b2])
            C.op("dve", lambda E, t2=t2: E.reciprocal(out=t2[:], in_=t2[:]), reads=[b2], writes=[b2])
            C.op("dve", lambda E, m=m, t2=t2: E.scalar_tensor_tensor(out=cat[:, 8 + m, :], in0=sFv(m), scalar=ssm_g[:, m:m + 1],
                                                                    in1=t2[:], op0=ALU.mult, op1=ALU.mult),
                 reads=sfb + [b2, Bssm_g], writes=[Bcat[8 + m]])
        sq1 = hb
        stream_mm(C, S, d["w_out"], D, D, lambda kk: cat[:, kk, :], lambda kk: [Bcat[kk]], evac_res(sq1, Bhb))
        layer_norm(ln1g, Bln1g, ln1b, Bln1b, sq1, Bhb)

        def evac_up(j, ps, pb):
            t1, b1 = nexttmp()
            C.op("act", lambda E: E.activation(out=t1[:], in_=ps[:], func=ACT.Relu), reads=[pb], writes=[b1])
            C.op("dve", lambda E: E.tensor_tensor(out=hb[:, j, :], in0=t1[:], in1=t1[:], op=ALU.mult), reads=[b1], writes=[Bhb[j]])
        stream_mm(C, S, d["w_up"], D, DFF, lambda kk: xb[:, kk, :], lambda kk: [Bxb[kk]], evac_up)
        stream_mm(C, S, d["w_down"], DFF, D, lambda kk: hb[:, kk, :], lambda kk: [Bhb[kk]], evac_res(cat, Bcat))
        layer_norm(ln2g, Bln2g, ln2b, Bln2b, cat, Bcat)
        for m in range(16):
            ob = Buf()
            out_bufs.append(ob)
            C.dma("sp", d["xout"][m * 128:(m + 1) * 128, tok], xr[:, m, :], reads=[Bxr[m]], writes=[ob])
        if t + 1 < TOK // T:
            emit_loads(t + 1)
        if do_proj:
            proj_stage(C, S, d["w_in"], d["projT"], xb, Bxb, stg, tok, out_bufs)


NSEG = SEQ // 512
TWO_PI_1 = 6.28125
TWO_PI_2 = 2.0 * math.pi - 6.28125
PI_LO = 3.1415925


def phase_B(C, d, out_bufs, do_attn=True, do_ssm=True):
    S0 = C.psum("b_S0", [128, 1024], F32); BS0 = [Buf(), Buf()]
    S1 = C.psum("b_S1", [128, 1024], F32); BS1 = [Buf(), Buf()]
    O = [(C.psum(f"b_O{a}", [128, 512], F32), Buf()) for a in range(2)]
    Lb = C.psum("b_L", [128, 512], F32); BL = Buf()
    Fb = C.psum("b_F", [128, 512], F32); _bf = Buf(); BF = [_bf, _bf]
    Sset = [(S0, BS0), (S1, BS1)]

    onesb = C.sbuf("b_ones", [128, 128], BF16); Bones = Buf()
    C.op("dve", lambda E: E.memset(onesb[:], 1.0), writes=[Bones])
    onesf = C.sbuf("b_onesf", [128, 128], F32); Bonesf = Buf()
    C.op("dve", lambda E: E.memset(onesf[:], 1.0 / 128.0), writes=[Bonesf])
    identf = C.sbuf("b_identf", [128, 128], F32); Bidf = Buf()
    C.dma("sp", identf[:], d["ident"], writes=[Bidf])
    identb = C.sbuf("b_identb", [128, 128], BF16); Bidb = Buf()
    C.op("dve", lambda E: E.tensor_copy(out=identb[:], in_=identf[:]), reads=[Bidf], writes=[Bidb])
    cvec, Bcvec = load_vec(C, "b_cvec", d["cvec"], 4)

    steps = []
    if do_ssm:
        steps = ssm_setup(C, d, out_bufs, Fb, BF, O[1], identf, Bidf)

    if do_attn:
        qT = C.sbuf("b_qT", [128, SEQ], BF16); Bq = [Buf() for _ in range(4)]
        kT = C.sbuf("b_kT", [128, SEQ], BF16); Bk = [Buf() for _ in range(4)]
        V = C.sbuf("b_V", [128, 128, 128], BF16); BV = [Buf() for _ in range(16)]
        Vf = V[:, :, :].rearrange("p b d -> p (b d)")
        for i in range(4):
            C.dma("sp", qT[:, i * 4096:(i + 1) * 4096], d["qT"][:, i * 4096:(i + 1) * 4096], writes=[Bq[i]])
            C.dma("sp", kT[:, i * 4096:(i + 1) * 4096], d["kT"][:, i * 4096:(i + 1) * 4096], writes=[Bk[i]])
        qm = C.sbuf("b_qm", [128, 4], F32); Bqm = Buf()
        qkmax = C.sbuf("b_qkmax", [128, 64], F32); Bqkmax = Buf()
        C.push_scope()
        vst = [(C.sbuf(f"b_vst{i}", [128, 1024], BF16), Buf()) for i in range(2)]
        ptb = Fb.bitcast(BF16)
        for g in range(16):
            vs, bvs = vst[g % 2]
            C.dma("sp", vs[:], d["vT"][:, g * 1024:(g + 1) * 1024], writes=[bvs])
            for i in range(8):
                C.op("pe", lambda E, vs=vs, i=i: E.transpose(out=ptb[:, i * 128:(i + 1) * 128], in_=vs[:, i * 128:(i + 1) * 128],
                                                              identity=identb[:]),
                     reads=[bvs, Bidb], writes=BF)
            C.op("dve", lambda E, g=g: E.tensor_copy(out=Vf[:, g * 1024:(g + 1) * 1024], in_=ptb[:, :]),
                 reads=BF, writes=[BV[g]])
        sqb = [(C.sbuf(f"b_sqb{i}", [128, 512], BF16), Buf()) for i in range(2)]
        for which, (src, Bsrc) in enumerate(((qT, Bq), (kT, Bk))):
            for ch in range(32):
                sq, bsq = sqb[ch % 2]
                ps, pb = O[ch % 2]
                C.op("act", lambda E, sq=sq, src=src, ch=ch: E.activation(out=sq[:], in_=src[:, ch * 512:(ch + 1) * 512], func=ACT.Square),
                     reads=[Bsrc[ch // 8]], writes=[bsq])
                C.op("pe", _mm(ps[:, :], onesb[:, :], sq[:, :], True, True), reads=[Bones, bsq], writes=[pb])
                col = which * 32 + ch
                C.op("dve", lambda E, ps=ps, col=col: E.tensor_reduce(out=qkmax[:, col:col + 1], in_=ps[:], axis=AX.X, op=ALU.max),
                     reads=[pb], writes=[Bqkmax])
        C.op("dve", lambda E: E.tensor_reduce(out=qm[:, 0:1], in_=qkmax[:, 0:32], axis=AX.X, op=ALU.max), reads=[Bqkmax], writes=[Bqm])
        C.op("dve", lambda E: E.tensor_reduce(out=qm[:, 1:2], in_=qkmax[:, 32:64], axis=AX.X, op=ALU.max), reads=[Bqkmax], writes=[Bqm])
        C.op("dve", lambda E: E.tensor_tensor(out=qm[:, 2:3], in0=qm[:, 0:1], in1=qm[:, 1:2], op=ALU.mult), reads=[Bqm], writes=[Bqm])
        C.op("act", lambda E: E.activation(out=qm[:, 3:4], in_=qm[:, 2:3], func=ACT.Sqrt), reads=[Bqm], writes=[Bqm])
        C.pop_scope()
        negc = C.sbuf("b_negc", [128, 1], F32); Bnegc = Buf()
        C.op("dve", lambda E: E.tensor_scalar(out=negc[:], in0=qm[:, 3:4], scalar1=-1.02 * 0.125, scalar2=None, op0=ALU.mult),
             reads=[Bqm], writes=[Bnegc])
        lamv = C.sbuf("b_lamv", [128, 4, 64], F32); Blamv = Buf()
        C.dma("sp", lamv[:], d["lamv"], writes=[Blamv])
        lt = C.sbuf("b_lt", [128, 2, 64], F32); Blt = Buf()
        ls = C.sbuf("b_ls", [128, 8], F32); Bls = Buf()
        for i in range(2):
            C.op("dve", lambda E, i=i: E.tensor_tensor(out=lt[:, i, :], in0=lamv[:, 2 * i, :], in1=lamv[:, 2 * i + 1, :], op=ALU.mult),
                 reads=[Blamv], writes=[Blt])
            C.op("dve", lambda E, i=i: E.tensor_reduce(out=ls[:, i:i + 1], in_=lt[:, i, :], axis=AX.X, op=ALU.add), reads=[Blt], writes=[Bls])
        C.op("act", lambda E: E.activation(out=ls[:, 2:4], in_=ls[:, 0:2], func=ACT.Exp), reads=[Bls], writes=[Bls])
        C.op("dve", lambda E: E.tensor_tensor(out=ls[:, 4:5], in0=ls[:, 2:3], in1=ls[:, 3:4], op=ALU.subtract), reads=[Bls], writes=[Bls])
        C.op("dve", lambda E: E.tensor_tensor(out=ls[:, 5:6], in0=ls[:, 4:5], in1=cvec[:, 0:1], op=ALU.add), reads=[Bls, Bcvec], writes=[Bls])
        C.op("dve", lambda E: E.tensor_scalar(out=ls[:, 6:7], in0=ls[:, 5:6], scalar1=-1.0, scalar2=None, op0=ALU.mult), reads=[Bls], writes=[Bls])
        neglam = ls[:, 6:7]
        gat, Bgat = load_vec(C, "b_gat", d["gattn"], 1)
        C.op("dve", lambda E: E.tensor_tensor(out=ls[:, 7:8], in0=gat[:, 0:1], in1=cvec[:, 1:2], op=ALU.mult), reads=[Bgat, Bcvec, Bls], writes=[Bls])
        gcoef = ls[:, 7:8]
        selT = C.sbuf("b_sel", [128, 2, 128], F32); Bsel = Buf()
        C.op("dve", lambda E: E.memset(selT[:], 0.0), writes=[Bsel])
        C.op("dve", lambda E: E.memset(selT[0:1, 0, :], 1.0), writes=[Bsel])
        C.op("dve", lambda E: E.memset(selT[64:65, 1, :], 1.0), writes=[Bsel])

        Pset = [(C.sbuf(f"b_P{i}", [128, 1024], BF16), Buf()) for i in range(3)]
        obs = [(C.sbuf(f"b_ob{i}", [128, 512], BF16), Buf()) for i in range(2)]
        fin = [[(C.sbuf(f"b_fin{j}_{i}", [128, 512], F32), Buf()) for i in range(7)] for j in range(1)]
        items = []
        for qb in range(SEQ // 512):
            nkb = 4 * qb + 4
            for kb in range(nkb):
                items.append((qb, kb, nkb))
        pending = []

        def emit_qk(i):
            qb, kb, nkb = items[i]
            S, BS = Sset[i % 2]
            col0 = max(0, 128 * (kb - 4 * qb))
            q0 = qb * 512
            for a in range(2):
                C.op("pe", _mm(S[:, a * 512 + col0:(a + 1) * 512], kT[a * 64:(a + 1) * 64, kb * 128:(kb + 1) * 128],
                               qT[a * 64:(a + 1) * 64, q0 + col0:q0 + 512], True, True),
                     reads=[Bk[(kb * 128) // 4096], Bq[q0 // 4096]], writes=[BS[a]])

        def finalize(qb, i):
            q0 = qb * 512
            (Lsb, bLsb), (Os0, bOs0), (Os1, bOs1), (o, bo), (sq, bsq), (rs, brs), (r_, br_) = fin[0]
            Os = [(Os0, bOs0), (Os1, bOs1)]
            C.op("dve", lambda E: E.tensor_copy(out=Lsb[:, :], in_=Lb[:, :]), reads=[BL], writes=[bLsb])
            C.op("act", lambda E: E.activation(out=Os0[:], in_=O[0][0][:], func=ACT.Copy), reads=[O[0][1]], writes=[bOs0])
            C.op("dve", lambda E: E.tensor_copy(out=Os1[:], in_=O[1][0][:]), reads=[O[1][1]], writes=[bOs1])

            def stage_bcast(a):
                def f():
                    C.op("pe", _mm(Fb[:, :], selT[:, a, :], Lsb[:, :], True, True), reads=[Bsel, bLsb], writes=BF)
                    C.op("dve", lambda E: E.reciprocal(out=r_[:], in_=Fb[:]), reads=BF, writes=[br_])
                    C.op("dve", lambda E: E.tensor_tensor(out=Os[a][0][:], in0=Os[a][0][:], in1=r_[:], op=ALU.mult),
                         reads=[Os[a][1], br_], writes=[Os[a][1]])
                    if a == 1:
                        C.op("dve", lambda E: E.scalar_tensor_tensor(out=o[:], in0=Os1[:], scalar=neglam, in1=Os0[:], op0=ALU.mult, op1=ALU.add),
                             reads=[bOs0, bOs1, Bls], writes=[bo])
                return f

            def stage_sq():
                C.op("act", lambda E: E.activation(out=sq[:], in_=o[:], func=ACT.Square), reads=[bo], writes=[bsq])

            def stage_ms():
                C.op("pe", _mm(Fb[:, :], onesf[:, :], sq[:, :], True, True), reads=[Bonesf, bsq], writes=BF)
                C.op("dve", lambda E: E.tensor_scalar(out=rs[:], in0=Fb[:], scalar1=EPS, scalar2=None, op0=ALU.add), reads=BF, writes=[brs])

            def stage_sqrt():
                C.op("act", lambda E: E.activation(out=rs[:], in_=rs[:], func=ACT.Sqrt), reads=[brs], writes=[brs])

            def stage_out():
                C.op("dve", lambda E: E.reciprocal(out=rs[:], in_=rs[:]), reads=[brs], writes=[brs])
                ob, bob = obs[qb % 2]
                C.op("dve", lambda E: E.scalar_tensor_tensor(out=ob[:], in0=o[:], scalar=gcoef, in1=rs[:], op0=ALU.mult, op1=ALU.mult),
                     reads=[bo, brs, Bls], writes=[bob])
                dst = Buf()
                out_bufs.append(dst)
                C.dma("sp", d["attnT"][:, q0:q0 + 512], ob[:], reads=[bob], writes=[dst])
            dls = (1, 2, 3, 4, 5, 6) if qb == 0 else (3, 6, 9, 11, 13, 15)
            for dl, fn in zip(dls, (stage_bcast(0), stage_bcast(1), stage_sq, stage_ms, stage_sqrt, stage_out)):
                pending.append((i + dl, fn))
            pending.sort(key=lambda x: x[0])

        nsteps = len(steps)
        every = max(1, len(items) // max(1, nsteps + 4))

        def emit_exp(i):
            qb, kb, nkb = items[i]
            S, BS = Sset[i % 2]
            P, BP = Pset[i % 3]
            col0 = max(0, 128 * (kb - 4 * qb))
            Sv = S[:, :].rearrange("p (a c) -> p a c", a=2)[:, :, col0:512]
            Pv = P[:, :].rearrange("p (a c) -> p a c", a=2)[:, :, col0:512]
            C.op("act", lambda E: E.activation(out=Pv, in_=Sv, func=ACT.Exp, bias=negc[:, 0:1], scale=0.125),
                 reads=[BS[0], BS[1], Bnegc], writes=[BP])
            if kb >= 4 * qb:
                Pm = P[64:128, :].rearrange("p (a c) -> p a c", a=2)[:, :, col0:col0 + 64]
                C.op("act", lambda E: E.activation(out=Pm, in_=Pm, func=ACT.Copy, scale=0.0), reads=[BP], writes=[BP])

        def emit_pv(i):
            qb, kb, nkb = items[i]
            P, BP = Pset[i % 3]
            col0 = max(0, 128 * (kb - 4 * qb))
            for a in range(2):
                for j in range(2):
                    C.op("pe", _mm(O[a][0][64 * j:64 * j + 64, col0:512], V[:, kb, 64 * j:64 * j + 64], P[:, a * 512 + col0:(a + 1) * 512],
                                   kb == 0, kb == nkb - 1),
                         reads=[BV[kb // 8], BP], writes=[O[a][1]])
            for a in range(2):
                C.op("pe", _mm(Lb[64 * a:64 * a + 64, col0:512], onesb[:, 0:64], P[:, a * 512 + col0:(a + 1) * 512], kb == 0, kb == nkb - 1),
                     reads=[Bones, BP], writes=[BL])
            if kb == nkb - 1:
                finalize(qb, i)

        emit_qk(0)
        for i in range(len(items)):
            if i + 1 < len(items):
                emit_qk(i + 1)
            while pending and pending[0][0] <= i:
                pending.pop(0)[1]()
            if steps and i % every == every - 1:
                for dl, fn in (steps.pop(0)() or ()):
                    pending.append((i + dl, fn))
                pending.sort(key=lambda x: x[0])
            emit_exp(i)
            if i >= 1:
                emit_pv(i - 1)
        emit_pv(len(items) - 1)
        while pending:
            pending.pop(0)[1]()
    while steps:
        for dl, fn in (steps.pop(0)() or ()):
            fn()


def ssm_setup(C, d, out_bufs, Fb, BF, tpbank, identf, Bidf):
    SG = 256
    tmpf = [(C.sbuf(f"s_tmp{i}", [128, SG], F32), Buf()) for i in range(10)]
    tcnt = [0]

    def nexttmp():
        r = tmpf[tcnt[0] % len(tmpf)]
        tcnt[0] += 1
        return r
    uT = C.sbuf("s_uT", [128, SEQ], BF16); Bu = [Buf() for _ in range(4)]
    for i in range(4):
        C.dma("sp", uT[:, i * 4096:(i + 1) * 4096], d["uT"][:, i * 4096:(i + 1) * 4096], writes=[Bu[i]])
    NP = 24
    pp = C.sbuf("s_pp", [128, NP, 4], F32); Bpp = Buf()
    ppi = C.sbuf("s_ppi", [128, 4], I32); Bppi = Buf()
    LR, LI, LDT, DT, TH, RHO, SIN, COS, ABR, ABI, DEN, NR, FR, FI, T0, T1, T2, KF, R0 = range(19)

    def col(i):
        return pp[:, i, :]
    C.dma("sp", col(LR), d["p_lr"], writes=[Bpp])
    C.dma("sp", col(LI), d["p_li"], writes=[Bpp])
    C.dma("sp", col(LDT), d["p_ldt"], writes=[Bpp])

    def v(fn):
        C.op("dve", fn, reads=[Bpp], writes=[Bpp])

    def a_(fn):
        C.op("act", fn, reads=[Bpp], writes=[Bpp])
    a_(lambda E: E.activation(out=col(DT), in_=col(LDT), func=ACT.Exp))
    v(lambda E: E.tensor_tensor(out=col(T0), in0=col(LR), in1=col(DT), op=ALU.mult))
    a_(lambda E: E.activation(out=col(RHO), in_=col(T0), func=ACT.Exp))
    v(lambda E: E.tensor_tensor(out=col(TH), in0=col(LI), in1=col(DT), op=ALU.mult))

    def sin_of(dst, src, shift):
        v(lambda E: E.tensor_scalar(out=col(T1), in0=col(src), scalar1=shift, scalar2=None, op0=ALU.add))
        v(lambda E: E.tensor_scalar(out=col(T2), in0=col(T1), scalar1=1.0 / (2.0 * math.pi), scalar2=None, op0=ALU.mult))
        C.op("dve", lambda E: E.tensor_copy(out=ppi[:], in_=col(T2)), reads=[Bpp], writes=[Bppi])
        C.op("dve", lambda E: E.tensor_copy(out=col(KF), in_=ppi[:]), reads=[Bppi], writes=[Bpp])
        v(lambda E: E.scalar_tensor_tensor(out=col(R0), in0=col(KF), scalar=-TWO_PI_1, in1=col(T1), op0=ALU.mult, op1=ALU.add))
        v(lambda E: E.scalar_tensor_tensor(out=col(R0), in0=col(KF), scalar=-TWO_PI_2, in1=col(R0), op0=ALU.mult, op1=ALU.add))
        v(lambda E: E.tensor_scalar(out=col(T2), in0=col(R0), scalar1=math.pi, scalar2=None, op0=ALU.is_gt))
        v(lambda E: E.scalar_tensor_tensor(out=col(R0), in0=col(T2), scalar=-2.0 * math.pi, in1=col(R0), op0=ALU.mult, op1=ALU.add))
        v(lambda E: E.tensor_scalar(out=col(T2), in0=col(R0), scalar1=-math.pi, scalar2=None, op0=ALU.is_lt))
        v(lambda E: E.scalar_tensor_tensor(out=col(R0), in0=col(T2), scalar=2.0 * math.pi, in1=col(R0), op0=ALU.mult, op1=ALU.add))
        v(lambda E: E.tensor_scalar(out=col(R0), in0=col(R0), scalar1=PI_LO, scalar2=-PI_LO, op0=ALU.min, op1=ALU.max))
        a_(lambda E: E.activation(out=col(dst), in_=col(R0), func=ACT.Sin))
    sin_of(SIN, TH, 0.0)
    sin_of(COS, TH, math.pi / 2.0)
    v(lambda E: E.tensor_tensor(out=col(ABR), in0=col(RHO), in1=col(COS), op=ALU.mult))
    v(lambda E: E.tensor_tensor(out=col(ABI), in0=col(RHO), in1=col(SIN), op=ALU.mult))
    v(lambda E: E.tensor_tensor(out=col(T0), in0=col(LR), in1=col(LR), op=ALU.mult))
    v(lambda E: E.tensor_tensor(out=col(T1), in0=col(LI), in1=col(LI), op=ALU.mult))
    v(lambda E: E.tensor_tensor(out=col(DEN), in0=col(T0), in1=col(T1), op=ALU.add))
    v(lambda E: E.reciprocal(out=col(DEN), in_=col(DEN)))
    v(lambda E: E.tensor_scalar(out=col(NR), in0=col(ABR), scalar1=-1.0, scalar2=None, op0=ALU.add))
    v(lambda E: E.tensor_tensor(out=col(T0), in0=col(NR), in1=col(LR), op=ALU.mult))
    v(lambda E: E.tensor_tensor(out=col(T1), in0=col(ABI), in1=col(LI), op=ALU.mult))
    v(lambda E: E.tensor_tensor(out=col(T0), in0=col(T0), in1=col(T1), op=ALU.add))
    v(lambda E: E.tensor_tensor(out=col(FR), in0=col(T0), in1=col(DEN), op=ALU.mult))
    v(lambda E: E.tensor_tensor(out=col(T0), in0=col(ABI), in1=col(LR), op=ALU.mult))
    v(lambda E: E.tensor_tensor(out=col(T1), in0=col(NR), in1=col(LI), op=ALU.mult))
    v(lambda E: E.tensor_tensor(out=col(T0), in0=col(T0), in1=col(T1), op=ALU.subtract))
    v(lambda E: E.tensor_tensor(out=col(FI), in0=col(T0), in1=col(DEN), op=ALU.mult))

    cosT = [C.sbuf(f"s_cos{g}", [128, SG], F32) for g in range(4)]
    sinT = [C.sbuf(f"s_sin{g}", [128, SG], F32) for g in range(4)]
    rhoT = [C.sbuf(f"s_rho{g}", [128, SG], F32) for g in range(4)]
    Btab = [Buf() for _ in range(4)]
    stp = C.sbuf("s_stp", [128, 4, 8], F32)
    Bstp = [Buf() for _ in range(4)]
    for g in range(4):
        cg_, sg_ = pp[:, COS, g:g + 1], pp[:, SIN, g:g + 1]
        rd = [Bpp, Btab[g], Bstp[g]]
        C.op("dve", lambda E, g=g: E.memset(cosT[g][:, 0:1], 1.0), writes=[Btab[g]])
        C.op("dve", lambda E, g=g: E.memset(sinT[g][:, 0:1], 0.0), writes=[Btab[g]])
        C.op("dve", lambda E, g=g: E.memset(rhoT[g][:], 1.0), writes=[Btab[g]])
        C.op("dve", lambda E, g=g: E.tensor_scalar(out=rhoT[g][:], in0=rhoT[g][:], scalar1=pp[:, RHO, g:g + 1], scalar2=None, op0=ALU.mult),
             reads=rd, writes=[Btab[g]])
        C.op("dve", lambda E, g=g, cg_=cg_: E.tensor_copy(out=stp[:, g, 0:1], in_=cg_), reads=rd, writes=[Bstp[g]])
        C.op("dve", lambda E, g=g, sg_=sg_: E.tensor_copy(out=stp[:, g, 1:2], in_=sg_), reads=rd, writes=[Bstp[g]])
        Lq = 1
        while True:
            cL, sL = stp[:, g, 0:1], stp[:, g, 1:2]
            if Lq < SG:
                n = min(Lq, SG - Lq)
                t1, b1 = nexttmp()
                C.op("dve", lambda E, g=g, t1=t1, n=n, sL=sL: E.tensor_scalar(out=t1[:, 0:n], in0=sinT[g][:, 0:n], scalar1=sL, scalar2=None, op0=ALU.mult),
                     reads=rd, writes=[b1])
                C.op("dve", lambda E, g=g, t1=t1, n=n, cL=cL, Lq=Lq: E.scalar_tensor_tensor(out=cosT[g][:, Lq:Lq + n], in0=cosT[g][:, 0:n], scalar=cL,
                                                                                         in1=t1[:, 0:n], op0=ALU.mult, op1=ALU.subtract),
                     reads=rd + [b1], writes=[Btab[g]])
                C.op("dve", lambda E, g=g, t1=t1, n=n, sL=sL: E.tensor_scalar(out=t1[:, 0:n], in0=cosT[g][:, 0:n], scalar1=sL, scalar2=None, op0=ALU.mult),
                     reads=rd + [b1], writes=[b1])
                C.op("dve", lambda E, g=g, t1=t1, n=n, cL=cL, Lq=Lq: E.scalar_tensor_tensor(out=sinT[g][:, Lq:Lq + n], in0=sinT[g][:, 0:n], scalar=cL,
                                                                                         in1=t1[:, 0:n], op0=ALU.mult, op1=ALU.add),
                     reads=rd + [b1], writes=[Btab[g]])
            if Lq >= SG:
                break
            C.op("dve", lambda E, g=g, cL=cL: E.tensor_tensor(out=stp[:, g, 2:3], in0=cL, in1=cL, op=ALU.mult), reads=rd, writes=[Bstp[g]])
            C.op("dve", lambda E, g=g, sL=sL: E.tensor_tensor(out=stp[:, g, 3:4], in0=sL, in1=sL, op=ALU.mult), reads=rd, writes=[Bstp[g]])
            C.op("dve", lambda E, g=g, cL=cL, sL=sL: E.tensor_tensor(out=stp[:, g, 4:5], in0=cL, in1=sL, op=ALU.mult), reads=rd, writes=[Bstp[g]])
            C.op("dve", lambda E, g=g: E.tensor_tensor(out=stp[:, g, 0:1], in0=stp[:, g, 2:3], in1=stp[:, g, 3:4], op=ALU.subtract), reads=rd, writes=[Bstp[g]])
            C.op("dve", lambda E, g=g: E.tensor_scalar(out=stp[:, g, 1:2], in0=stp[:, g, 4:5], scalar1=2.0, scalar2=None, op0=ALU.mult), reads=rd, writes=[Bstp[g]])
            Lq *= 2

    BbT = [[C.sbuf(f"s_bbT{g}{ri}", [128, 128], BF16) for ri in range(2)] for g in range(4)]
    CT = [[C.sbuf(f"s_cT{g}{ri}", [128, 128], BF16) for ri in range(2)] for g in range(4)]
    Bmat = [Buf() for _ in range(4)]
    Z = [(C.sbuf(f"s_Z{i}", [128, 128], F32), Buf()) for i in range(4)]
    tp, btp = tpbank
    for g in range(4):
        for i in range(4):
            C.op("dve", lambda E, i=i: E.memset(Z[i][0][:], 0.0), writes=[Z[i][1]])
        for gl in range(2):
            gg = 2 * g + gl
            rs = slice(gl * 64, gl * 64 + 64)
            cs = slice(32 * g + 16 * gl, 32 * g + 16 * gl + 16)
            C.dma("sp", Z[0][0][rs, cs], d["b_re"][gg], writes=[Z[0][1]])
            C.dma("sp", Z[1][0][rs, cs], d["b_im"][gg], writes=[Z[1][1]])
            C.dma("sp", Z[2][0][rs, cs], d["c_reT"][gg], writes=[Z[2][1]])
            C.dma("sp", Z[3][0][rs, cs], d["c_imT"][gg], writes=[Z[3][1]])
        fr, fi = pp[:, FR, g:g + 1], pp[:, FI, g:g + 1]
        t1, b1 = nexttmp()
        t2, b2 = nexttmp()
        C.op("dve", lambda E, t1=t1, fi=fi: E.tensor_scalar(out=t1[:, 0:128], in0=Z[1][0][:], scalar1=fi, scalar2=None, op0=ALU.mult),
             reads=[Z[1][1], Bpp], writes=[b1])
        C.op("dve", lambda E, t1=t1, fr=fr: E.scalar_tensor_tensor(out=t1[:, 0:128], in0=Z[0][0][:], scalar=fr, in1=t1[:, 0:128],
                                                                   op0=ALU.mult, op1=ALU.subtract),
             reads=[Z[0][1], Bpp, b1], writes=[b1])
        C.op("dve", lambda E, t2=t2, fi=fi: E.tensor_scalar(out=t2[:, 0:128], in0=Z[0][0][:], scalar1=fi, scalar2=None, op0=ALU.mult),
             reads=[Z[0][1], Bpp], writes=[b2])
        C.op("dve", lambda E, t2=t2, fr=fr: E.scalar_tensor_tensor(out=t2[:, 0:128], in0=Z[1][0][:], scalar=fr, in1=t2[:, 0:128],
                                                                   op0=ALU.mult, op1=ALU.add),
             reads=[Z[1][1], Bpp, b2], writes=[b2])
        for ri, (tt, bt) in enumerate(((t1, b1), (t2, b2))):
            C.op("pe", lambda E, tt=tt: E.transpose(out=tp[:, 0:128], in_=tt[:, 0:128], identity=identf[:]), reads=[bt, Bidf], writes=[btp])
            C.op("act", lambda E, g=g, ri=ri: E.activation(out=BbT[g][ri][:], in_=tp[:, 0:128], func=ACT.Copy), reads=[btp], writes=[Bmat[g]])
        C.op("act", lambda E, g=g: E.activation(out=CT[g][0][:], in_=Z[2][0][:], func=ACT.Copy), reads=[Z[2][1]], writes=[Bmat[g]])
        C.op("act", lambda E, g=g: E.activation(out=CT[g][1][:], in_=Z[3][0][:], func=ACT.Copy, scale=-1.0), reads=[Z[3][1]], writes=[Bmat[g]])
    dsk, Bdsk = load_vec(C, "s_dsk", d["dsk"], 1)

    init = C.sbuf("s_init", [128, 4, 2, 2], F32)
    Binit = [[Buf(), Buf()] for _ in range(4)]
    C.op("dve", lambda E: E.memset(init[:], 0.0), writes=[b for bb in Binit for b in bb])
    gin = [[(C.sbuf(f"s_gin{i}{ri}", [128, SG], F32), Buf()) for ri in range(2)] for i in range(2)]
    gst = [[(C.sbuf(f"s_g{i}{ri}", [128, SG], F32), Buf()) for ri in range(2)] for i in range(2)]
    hbf = [[[(C.sbuf(f"s_h{i}{g}{ri}", [128, SG], BF16), Buf()) for ri in range(2)] for g in range(4)] for i in range(2)]
    yob = [(C.sbuf(f"s_yo{i}", [128, SG], BF16), Buf()) for i in range(2)]
    ytmp = [[(C.sbuf(f"s_yt{i}{j}", [128, SG], F32), Buf()) for j in range(2)] for i in range(2)]
    psR, psI = Fb[:, 0:SG], Fb[:, SG:2 * SG]
    psY = Fb[:, 0:SG]
    itc = [0]

    def step_bu(seg, g):
        def f():
            t0 = seg * SG
            bu_ = Bu[t0 // 4096]
            par = itc[0] % 2
            itc[0] += 1
            for ri in range(2):
                C.op("pe", _mm(Fb[:, ri * SG:(ri + 1) * SG], BbT[g][ri][:, :], uT[:, t0:t0 + SG], True, True),
                     reads=[Bmat[g], bu_], writes=[BF[ri]])
            m1, bm1 = nexttmp(); m2, bm2 = nexttmp(); m3, bm3 = nexttmp(); m4, bm4 = nexttmp()
            C.op("dve", lambda E: E.tensor_tensor(out=m1[:], in0=psR, in1=cosT[g][:], op=ALU.mult), reads=[BF[0], Btab[g]], writes=[bm1])
            C.op("dve", lambda E: E.tensor_tensor(out=m4[:], in0=psR, in1=sinT[g][:], op=ALU.mult), reads=[BF[0], Btab[g]], writes=[bm4])
            C.op("dve", lambda E: E.tensor_tensor(out=m2[:], in0=psI, in1=sinT[g][:], op=ALU.mult), reads=[BF[1], Btab[g]], writes=[bm2])
            C.op("dve", lambda E: E.tensor_tensor(out=m3[:], in0=psI, in1=cosT[g][:], op=ALU.mult), reads=[BF[1], Btab[g]], writes=[bm3])
            (gir, bgir), (gii, bgii) = gin[par]
            C.op("dve", lambda E: E.tensor_tensor(out=gir[:], in0=m1[:], in1=m2[:], op=ALU.add), reads=[bm1, bm2], writes=[bgir])
            C.op("dve", lambda E: E.tensor_tensor(out=gii[:], in0=m3[:], in1=m4[:], op=ALU.subtract), reads=[bm3, bm4], writes=[bgii])
            (gr, bgr), (gi, bgi) = gst[par]
            ip = seg % 2
            C.op("dve", lambda E: E.tensor_tensor_scan(out=gr[:], data0=rhoT[g][:], data1=gir[:], initial=init[:, g, ip, 0:1],
                                                       op0=ALU.mult, op1=ALU.add),
                 reads=[Btab[g], bgir, Binit[g][ip]], writes=[bgr])
            C.op("dve", lambda E: E.tensor_tensor_scan(out=gi[:], data0=rhoT[g][:], data1=gii[:], initial=init[:, g, ip, 1:2],
                                                       op0=ALU.mult, op1=ALU.add),
                 reads=[Btab[g], bgii, Binit[g][ip]], writes=[bgi])
            cS, sS = stp[:, g, 0:1], stp[:, g, 1:2]
            nx = 1 - ip
            rdn = [bgr, bgi, Bstp[g], Binit[g][nx]]
            C.op("dve", lambda E: E.tensor_scalar(out=stp[:, g, 5:6], in0=gi[:, SG - 1:SG], scalar1=sS, scalar2=None, op0=ALU.mult),
                 reads=rdn, writes=[Bstp[g]])
            C.op("dve", lambda E: E.scalar_tensor_tensor(out=init[:, g, nx, 0:1], in0=gr[:, SG - 1:SG], scalar=cS, in1=stp[:, g, 5:6],
                                                         op0=ALU.mult, op1=ALU.subtract),
                 reads=rdn, writes=[Binit[g][nx]])
            C.op("dve", lambda E: E.tensor_scalar(out=stp[:, g, 6:7], in0=gr[:, SG - 1:SG], scalar1=sS, scalar2=None, op0=ALU.mult),
                 reads=rdn, writes=[Bstp[g]])
            C.op("dve", lambda E: E.scalar_tensor_tensor(out=init[:, g, nx, 1:2], in0=gi[:, SG - 1:SG], scalar=cS, in1=stp[:, g, 6:7],
                                                         op0=ALU.mult, op1=ALU.add),
                 reads=rdn, writes=[Binit[g][nx]])
            n1, bn1 = nexttmp(); n2, bn2 = nexttmp(); n3, bn3 = nexttmp(); n4, bn4 = nexttmp()
            C.op("pool", lambda E: E.tensor_tensor(out=n1[:], in0=gr[:], in1=cosT[g][:], op=ALU.mult), reads=[bgr, Btab[g]], writes=[bn1])
            C.op("pool", lambda E: E.tensor_tensor(out=n2[:], in0=gi[:], in1=sinT[g][:], op=ALU.mult), reads=[bgi, Btab[g]], writes=[bn2])
            C.op("pool", lambda E: E.tensor_tensor(out=n3[:], in0=gr[:], in1=sinT[g][:], op=ALU.mult), reads=[bgr, Btab[g]], writes=[bn3])
            C.op("pool", lambda E: E.tensor_tensor(out=n4[:], in0=gi[:], in1=cosT[g][:], op=ALU.mult), reads=[bgi, Btab[g]], writes=[bn4])
            (hr, bhr), (hi, bhi) = hbf[seg % 2][g]
            C.op("pool", lambda E: E.tensor_tensor(out=hr[:], in0=n1[:], in1=n2[:], op=ALU.subtract), reads=[bn1, bn2], writes=[bhr])
            C.op("pool", lambda E: E.tensor_tensor(out=hi[:], in0=n3[:], in1=n4[:], op=ALU.add), reads=[bn3, bn4], writes=[bhi])
        return f

    def step_y(seg):
        def f():
            t0 = seg * SG
            bu_ = Bu[t0 // 4096]
            for g in range(4):
                (hr, bhr), (hi, bhi) = hbf[seg % 2][g]
                C.op("pe", _mm(psY, CT[g][0][:, :], hr[:, :], g == 0, False), reads=[Bmat[g], bhr], writes=[BF[0]])
                C.op("pe", _mm(psY, CT[g][1][:, :], hi[:, :], False, g == 3), reads=[Bmat[g], bhi], writes=[BF[0]])
            yt, byt = ytmp[seg % 2][0]
            w, bw = ytmp[seg % 2][1]
            C.op("dve", lambda E: E.scalar_tensor_tensor(out=yt[:], in0=uT[:, t0:t0 + SG], scalar=dsk[:, 0:1], in1=psY, op0=ALU.mult, op1=ALU.add),
                 reads=[bu_, Bdsk, BF[0]], writes=[byt])
            C.op("pool", lambda E: E.tensor_tensor(out=w[:], in0=yt[:], in1=yt[:], op=ALU.mult), reads=[byt], writes=[bw])
            C.op("pool", lambda E: E.tensor_scalar(out=w[:], in0=w[:], scalar1=0.044715, scalar2=1.0, op0=ALU.mult, op1=ALU.add), reads=[bw], writes=[bw])
            C.op("pool", lambda E: E.tensor_tensor(out=w[:], in0=w[:], in1=yt[:], op=ALU.mult), reads=[bw, byt], writes=[bw])

            def part2():
                C.op("act", lambda E: E.activation(out=w[:], in_=w[:], func=ACT.Sigmoid, scale=1.5957691216057308), reads=[bw], writes=[bw])
                yo, byo = yob[seg % 2]
                C.op("pool", lambda E: E.tensor_tensor(out=yo[:], in0=yt[:], in1=w[:], op=ALU.mult), reads=[bw, byt], writes=[byo])
                dst = Buf()
                out_bufs.append(dst)
                C.dma("sp", d["yT"][:, t0:t0 + SG], yo[:], reads=[byo], writes=[dst])
            return [(5, part2)]
        return f

    steps = []
    nseg = SEQ // SG
    for seg in range(nseg):
        for g in range(4):
            steps.append(step_bu(seg, g))
            if g == 0 and seg > 0:
                steps.append(step_y(seg - 1))
    steps.append(step_y(nseg - 1))
    return steps


from concourse.bass_utils import run_bass_kernel_spmd
import ml_dtypes

NCORES = 8
_BF = ml_dtypes.bfloat16
_PROGS = {}


def _din(nc, name, shape, dt=F32):
    return nc.dram_tensor(name, list(shape), dt, kind="ExternalInput").ap()


def _dout(nc, name, shape, dt=F32):
    return nc.dram_tensor(name, list(shape), dt, kind="ExternalOutput").ap()


def _vec128(v):
    return np.ascontiguousarray(np.asarray(v, np.float32).reshape(-1, 128).T)


def build_A():
    nc = bass.Bass("TRN2", target_bir_lowering=False)
    d = dict(xT=_din(nc, "xT", [D, TOK]), w_in=_din(nc, "w_in", [D, 4096]), projT=_dout(nc, "projT", [4096, TOK], BF16))
    C = Ctx(nc)
    S = Shared(C)
    outs = []
    phase_A(C, S, d, outs)
    C.finish(outs, "sp")
    C.emit()
    return nc


def build_B():
    nc = bass.Bass("TRN2", target_bir_lowering=False)
    d = dict(qT=_din(nc, "qT", [128, SEQ], BF16), kT=_din(nc, "kT", [128, SEQ], BF16), vT=_din(nc, "vT", [128, SEQ], BF16),
             uT=_din(nc, "uT", [128, SEQ], BF16), ident=_din(nc, "ident", [128, 128]), lamv=_din(nc, "lamv", [128, 4, 64]),
             cvec=_din(nc, "cvec", [128, 4]), gattn=_din(nc, "gattn", [128, 1]),
             p_lr=_din(nc, "p_lr", [128, 4]), p_li=_din(nc, "p_li", [128, 4]), p_ldt=_din(nc, "p_ldt", [128, 4]),
             b_re=_din(nc, "b_re", [8, 64, 16]), b_im=_din(nc, "b_im", [8, 64, 16]), c_reT=_din(nc, "c_reT", [8, 64, 16]),
             c_imT=_din(nc, "c_imT", [8, 64, 16]), dsk=_din(nc, "dsk", [128, 1]),
             attnT=_dout(nc, "attnT", [128, SEQ], BF16), yT=_dout(nc, "yT", [128, SEQ], BF16))
    C = Ctx(nc)
    outs = []
    phase_B(C, d, outs)
    C.finish(outs, "sp")
    C.emit()
    return nc


def build_C(do_proj):
    nc = bass.Bass("TRN2", target_bir_lowering=False)
    d = dict(xres=_din(nc, "xres", [D, TOK]), attnT=_din(nc, "attnT", [1024, TOK], BF16), yT=_din(nc, "yT", [1024, TOK], BF16),
             glu_w=_din(nc, "glu_w", [1024, 1024]), glu_b=_din(nc, "glu_b", [128, 8]), ssm_g=_din(nc, "ssm_g", [128, 8]),
             w_out=_din(nc, "w_out", [D, D]), ln1_g=_din(nc, "ln1_g", [128, 16]), ln1_b=_din(nc, "ln1_b", [128, 16]),
             w_up=_din(nc, "w_up", [D, DFF]), w_down=_din(nc, "w_down", [DFF, D]), ln2_g=_din(nc, "ln2_g", [128, 16]),
             ln2_b=_din(nc, "ln2_b", [128, 16]), bd16=_din(nc, "bd16", [128, 128]), xout=_dout(nc, "xout", [D, TOK]))
    if do_proj:
        d["w_in"] = _din(nc, "w_in", [D, 4096])
        d["projT"] = _dout(nc, "projT", [4096, TOK], BF16)
    C = Ctx(nc)
    S = Shared(C)
    outs = []
    phase_C(C, S, d, do_proj, outs)
    C.finish(outs, "sp")
    C.emit()
    return nc


def _host_B_inputs(inp, l, projT_full, h):
    lam_init = 0.8 - 0.6 * math.exp(-0.3 * l)
    m = {}
    m["qT"] = np.ascontiguousarray(projT_full[h * 128:(h + 1) * 128])
    m["kT"] = np.ascontiguousarray(projT_full[1024 + h * 128:1024 + (h + 1) * 128])
    m["vT"] = np.ascontiguousarray(projT_full[2048 + h * 128:2048 + (h + 1) * 128])
    m["uT"] = np.ascontiguousarray(projT_full[3072 + h * 128:3072 + (h + 1) * 128])
    m["ident"] = np.eye(128, dtype=np.float32)
    lam4 = np.stack([inp["lambda_q1"][l], inp["lambda_k1"][l], inp["lambda_q2"][l], inp["lambda_k2"][l]])
    m["lamv"] = np.ascontiguousarray(np.broadcast_to(lam4[None], (128, 4, 64))).astype(np.float32)
    cv = np.zeros((128, 4), np.float32)
    cv[:, 0] = lam_init
    cv[:, 1] = 1.0 - lam_init
    m["cvec"] = cv
    m["gattn"] = np.ascontiguousarray(inp["attn_norm_g"][l][h * 128:(h + 1) * 128].reshape(128, 1)).astype(np.float32)
    gs = slice(8 * h, 8 * h + 8)

    def pcol(a):
        return np.ascontiguousarray(np.asarray(a).reshape(4, 128).T).astype(np.float32)
    m["p_lr"] = pcol(inp["ssm_lambda_re"][l][gs])
    m["p_li"] = pcol(inp["ssm_lambda_im"][l][gs])
    m["p_ldt"] = pcol(np.broadcast_to(inp["ssm_log_dt"][l][gs][:, None], (8, 64)))
    m["b_re"] = np.ascontiguousarray(inp["ssm_b_re"][l][gs]).astype(np.float32)
    m["b_im"] = np.ascontiguousarray(inp["ssm_b_im"][l][gs]).astype(np.float32)
    m["c_reT"] = np.ascontiguousarray(inp["ssm_c_re"][l][gs].transpose(0, 2, 1)).astype(np.float32)
    m["c_imT"] = np.ascontiguousarray(inp["ssm_c_im"][l][gs].transpose(0, 2, 1)).astype(np.float32)
    m["dsk"] = np.ascontiguousarray(inp["ssm_d"][l][h * 128:(h + 1) * 128].reshape(128, 1)).astype(np.float32)
    return m


def _host_C_inputs(inp, l, do_proj):
    m = dict(glu_w=np.ascontiguousarray(inp["glu_w"][l]), glu_b=_vec128(inp["glu_b"][l]), ssm_g=_vec128(inp["ssm_norm_g"][l]),
             w_out=np.ascontiguousarray(inp["w_out"][l]), ln1_g=_vec128(inp["ln1_g"][l]), ln1_b=_vec128(inp["ln1_b"][l]),
             w_up=np.ascontiguousarray(inp["w_up"][l]), w_down=np.ascontiguousarray(inp["w_down"][l]),
             ln2_g=_vec128(inp["ln2_g"][l]), ln2_b=_vec128(inp["ln2_b"][l]),
             bd16=np.kron(np.eye(8), np.ones((16, 16)) / 16).astype(np.float32))
    if do_proj:
        m["w_in"] = np.ascontiguousarray(inp["w_in"][l + 1])
    return m


def _prog(key, fn):
    if key not in _PROGS:
        _PROGS[key] = fn()
    return _PROGS[key]


def kernel(**inputs):
    inp = {k: np.asarray(v) for k, v in inputs.items()}
    cores = list(range(NCORES))
    x = inp["x"][0]
    xT = [np.ascontiguousarray(x[c * TOK:(c + 1) * TOK].T) for c in cores]
    res = run_bass_kernel_spmd(_prog("A", build_A), [dict(xT=xT[c], w_in=np.ascontiguousarray(inp["w_in"][0])) for c in cores],
                               core_ids=cores)
    projT = np.concatenate([res.results[c]["projT"] for c in cores], axis=1)
    for l in range(DEPTH):
        res = run_bass_kernel_spmd(_prog("B", build_B), [_host_B_inputs(inp, l, projT, h) for h in cores], core_ids=cores)
        attnT = np.concatenate([res.results[h]["attnT"] for h in cores], axis=0)
        yT = np.concatenate([res.results[h]["yT"] for h in cores], axis=0)
        do_proj = l + 1 < DEPTH
        common = _host_C_inputs(inp, l, do_proj)
        maps = []
        for c in cores:
            m = dict(common)
            m["xres"] = xT[c]
            m["attnT"] = np.ascontiguousarray(attnT[:, c * TOK:(c + 1) * TOK])
            m["yT"] = np.ascontiguousarray(yT[:, c * TOK:(c + 1) * TOK])
            maps.append(m)
        res = run_bass_kernel_spmd(_prog(("C", do_proj), lambda: build_C(do_proj)), maps, core_ids=cores)
        xT = [res.results[c]["xout"] for c in cores]
        if do_proj:
            projT = np.concatenate([res.results[c]["projT"] for c in cores], axis=1)
    out = np.concatenate([np.ascontiguousarray(xT[c].T) for c in cores], axis=0)[None]
    return out.astype(np.float32)
```

```python
import numpy as np
from contextlib import ExitStack
import concourse.bass as bass
import concourse.mybir as mybir

ACT = mybir.ActivationFunctionType
ALU = mybir.AluOpType
AX = mybir.AxisListType
F32 = mybir.dt.float32
BF16 = mybir.dt.bfloat16
I32 = mybir.dt.int32

SEM_ROLL = 30000
DMA_RING = 8


class Buf:
    __slots__ = ("name", "w", "r")

    def __init__(self, name=""):
        self.name = name
        self.w = None
        self.r = {}


class Ctx:
    ENG = ("pe", "act", "dve", "pool", "sp")

    def __init__(self, nc, same_engine_sync=True):
        self.nc = nc
        self.es = ExitStack()
        self.stream = {e: [] for e in self.ENG}
        self.same_engine_sync = same_engine_sync
        self.nsem = 0
        self.cur = {}
        for e in ("pe", "act", "dve", "pool"):
            self.cur[e] = [self._newsem(e), 0]
        self.ring = {}
        self.ringpos = {}
        for q in ("sp", "act", "pool"):
            self.ring[q] = [[self._newsem("dq" + q), 0] for _ in range(DMA_RING)]
            self.ringpos[q] = 0
        self.waited = {}
        self.n_inst = 0
        self.n_wait = 0
        self.scopes = []

    def _newsem(self, name):
        self.nsem += 1
        return self.es.enter_context(self.nc.semaphore(f"s{self.nsem}_{name}"))

    def push_scope(self):
        self.scopes.append(ExitStack())

    def pop_scope(self):
        self.barrier()
        self.scopes.pop().close()

    def barrier(self):
        toks = []
        for e, (sem, cnt) in self.cur.items():
            if cnt > 0:
                toks.append((sem, cnt))
        for q, ring in self.ring.items():
            for sem, cnt in ring:
                if cnt > 0:
                    toks.append((sem, cnt))
        for e in self.ENG:
            own = self.cur[e][0] if e in self.cur else None
            self._need(e, [t for t in toks if t[0] is not own])

    def sbuf(self, name, shape, dt):
        es = self.scopes[-1] if self.scopes else self.es
        return es.enter_context(self.nc.sbuf_tensor(name, list(shape), dt))

    def psum(self, name, shape, dt=F32):
        return self.es.enter_context(self.nc.psum_tensor(name, list(shape), dt))

    def _need(self, eng, deps):
        best = {}
        for tok in deps:
            if tok is None:
                continue
            sem, val = tok
            k = id(sem)
            if k not in best or best[k][1] < val:
                best[k] = (sem, val)
        for k, (sem, val) in best.items():
            if self.waited.get((eng, k), 0) >= val:
                continue
            self.waited[(eng, k)] = val
            self.n_wait += 1
            self.stream[eng].append(lambda E, sem=sem, val=val: E.wait_ge(sem, val))

    def _deps(self, eng, reads, writes):
        own = self.cur[eng][0] if eng in self.cur else None
        deps = []
        for b in reads:
            if b.w is not None:
                deps.append(b.w)
        for b in writes:
            if b.w is not None:
                deps.append(b.w)
            deps.extend(b.r.values())
        if own is not None and (eng == "pe" or not self.same_engine_sync):
            deps = [d for d in deps if d[0] is not own]
        return deps

    def _mark(self, tok, reads, writes):
        for b in writes:
            b.w = tok
            b.r = {}
        for b in reads:
            b.r[id(tok[0])] = tok

    def op(self, eng, fn, reads=(), writes=()):
        self._need(eng, self._deps(eng, reads, writes))
        c = self.cur[eng]
        if c[1] >= SEM_ROLL:
            c[0] = self._newsem(eng)
            c[1] = 0
        c[1] += 1
        sem, val = c[0], c[1]
        self.n_inst += 1
        self.stream[eng].append(lambda E, fn=fn, sem=sem: fn(E).then_inc(sem, 1))
        self._mark((sem, val), reads, writes)

    def dma(self, q, out, in_, reads=(), writes=(), **kw):
        ring = self.ring[q]
        pos = self.ringpos[q]
        self.ringpos[q] = (pos + 1) % len(ring)
        slot = ring[pos]
        deps = self._deps(q, reads, writes)
        if slot[1] > 0:
            deps.append((slot[0], slot[1]))
        self._need(q, deps)
        slot[1] += 16
        sem, val = slot[0], slot[1]
        self.n_inst += 1
        self.stream[q].append(
            lambda E, out=out, in_=in_, sem=sem, kw=kw: E.dma_start(out=out, in_=in_, **kw).then_inc(sem, 16))
        self._mark((sem, val), reads, writes)

    def finish(self, bufs, eng="sp"):
        self._need(eng, [b.w for b in bufs])

    def emit(self):
        nc = self.nc
        st = self.stream
        with nc.Block() as block:
            @block.sync
            def _(E):
                for f in st["sp"]:
                    f(E)

            @block.tensor
            def _(E):
                for f in st["pe"]:
                    f(E)

            @block.scalar
            def _(E):
                for f in st["act"]:
                    f(E)

            @block.vector
            def _(E):
                for f in st["dve"]:
                    f(E)

            @block.gpsimd
            def _(E):
                for f in st["pool"]:
                    f(E)
        self.es.close()


import math
import numpy as np

D = 2048
DFF = 8192
TOK = 2048
T = 512
SEQ = 16384
DEPTH = 4
ALPHA = (2.0 * DEPTH) ** 0.25
EPS = 1e-5
NW = 6


class Shared:
    def __init__(self, C):
        self.C = C
        self.ps = []
        for i in range(8):
            self.ps.append((C.psum(f"ps{i}", [128, 512], F32), Buf(f"ps{i}")))
        self.pset_i = 0
        self.w = [(C.sbuf(f"w{i}", [128, 4, 512], BF16), Buf(f"w{i}")) for i in range(NW)]
        self.w_i = 0

    def next_pset(self):
        s = self.ps[self.pset_i * 4:(self.pset_i + 1) * 4]
        self.pset_i ^= 1
        return s

    def next_w(self):
        r = self.w[self.w_i]
        self.w_i = (self.w_i + 1) % NW
        return r


def _mm(ps, lhsT, rhs, start, stop):
    return lambda E: E.matmul(ps, lhsT, rhs, start=start, stop=stop)


def stream_mm(C, S, W, K, M, rhs, rhs_bufs, evac):
    ncg = M // 512
    nks = K // 512
    nk = K // 128
    for cg in range(ncg):
        pset = S.next_pset()
        for ks in range(nks):
            wt, wb = S.next_w()
            src = W[ks * 512:(ks + 1) * 512, cg * 512:(cg + 1) * 512].rearrange("(kc p) n -> p kc n", p=128)
            C.dma("pool", wt[:], src, writes=[wb])
            for kc in range(4):
                kk = ks * 4 + kc
                for m in range(4):
                    ps, pb = pset[m]
                    C.op("pe", _mm(ps[:, :], wt[:, kc, m * 128:(m + 1) * 128], rhs(kk), kk == 0, kk == nk - 1),
                         reads=[wb] + rhs_bufs(kk), writes=[pb])
        for m in range(4):
            evac(cg * 4 + m, pset[m][0], pset[m][1])


def load_vec(C, name, ap, ncols):
    t = C.sbuf(name, [128, ncols], F32)
    b = Buf(name)
    C.dma("sp", t[:], ap, writes=[b])
    return t, b


def phase_A(C, S, d, out_bufs):
    xr = C.sbuf("a_xr", [128, 16, T], F32); Bxr = Buf()
    xb = C.sbuf("a_xb", [128, 16, T], BF16); Bxb = [Buf() for _ in range(16)]
    stg = [(C.sbuf(f"a_stg{i}", [128, T], BF16), Buf()) for i in range(4)]
    for t in range(TOK // T):
        tok = slice(t * T, (t + 1) * T)
        C.dma("sp", xr[:], d["xT"][:, tok].rearrange("(kc p) n -> p kc n", p=128), writes=[Bxr])
        for m in range(16):
            C.op("act", lambda E, m=m: E.activation(out=xb[:, m, :], in_=xr[:, m, :], func=ACT.Copy),
                 reads=[Bxr], writes=[Bxb[m]])
        proj_stage(C, S, d["w_in"], d["projT"], xb, Bxb, stg, tok, out_bufs)


def proj_stage(C, S, w_in, projT, xb, Bxb, stg, tok, out_bufs):
    cnt = [0]

    def evac(m, ps, pb):
        st, sb = stg[cnt[0] % len(stg)]
        cnt[0] += 1
        C.op("act", lambda E: E.activation(out=st[:], in_=ps[:], func=ACT.Copy), reads=[pb], writes=[sb])
        ob = Buf()
        out_bufs.append(ob)
        C.dma("sp", projT[m * 128:(m + 1) * 128, tok], st[:], reads=[sb], writes=[ob])
    stream_mm(C, S, w_in, D, 4096, lambda kk: xb[:, kk, :], lambda kk: [Bxb[kk]], evac)


def phase_C(C, S, d, do_proj, out_bufs):
    nc = C.nc
    xr = C.sbuf("c_xr", [128, 16, T], F32); Bxr = [Buf() for _ in range(16)]
    xb = C.sbuf("c_xb", [128, 16, T], BF16); Bxb = [Buf() for _ in range(16)]
    sF32 = xb.bitcast(F32)

    def sFv(m):
        return sF32[:, 2 * m:2 * m + 2, :].rearrange("p a c -> p (a c)")
    cat = C.sbuf("c_cat", [128, 16, T], BF16); Bcat = [Buf() for _ in range(16)]
    yb = C.sbuf("c_yb", [128, 8, T], BF16); Byb = Buf()
    hb = C.sbuf("c_hb", [128, 64, T], BF16); Bhb = [Buf() for _ in range(64)]
    hF32 = hb.bitcast(F32)

    def hFv(m):
        return hF32[:, 2 * m:2 * m + 2, :].rearrange("p a c -> p (a c)")
    tmp = [(C.sbuf(f"c_tmp{i}", [128, T], F32), Buf()) for i in range(4)]
    mean_sb = C.sbuf("c_mean", [128, T], F32); Bmean = Buf()
    rstd = C.sbuf("c_rstd", [128, T], F32); Brstd = Buf()
    var_sb = C.sbuf("c_var", [128, T], F32); Bvar = Buf()
    stg = [(C.sbuf(f"c_stg{i}", [128, T], BF16), Buf()) for i in range(4)]
    glu_b, Bglu_b = load_vec(C, "c_glub", d["glu_b"], 8)
    ssm_g, Bssm_g = load_vec(C, "c_ssmg", d["ssm_g"], 8)
    ln1g, Bln1g = load_vec(C, "c_ln1g", d["ln1_g"], 16)
    ln1b, Bln1b = load_vec(C, "c_ln1b", d["ln1_b"], 16)
    ln2g, Bln2g = load_vec(C, "c_ln2g", d["ln2_g"], 16)
    ln2b, Bln2b = load_vec(C, "c_ln2b", d["ln2_b"], 16)
    bd16 = C.sbuf("c_bd16", [128, 128], F32); Bbd16 = Buf()
    C.dma("sp", bd16[:], d["bd16"], writes=[Bbd16])
    onesb = C.sbuf("c_ones", [128, 128], BF16); Bones = Buf()
    C.op("dve", lambda E: E.memset(onesb[:], 1.0 / D), writes=[Bones])
    tcnt = [0]

    def nexttmp():
        r = tmp[tcnt[0] % len(tmp)]
        tcnt[0] += 1
        return r

    def evac_res(sq, Bsq):
        def f(m, ps, pb):
            C.op("dve", lambda E: E.scalar_tensor_tensor(out=xr[:, m, :], in0=xr[:, m, :], scalar=ALPHA, in1=ps[:],
                                                          op0=ALU.mult, op1=ALU.add),
                 reads=[Bxr[m], pb], writes=[Bxr[m]])
            C.op("act", lambda E: E.activation(out=xb[:, m, :], in_=xr[:, m, :], func=ACT.Copy),
                 reads=[Bxr[m]], writes=[Bxb[m]])
            C.op("act", lambda E: E.activation(out=sq[:, m, :], in_=xr[:, m, :], func=ACT.Square),
                 reads=[Bxr[m]], writes=[Bsq[m]])
        return f

    def layer_norm(g, Bg, b, Bb, sq, Bsq):
        pset = S.next_pset()
        (psA, pbA), (psB, pbB) = pset[0], pset[1]
        for m in range(16):
            C.op("pe", _mm(psA[:, :], onesb[:, :], xb[:, m, :], m == 0, m == 15), reads=[Bones, Bxb[m]], writes=[pbA])
        for m in range(16):
            C.op("pe", _mm(psB[:, :], onesb[:, :], sq[:, m, :], m == 0, m == 15), reads=[Bones, Bsq[m]], writes=[pbB])
        C.op("dve", lambda E: E.tensor_copy(out=mean_sb[:], in_=psA[:]), reads=[pbA], writes=[Bmean])
        C.op("dve", lambda E: E.tensor_tensor(out=var_sb[:], in0=mean_sb[:], in1=mean_sb[:], op=ALU.mult),
             reads=[Bmean], writes=[Bvar])
        C.op("dve", lambda E: E.tensor_tensor(out=var_sb[:], in0=psB[:], in1=var_sb[:], op=ALU.subtract),
             reads=[pbB, Bvar], writes=[Bvar])
        C.op("dve", lambda E: E.tensor_scalar(out=var_sb[:], in0=var_sb[:], scalar1=EPS, scalar2=None, op0=ALU.add),
             reads=[Bvar], writes=[Bvar])
        C.op("act", lambda E: E.activation(out=rstd[:], in_=var_sb[:], func=ACT.Sqrt), reads=[Bvar], writes=[Brstd])
        C.op("dve", lambda E: E.reciprocal(out=rstd[:], in_=rstd[:]), reads=[Brstd], writes=[Brstd])
        for m in range(16):
            t1, b1 = nexttmp()
            C.op("dve", lambda E, m=m, t1=t1: E.tensor_tensor(out=t1[:], in0=xr[:, m, :], in1=mean_sb[:], op=ALU.subtract),
                 reads=[Bxr[m], Bmean], writes=[b1])
            C.op("dve", lambda E, m=m, t1=t1: E.tensor_tensor(out=t1[:], in0=t1[:], in1=rstd[:], op=ALU.mult),
                 reads=[b1, Brstd], writes=[b1])
            C.op("act", lambda E, m=m, t1=t1: E.activation(out=xr[:, m, :], in_=t1[:], func=ACT.Identity,
                                                            scale=g[:, m:m + 1], bias=b[:, m:m + 1]),
                 reads=[b1, Bg, Bb], writes=[Bxr[m]])
            C.op("act", lambda E, m=m: E.activation(out=xb[:, m, :], in_=xr[:, m, :], func=ACT.Copy),
                 reads=[Bxr[m]], writes=[Bxb[m]])

    def emit_loads(t):
        tok = slice(t * T, (t + 1) * T)
        C.dma("sp", yb[:], d["yT"][:, tok].rearrange("(kc p) n -> p kc n", p=128), writes=[Byb])
        for m in range(8):
            C.dma("sp", cat[:, m, :], d["attnT"][m * 128:(m + 1) * 128, tok], writes=[Bcat[m]])
        for m in range(16):
            C.dma("sp", xr[:, m, :], d["xres"][m * 128:(m + 1) * 128, tok], writes=[Bxr[m]])

    emit_loads(0)
    for t in range(TOK // T):
        tok = slice(t * T, (t + 1) * T)

        def evac_glu(m, ps, pb):
            t1, b1 = nexttmp()
            C.op("act", lambda E: E.activation(out=t1[:], in_=ps[:], func=ACT.Sigmoid, bias=glu_b[:, m:m + 1]),
                 reads=[pb, Bglu_b], writes=[b1])
            C.op("dve", lambda E: E.tensor_tensor(out=sFv(m), in0=yb[:, m, :], in1=t1[:], op=ALU.mult),
                 reads=[Byb, b1], writes=[Bxb[2 * m], Bxb[2 * m + 1]])
        stream_mm(C, S, d["glu_w"], 1024, 1024, lambda kk: yb[:, kk, :], lambda kk: [Byb], evac_glu)
        psA = S.next_pset()
        psB = S.next_pset()
        ps8 = psA + psB
        for m in range(8):
            sfb = [Bxb[2 * m], Bxb[2 * m + 1]]
            C.op("act", lambda E, m=m: E.activation(out=hFv(m), in_=sFv(m), func=ACT.Square), reads=sfb, writes=[Bhb[2 * m], Bhb[2 * m + 1]])
        for m in range(8):
            ps, pb = ps8[m]
            C.op("pe", _mm(ps[:, :], bd16[:, :], hFv(m), True, True), reads=[Bbd16, Bhb[2 * m], Bhb[2 * m + 1]], writes=[pb])
        for m in range(8):
            ps, pb = ps8[m]
            sfb = [Bxb[2 * m], Bxb[2 * m + 1]]
            t2, b2 = nexttmp()
            C.op("dve", lambda E, ps=ps, t2=t2: E.tensor_scalar(out=t2[:], in0=ps[:], scalar1=EPS, scalar2=None, op0=ALU.add),
                 reads=[pb], writes=[b2])
            C.op("act", lambda E, t2=t2: E.activation(out=t2[:], in_=t2[:], func=ACT.Sqrt), reads=[b2], writes=[b2])
            C.op("dve", lambda E, t2=t2: E.reciprocal(out=t2[:], in_=t2[:]), reads=[b2], writes=[b2])
            C.op("dve", lambda E, m=m, t2=t2: E.scalar_tensor_tensor(out=cat[:, 8 + m, :], in0=sFv(m), scalar=ssm_g[:, m:m + 1],
                                                                    in1=t2[:], op0=ALU.mult, op1=ALU.mult),
                 reads=sfb + [b2, Bssm_g], writes=[Bcat[8 + m]])
        sq1 = hb
        stream_mm(C, S, d["w_out"], D, D, lambda kk: cat[:, kk, :], lambda kk: [Bcat[kk]], evac_res(sq1, Bhb))
        layer_norm(ln1g, Bln1g, ln1b, Bln1b, sq1, Bhb)

        def evac_up(j, ps, pb):
            t1, b1 = nexttmp()
            C.op("act", lambda E: E.activation(out=t1[:], in_=ps[:], func=ACT.Relu), reads=[pb], writes=[b1])
            C.op("dve", lambda E: E.tensor_tensor(out=hb[:, j, :], in0=t1[:], in1=t1[:], op=ALU.mult), reads=[b1], writes=[Bhb[j]])
        stream_mm(C, S, d["w_up"], D, DFF, lambda kk: xb[:, kk, :], lambda kk: [Bxb[kk]], evac_up)
        stream_mm(C, S, d["w_down"], DFF, D, lambda kk: hb[:, kk, :], lambda kk: [Bhb[kk]], evac_res(cat, Bcat))
        layer_norm(ln2g, Bln2g, ln2b, Bln2b, cat, Bcat)
        for m in range(16):
            ob = Buf()
            out_bufs.append(ob)
            C.dma("sp", d["xout"][m * 128:(m + 1) * 128, tok], xr[:, m, :], reads=[Bxr[m]], writes=[ob])
        if t + 1 < TOK // T:
            emit_loads(t + 1)
        if do_proj:
            proj_stage(C, S, d["w_in"], d["projT"], xb, Bxb, stg, tok, out_bufs)


NSEG = SEQ // 512
TWO_PI_1 = 6.28125
TWO_PI_2 = 2.0 * math.pi - 6.28125
PI_LO = 3.1415925


def phase_B(C, d, out_bufs, do_attn=True, do_ssm=True):
    S0 = C.psum("b_S0", [128, 1024], F32); BS0 = [Buf(), Buf()]
    S1 = C.psum("b_S1", [128, 1024], F32); BS1 = [Buf(), Buf()]
    O = [(C.psum(f"b_O{a}", [128, 512], F32), Buf()) for a in range(2)]
    Lb = C.psum("b_L", [128, 512], F32); BL = Buf()
    Fb = C.psum("b_F", [128, 512], F32); _bf = Buf(); BF = [_bf, _bf]
    Sset = [(S0, BS0), (S1, BS1)]

    onesb = C.sbuf("b_ones", [128, 128], BF16); Bones = Buf()
    C.op("dve", lambda E: E.memset(onesb[:], 1.0), writes=[Bones])
    onesf = C.sbuf("b_onesf", [128, 128], F32); Bonesf = Buf()
    C.op("dve", lambda E: E.memset(onesf[:], 1.0 / 128.0), writes=[Bonesf])
    identf = C.sbuf("b_identf", [128, 128], F32); Bidf = Buf()
    C.dma("sp", identf[:], d["ident"], writes=[Bidf])
    identb = C.sbuf("b_identb", [128, 128], BF16); Bidb = Buf()
    C.op("dve", lambda E: E.tensor_copy(out=identb[:], in_=identf[:]), reads=[Bidf], writes=[Bidb])
    cvec, Bcvec = load_vec(C, "b_cvec", d["cvec"], 4)

    steps = []
    if do_ssm:
        steps = ssm_setup(C, d, out_bufs, Fb, BF, O[1], identf, Bidf)

    if do_attn:
        qT = C.sbuf("b_qT", [128, SEQ], BF16); Bq = [Buf() for _ in range(4)]
        kT = C.sbuf("b_kT", [128, SEQ], BF16); Bk = [Buf() for _ in range(4)]
        V = C.sbuf("b_V", [128, 128, 128], BF16); BV = [Buf() for _ in range(16)]
        Vf = V[:, :, :].rearrange("p b d -> p (b d)")
        for i in range(4):
            C.dma("sp", qT[:, i * 4096:(i + 1) * 4096], d["qT"][:, i * 4096:(i + 1) * 4096], writes=[Bq[i]])
            C.dma("sp", kT[:, i * 4096:(i + 1) * 4096], d["kT"][:, i * 4096:(i + 1) * 4096], writes=[Bk[i]])
        qm = C.sbuf("b_qm", [128, 4], F32); Bqm = Buf()
        qkmax = C.sbuf("b_qkmax", [128, 64], F32); Bqkmax = Buf()
        C.push_scope()
        vst = [(C.sbuf(f"b_vst{i}", [128, 1024], BF16), Buf()) for i in range(2)]
        ptb = Fb.bitcast(BF16)
        for g in range(16):
            vs, bvs = vst[g % 2]
            C.dma("sp", vs[:], d["vT"][:, g * 1024:(g + 1) * 1024], writes=[bvs])
            for i in range(8):
                C.op("pe", lambda E, vs=vs, i=i: E.transpose(out=ptb[:, i * 128:(i + 1) * 128], in_=vs[:, i * 128:(i + 1) * 128],
                                                              identity=identb[:]),
                     reads=[bvs, Bidb], writes=BF)
            C.op("dve", lambda E, g=g: E.tensor_copy(out=Vf[:, g * 1024:(g + 1) * 1024], in_=ptb[:, :]),
                 reads=BF, writes=[BV[g]])
        sqb = [(C.sbuf(f"b_sqb{i}", [128, 512], BF16), Buf()) for i in range(2)]
        for which, (src, Bsrc) in enumerate(((qT, Bq), (kT, Bk))):
            for ch in range(32):
                sq, bsq = sqb[ch % 2]
                ps, pb = O[ch % 2]
                C.op("act", lambda E, sq=sq, src=src, ch=ch: E.activation(out=sq[:], in_=src[:, ch * 512:(ch + 1) * 512], func=ACT.Square),
                     reads=[Bsrc[ch // 8]], writes=[bsq])
                C.op("pe", _mm(ps[:, :], onesb[:, :], sq[:, :], True, True), reads=[Bones, bsq], writes=[pb])
                col = which * 32 + ch
                C.op("dve", lambda E, ps=ps, col=col: E.tensor_reduce(out=qkmax[:, col:col + 1], in_=ps[:], axis=AX.X, op=ALU.max),
                     reads=[pb], writes=[Bqkmax])
        C.op("dve", lambda E: E.tensor_reduce(out=qm[:, 0:1], in_=qkmax[:, 0:32], axis=AX.X, op=ALU.max), reads=[Bqkmax], writes=[Bqm])
        C.op("dve", lambda E: E.tensor_reduce(out=qm[:, 1:2], in_=qkmax[:, 32:64], axis=AX.X, op=ALU.max), reads=[Bqkmax], writes=[Bqm])
        C.op("dve", lambda E: E.tensor_tensor(out=qm[:, 2:3], in0=qm[:, 0:1], in1=qm[:, 1:2], op=ALU.mult), reads=[Bqm], writes=[Bqm])
        C.op("act", lambda E: E.activation(out=qm[:, 3:4], in_=qm[:, 2:3], func=ACT.Sqrt), reads=[Bqm], writes=[Bqm])
        C.pop_scope()
        negc = C.sbuf("b_negc", [128, 1], F32); Bnegc = Buf()
        C.op("dve", lambda E: E.tensor_scalar(out=negc[:], in0=qm[:, 3:4], scalar1=-1.02 * 0.125, scalar2=None, op0=ALU.mult),
             reads=[Bqm], writes=[Bnegc])
        lamv = C.sbuf("b_lamv", [128, 4, 64], F32); Blamv = Buf()
        C.dma("sp", lamv[:], d["lamv"], writes=[Blamv])
        lt = C.sbuf("b_lt", [128, 2, 64], F32); Blt = Buf()
        ls = C.sbuf("b_ls", [128, 8], F32); Bls = Buf()
        for i in range(2):
            C.op("dve", lambda E, i=i: E.tensor_tensor(out=lt[:, i, :], in0=lamv[:, 2 * i, :], in1=lamv[:, 2 * i + 1, :], op=ALU.mult),
                 reads=[Blamv], writes=[Blt])
            C.op("dve", lambda E, i=i: E.tensor_reduce(out=ls[:, i:i + 1], in_=lt[:, i, :], axis=AX.X, op=ALU.add), reads=[Blt], writes=[Bls])
        C.op("act", lambda E: E.activation(out=ls[:, 2:4], in_=ls[:, 0:2], func=ACT.Exp), reads=[Bls], writes=[Bls])
        C.op("dve", lambda E: E.tensor_tensor(out=ls[:, 4:5], in0=ls[:, 2:3], in1=ls[:, 3:4], op=ALU.subtract), reads=[Bls], writes=[Bls])
        C.op("dve", lambda E: E.tensor_tensor(out=ls[:, 5:6], in0=ls[:, 4:5], in1=cvec[:, 0:1], op=ALU.add), reads=[Bls, Bcvec], writes=[Bls])
        C.op("dve", lambda E: E.tensor_scalar(out=ls[:, 6:7], in0=ls[:, 5:6], scalar1=-1.0, scalar2=None, op0=ALU.mult), reads=[Bls], writes=[Bls])
        neglam = ls[:, 6:7]
        gat, Bgat = load_vec(C, "b_gat", d["gattn"], 1)
        C.op("dve", lambda E: E.tensor_tensor(out=ls[:, 7:8], in0=gat[:, 0:1], in1=cvec[:, 1:2], op=ALU.mult), reads=[Bgat, Bcvec, Bls], writes=[Bls])
        gcoef = ls[:, 7:8]
        selT = C.sbuf("b_sel", [128, 2, 128], F32); Bsel = Buf()
        C.op("dve", lambda E: E.memset(selT[:], 0.0), writes=[Bsel])
        C.op("dve", lambda E: E.memset(selT[0:1, 0, :], 1.0), writes=[Bsel])
        C.op("dve", lambda E: E.memset(selT[64:65, 1, :], 1.0), writes=[Bsel])

        Pset = [(C.sbuf(f"b_P{i}", [128, 1024], BF16), Buf()) for i in range(3)]
        obs = [(C.sbuf(f"b_ob{i}", [128, 512], BF16), Buf()) for i in range(2)]
        fin = [[(C.sbuf(f"b_fin{j}_{i}", [128, 512], F32), Buf()) for i in range(7)] for j in range(1)]
        items = []
        for qb in range(SEQ // 512):
            nkb = 4 * qb + 4
            for kb in range(nkb):
                items.append((qb, kb, nkb))
        pending = []

        def emit_qk(i):
            qb, kb, nkb = items[i]
            S, BS = Sset[i % 2]
            col0 = max(0, 128 * (kb - 4 * qb))
            q0 = qb * 512
            for a in range(2):
                C.op("pe", _mm(S[:, a * 512 + col0:(a + 1) * 512], kT[a * 64:(a + 1) * 64, kb * 128:(kb + 1) * 128],
                               qT[a * 64:(a + 1) * 64, q0 + col0:q0 + 512], True, True),
                     reads=[Bk[(kb * 128) // 4096], Bq[q0 // 4096]], writes=[BS[a]])

        def finalize(qb, i):
            q0 = qb * 512
            (Lsb, bLsb), (Os0, bOs0), (Os1, bOs1), (o, bo), (sq, bsq), (rs, brs), (r_, br_) = fin[0]
            Os = [(Os0, bOs0), (Os1, bOs1)]
            C.op("dve", lambda E: E.tensor_copy(out=Lsb[:, :], in_=Lb[:, :]), reads=[BL], writes=[bLsb])
            C.op("act", lambda E: E.activation(out=Os0[:], in_=O[0][0][:], func=ACT.Copy), reads=[O[0][1]], writes=[bOs0])
            C.op("dve", lambda E: E.tensor_copy(out=Os1[:], in_=O[1][0][:]), reads=[O[1][1]], writes=[bOs1])

            def stage_bcast(a):
                def f():
                    C.op("pe", _mm(Fb[:, :], selT[:, a, :], Lsb[:, :], True, True), reads=[Bsel, bLsb], writes=BF)
                    C.op("dve", lambda E: E.reciprocal(out=r_[:], in_=Fb[:]), reads=BF, writes=[br_])
                    C.op("dve", lambda E: E.tensor_tensor(out=Os[a][0][:], in0=Os[a][0][:], in1=r_[:], op=ALU.mult),
                         reads=[Os[a][1], br_], writes=[Os[a][1]])
                    if a == 1:
                        C.op("dve", lambda E: E.scalar_tensor_tensor(out=o[:], in0=Os1[:], scalar=neglam, in1=Os0[:], op0=ALU.mult, op1=ALU.add),
                             reads=[bOs0, bOs1, Bls], writes=[bo])
                return f

            def stage_sq():
                C.op("act", lambda E: E.activation(out=sq[:], in_=o[:], func=ACT.Square), reads=[bo], writes=[bsq])

            def stage_ms():
                C.op("pe", _mm(Fb[:, :], onesf[:, :], sq[:, :], True, True), reads=[Bonesf, bsq], writes=BF)
                C.op("dve", lambda E: E.tensor_scalar(out=rs[:], in0=Fb[:], scalar1=EPS, scalar2=None, op0=ALU.add), reads=BF, writes=[brs])

            def stage_sqrt():
                C.op("act", lambda E: E.activation(out=rs[:], in_=rs[:], func=ACT.Ln), reads=[brs], writes=[brs])
                C.op("act", lambda E: E.activation(out=rs[:], in_=rs[:], func=ACT.Exp, scale=-0.5), reads=[brs], writes=[brs])

            def stage_out():
                ob, bob = obs[qb % 2]
                C.op("dve", lambda E: E.scalar_tensor_tensor(out=ob[:], in0=o[:], scalar=gcoef, in1=rs[:], op0=ALU.mult, op1=ALU.mult),
                     reads=[bo, brs, Bls], writes=[bob])
                dst = Buf()
                out_bufs.append(dst)
                C.dma("sp", d["attnT"][:, q0:q0 + 512], ob[:], reads=[bob], writes=[dst])
            dls = (1, 2, 3, 4, 5, 6) if qb == 0 else (3, 6, 9, 11, 13, 15)
            for dl, fn in zip(dls, (stage_bcast(0), stage_bcast(1), stage_sq, stage_ms, stage_sqrt, stage_out)):
                pending.append((i + dl, fn))
            pending.sort(key=lambda x: x[0])

        nsteps = len(steps)
        every = max(1, len(items) // max(1, nsteps + 4))

        def emit_exp(i):
            qb, kb, nkb = items[i]
            S, BS = Sset[i % 2]
            P, BP = Pset[i % 3]
            col0 = max(0, 128 * (kb - 4 * qb))
            Sv = S[:, :].rearrange("p (a c) -> p a c", a=2)[:, :, col0:512]
            Pv = P[:, :].rearrange("p (a c) -> p a c", a=2)[:, :, col0:512]
            C.op("act", lambda E: E.activation(out=Pv, in_=Sv, func=ACT.Exp, bias=negc[:, 0:1], scale=0.125),
                 reads=[BS[0], BS[1], Bnegc], writes=[BP])
            if kb >= 4 * qb:
                Pm = P[64:128, :].rearrange("p (a c) -> p a c", a=2)[:, :, col0:col0 + 64]
                C.op("act", lambda E: E.activation(out=Pm, in_=Pm, func=ACT.Copy, scale=0.0), reads=[BP], writes=[BP])

        def emit_pv(i):
            qb, kb, nkb = items[i]
            P, BP = Pset[i % 3]
            col0 = max(0, 128 * (kb - 4 * qb))
            for a in range(2):
                for j in range(2):
                    C.op("pe", _mm(O[a][0][64 * j:64 * j + 64, col0:512], V[:, kb, 64 * j:64 * j + 64], P[:, a * 512 + col0:(a + 1) * 512],
                                   kb == 0, kb == nkb - 1),
                         reads=[BV[kb // 8], BP], writes=[O[a][1]])
            for a in range(2):
                C.op("pe", _mm(Lb[64 * a:64 * a + 64, col0:512], onesb[:, 0:64], P[:, a * 512 + col0:(a + 1) * 512], kb == 0, kb == nkb - 1),
                     reads=[Bones, BP], writes=[BL])
            if kb == nkb - 1:
                finalize(qb, i)

        emit_qk(0)
        for i in range(len(items)):
            if i + 1 < len(items):
                emit_qk(i + 1)
            while pending and pending[0][0] <= i:
                pending.pop(0)[1]()
            if steps and i % every == every - 1:
                for dl, fn in (steps.pop(0)() or ()):
                    pending.append((i + dl, fn))
                pending.sort(key=lambda x: x[0])
            emit_exp(i)
            if i >= 1:
                emit_pv(i - 1)
        emit_pv(len(items) - 1)
        while pending:
            pending.pop(0)[1]()
    while steps:
        for dl, fn in (steps.pop(0)() or ()):
            fn()


def ssm_setup(C, d, out_bufs, Fb, BF, tpbank, identf, Bidf):
    SG = 256
    tmpf = [(C.sbuf(f"s_tmp{i}", [128, SG], F32), Buf()) for i in range(10)]
    tcnt = [0]

    def nexttmp():
        r = tmpf[tcnt[0] % len(tmpf)]
        tcnt[0] += 1
        return r
    uT = C.sbuf("s_uT", [128, SEQ], BF16); Bu = [Buf() for _ in range(4)]
    for i in range(4):
        C.dma("sp", uT[:, i * 4096:(i + 1) * 4096], d["uT"][:, i * 4096:(i + 1) * 4096], writes=[Bu[i]])
    NP = 24
    pp = C.sbuf("s_pp", [128, NP, 4], F32); Bpp = Buf()
    ppi = C.sbuf("s_ppi", [128, 4], I32); Bppi = Buf()
    LR, LI, LDT, DT, TH, RHO, SIN, COS, ABR, ABI, DEN, NR, FR, FI, T0, T1, T2, KF, R0 = range(19)

    def col(i):
        return pp[:, i, :]
    C.dma("sp", col(LR), d["p_lr"], writes=[Bpp])
    C.dma("sp", col(LI), d["p_li"], writes=[Bpp])
    C.dma("sp", col(LDT), d["p_ldt"], writes=[Bpp])

    def v(fn):
        C.op("dve", fn, reads=[Bpp], writes=[Bpp])

    def a_(fn):
        C.op("act", fn, reads=[Bpp], writes=[Bpp])
    a_(lambda E: E.activation(out=col(DT), in_=col(LDT), func=ACT.Exp))
    v(lambda E: E.tensor_tensor(out=col(T0), in0=col(LR), in1=col(DT), op=ALU.mult))
    a_(lambda E: E.activation(out=col(RHO), in_=col(T0), func=ACT.Exp))
    v(lambda E: E.tensor_tensor(out=col(TH), in0=col(LI), in1=col(DT), op=ALU.mult))

    def sin_of(dst, src, shift):
        v(lambda E: E.tensor_scalar(out=col(T1), in0=col(src), scalar1=shift, scalar2=None, op0=ALU.add))
        v(lambda E: E.tensor_scalar(out=col(T2), in0=col(T1), scalar1=1.0 / (2.0 * math.pi), scalar2=None, op0=ALU.mult))
        C.op("dve", lambda E: E.tensor_copy(out=ppi[:], in_=col(T2)), reads=[Bpp], writes=[Bppi])
        C.op("dve", lambda E: E.tensor_copy(out=col(KF), in_=ppi[:]), reads=[Bppi], writes=[Bpp])
        v(lambda E: E.scalar_tensor_tensor(out=col(R0), in0=col(KF), scalar=-TWO_PI_1, in1=col(T1), op0=ALU.mult, op1=ALU.add))
        v(lambda E: E.scalar_tensor_tensor(out=col(R0), in0=col(KF), scalar=-TWO_PI_2, in1=col(R0), op0=ALU.mult, op1=ALU.add))
        v(lambda E: E.tensor_scalar(out=col(T2), in0=col(R0), scalar1=math.pi, scalar2=None, op0=ALU.is_gt))
        v(lambda E: E.scalar_tensor_tensor(out=col(R0), in0=col(T2), scalar=-2.0 * math.pi, in1=col(R0), op0=ALU.mult, op1=ALU.add))
        v(lambda E: E.tensor_scalar(out=col(T2), in0=col(R0), scalar1=-math.pi, scalar2=None, op0=ALU.is_lt))
        v(lambda E: E.scalar_tensor_tensor(out=col(R0), in0=col(T2), scalar=2.0 * math.pi, in1=col(R0), op0=ALU.mult, op1=ALU.add))
        v(lambda E: E.tensor_scalar(out=col(R0), in0=col(R0), scalar1=PI_LO, scalar2=-PI_LO, op0=ALU.min, op1=ALU.max))
        a_(lambda E: E.activation(out=col(dst), in_=col(R0), func=ACT.Sin))
    sin_of(SIN, TH, 0.0)
    sin_of(COS, TH, math.pi / 2.0)
    v(lambda E: E.tensor_tensor(out=col(ABR), in0=col(RHO), in1=col(COS), op=ALU.mult))
    v(lambda E: E.tensor_tensor(out=col(ABI), in0=col(RHO), in1=col(SIN), op=ALU.mult))
    v(lambda E: E.tensor_tensor(out=col(T0), in0=col(LR), in1=col(LR), op=ALU.mult))
    v(lambda E: E.tensor_tensor(out=col(T1), in0=col(LI), in1=col(LI), op=ALU.mult))
    v(lambda E: E.tensor_tensor(out=col(DEN), in0=col(T0), in1=col(T1), op=ALU.add))
    v(lambda E: E.reciprocal(out=col(DEN), in_=col(DEN)))
    v(lambda E: E.tensor_scalar(out=col(NR), in0=col(ABR), scalar1=-1.0, scalar2=None, op0=ALU.add))
    v(lambda E: E.tensor_tensor(out=col(T0), in0=col(NR), in1=col(LR), op=ALU.mult))
    v(lambda E: E.tensor_tensor(out=col(T1), in0=col(ABI), in1=col(LI), op=ALU.mult))
    v(lambda E: E.tensor_tensor(out=col(T0), in0=col(T0), in1=col(T1), op=ALU.add))
    v(lambda E: E.tensor_tensor(out=col(FR), in0=col(T0), in1=col(DEN), op=ALU.mult))
    v(lambda E: E.tensor_tensor(out=col(T0), in0=col(ABI), in1=col(LR), op=ALU.mult))
    v(lambda E: E.tensor_tensor(out=col(T1), in0=col(NR), in1=col(LI), op=ALU.mult))
    v(lambda E: E.tensor_tensor(out=col(T0), in0=col(T0), in1=col(T1), op=ALU.subtract))
    v(lambda E: E.tensor_tensor(out=col(FI), in0=col(T0), in1=col(DEN), op=ALU.mult))

    cosT = [C.sbuf(f"s_cos{g}", [128, SG], F32) for g in range(4)]
    sinT = [C.sbuf(f"s_sin{g}", [128, SG], F32) for g in range(4)]
    rhoT = [C.sbuf(f"s_rho{g}", [128, SG], F32) for g in range(4)]
    Btab = [Buf() for _ in range(4)]
    stp = C.sbuf("s_stp", [128, 4, 8], F32)
    Bstp = [Buf() for _ in range(4)]
    for g in range(4):
        cg_, sg_ = pp[:, COS, g:g + 1], pp[:, SIN, g:g + 1]
        rd = [Bpp, Btab[g], Bstp[g]]
        C.op("dve", lambda E, g=g: E.memset(cosT[g][:, 0:1], 1.0), writes=[Btab[g]])
        C.op("dve", lambda E, g=g: E.memset(sinT[g][:, 0:1], 0.0), writes=[Btab[g]])
        C.op("dve", lambda E, g=g: E.memset(rhoT[g][:], 1.0), writes=[Btab[g]])
        C.op("dve", lambda E, g=g: E.tensor_scalar(out=rhoT[g][:], in0=rhoT[g][:], scalar1=pp[:, RHO, g:g + 1], scalar2=None, op0=ALU.mult),
             reads=rd, writes=[Btab[g]])
        C.op("dve", lambda E, g=g, cg_=cg_: E.tensor_copy(out=stp[:, g, 0:1], in_=cg_), reads=rd, writes=[Bstp[g]])
        C.op("dve", lambda E, g=g, sg_=sg_: E.tensor_copy(out=stp[:, g, 1:2], in_=sg_), reads=rd, writes=[Bstp[g]])
        Lq = 1
        while True:
            cL, sL = stp[:, g, 0:1], stp[:, g, 1:2]
            if Lq < SG:
                n = min(Lq, SG - Lq)
                t1, b1 = nexttmp()
                C.op("dve", lambda E, g=g, t1=t1, n=n, sL=sL: E.tensor_scalar(out=t1[:, 0:n], in0=sinT[g][:, 0:n], scalar1=sL, scalar2=None, op0=ALU.mult),
                     reads=rd, writes=[b1])
                C.op("dve", lambda E, g=g, t1=t1, n=n, cL=cL, Lq=Lq: E.scalar_tensor_tensor(out=cosT[g][:, Lq:Lq + n], in0=cosT[g][:, 0:n], scalar=cL,
                                                                                         in1=t1[:, 0:n], op0=ALU.mult, op1=ALU.subtract),
                     reads=rd + [b1], writes=[Btab[g]])
                C.op("dve", lambda E, g=g, t1=t1, n=n, sL=sL: E.tensor_scalar(out=t1[:, 0:n], in0=cosT[g][:, 0:n], scalar1=sL, scalar2=None, op0=ALU.mult),
                     reads=rd + [b1], writes=[b1])
                C.op("dve", lambda E, g=g, t1=t1, n=n, cL=cL, Lq=Lq: E.scalar_tensor_tensor(out=sinT[g][:, Lq:Lq + n], in0=sinT[g][:, 0:n], scalar=cL,
                                                                                         in1=t1[:, 0:n], op0=ALU.mult, op1=ALU.add),
                     reads=rd + [b1], writes=[Btab[g]])
            if Lq >= SG:
                break
            C.op("dve", lambda E, g=g, cL=cL: E.tensor_tensor(out=stp[:, g, 2:3], in0=cL, in1=cL, op=ALU.mult), reads=rd, writes=[Bstp[g]])
            C.op("dve", lambda E, g=g, sL=sL: E.tensor_tensor(out=stp[:, g, 3:4], in0=sL, in1=sL, op=ALU.mult), reads=rd, writes=[Bstp[g]])
            C.op("dve", lambda E, g=g, cL=cL, sL=sL: E.tensor_tensor(out=stp[:, g, 4:5], in0=cL, in1=sL, op=ALU.mult), reads=rd, writes=[Bstp[g]])
            C.op("dve", lambda E, g=g: E.tensor_tensor(out=stp[:, g, 0:1], in0=stp[:, g, 2:3], in1=stp[:, g, 3:4], op=ALU.subtract), reads=rd, writes=[Bstp[g]])
            C.op("dve", lambda E, g=g: E.tensor_scalar(out=stp[:, g, 1:2], in0=stp[:, g, 4:5], scalar1=2.0, scalar2=None, op0=ALU.mult), reads=rd, writes=[Bstp[g]])
            Lq *= 2

    BbT = [[C.sbuf(f"s_bbT{g}{ri}", [128, 128], BF16) for ri in range(2)] for g in range(4)]
    CT = [[C.sbuf(f"s_cT{g}{ri}", [128, 128], BF16) for ri in range(2)] for g in range(4)]
    Bmat = [Buf() for _ in range(4)]
    Z = [(C.sbuf(f"s_Z{i}", [128, 128], F32), Buf()) for i in range(4)]
    tp, btp = tpbank
    for g in range(4):
        for i in range(4):
            C.op("dve", lambda E, i=i: E.memset(Z[i][0][:], 0.0), writes=[Z[i][1]])
        for gl in range(2):
            gg = 2 * g + gl
            rs = slice(gl * 64, gl * 64 + 64)
            cs = slice(32 * g + 16 * gl, 32 * g + 16 * gl + 16)
            C.dma("sp", Z[0][0][rs, cs], d["b_re"][gg], writes=[Z[0][1]])
            C.dma("sp", Z[1][0][rs, cs], d["b_im"][gg], writes=[Z[1][1]])
            C.dma("sp", Z[2][0][rs, cs], d["c_reT"][gg], writes=[Z[2][1]])
            C.dma("sp", Z[3][0][rs, cs], d["c_imT"][gg], writes=[Z[3][1]])
        fr, fi = pp[:, FR, g:g + 1], pp[:, FI, g:g + 1]
        t1, b1 = nexttmp()
        t2, b2 = nexttmp()
        C.op("dve", lambda E, t1=t1, fi=fi: E.tensor_scalar(out=t1[:, 0:128], in0=Z[1][0][:], scalar1=fi, scalar2=None, op0=ALU.mult),
             reads=[Z[1][1], Bpp], writes=[b1])
        C.op("dve", lambda E, t1=t1, fr=fr: E.scalar_tensor_tensor(out=t1[:, 0:128], in0=Z[0][0][:], scalar=fr, in1=t1[:, 0:128],
                                                                   op0=ALU.mult, op1=ALU.subtract),
             reads=[Z[0][1], Bpp, b1], writes=[b1])
        C.op("dve", lambda E, t2=t2, fi=fi: E.tensor_scalar(out=t2[:, 0:128], in0=Z[0][0][:], scalar1=fi, scalar2=None, op0=ALU.mult),
             reads=[Z[0][1], Bpp], writes=[b2])
        C.op("dve", lambda E, t2=t2, fr=fr: E.scalar_tensor_tensor(out=t2[:, 0:128], in0=Z[1][0][:], scalar=fr, in1=t2[:, 0:128],
                                                                   op0=ALU.mult, op1=ALU.add),
             reads=[Z[1][1], Bpp, b2], writes=[b2])
        for ri, (tt, bt) in enumerate(((t1, b1), (t2, b2))):
            C.op("pe", lambda E, tt=tt: E.transpose(out=tp[:, 0:128], in_=tt[:, 0:128], identity=identf[:]), reads=[bt, Bidf], writes=[btp])
            C.op("act", lambda E, g=g, ri=ri: E.activation(out=BbT[g][ri][:], in_=tp[:, 0:128], func=ACT.Copy), reads=[btp], writes=[Bmat[g]])
        C.op("act", lambda E, g=g: E.activation(out=CT[g][0][:], in_=Z[2][0][:], func=ACT.Copy), reads=[Z[2][1]], writes=[Bmat[g]])
        C.op("act", lambda E, g=g: E.activation(out=CT[g][1][:], in_=Z[3][0][:], func=ACT.Copy, scale=-1.0), reads=[Z[3][1]], writes=[Bmat[g]])
    dsk, Bdsk = load_vec(C, "s_dsk", d["dsk"], 1)

    init = C.sbuf("s_init", [128, 4, 2, 2], F32)
    Binit = [[Buf(), Buf()] for _ in range(4)]
    C.op("dve", lambda E: E.memset(init[:], 0.0), writes=[b for bb in Binit for b in bb])
    gin = [[(C.sbuf(f"s_gin{i}{ri}", [128, SG], F32), Buf()) for ri in range(2)] for i in range(2)]
    gst = [[(C.sbuf(f"s_g{i}{ri}", [128, SG], F32), Buf()) for ri in range(2)] for i in range(2)]
    hbf = [[[(C.sbuf(f"s_h{i}{g}{ri}", [128, SG], BF16), Buf()) for ri in range(2)] for g in range(4)] for i in range(2)]
    yob = [(C.sbuf(f"s_yo{i}", [128, SG], BF16), Buf()) for i in range(2)]
    ytmp = [[(C.sbuf(f"s_yt{i}{j}", [128, SG], F32), Buf()) for j in range(2)] for i in range(2)]
    psR, psI = Fb[:, 0:SG], Fb[:, SG:2 * SG]
    psY = Fb[:, 0:SG]
    itc = [0]

    def step_bu(seg, g):
        def f():
            t0 = seg * SG
            bu_ = Bu[t0 // 4096]
            par = itc[0] % 2
            itc[0] += 1
            for ri in range(2):
                C.op("pe", _mm(Fb[:, ri * SG:(ri + 1) * SG], BbT[g][ri][:, :], uT[:, t0:t0 + SG], True, True),
                     reads=[Bmat[g], bu_], writes=[BF[ri]])
            m1, bm1 = nexttmp(); m2, bm2 = nexttmp(); m3, bm3 = nexttmp(); m4, bm4 = nexttmp()
            C.op("dve", lambda E: E.tensor_tensor(out=m1[:], in0=psR, in1=cosT[g][:], op=ALU.mult), reads=[BF[0], Btab[g]], writes=[bm1])
            C.op("dve", lambda E: E.tensor_tensor(out=m4[:], in0=psR, in1=sinT[g][:], op=ALU.mult), reads=[BF[0], Btab[g]], writes=[bm4])
            C.op("dve", lambda E: E.tensor_tensor(out=m2[:], in0=psI, in1=sinT[g][:], op=ALU.mult), reads=[BF[1], Btab[g]], writes=[bm2])
            C.op("dve", lambda E: E.tensor_tensor(out=m3[:], in0=psI, in1=cosT[g][:], op=ALU.mult), reads=[BF[1], Btab[g]], writes=[bm3])
            (gir, bgir), (gii, bgii) = gin[par]
            C.op("dve", lambda E: E.tensor_tensor(out=gir[:], in0=m1[:], in1=m2[:], op=ALU.add), reads=[bm1, bm2], writes=[bgir])
            C.op("dve", lambda E: E.tensor_tensor(out=gii[:], in0=m3[:], in1=m4[:], op=ALU.subtract), reads=[bm3, bm4], writes=[bgii])
            (gr, bgr), (gi, bgi) = gst[par]
            ip = seg % 2
            C.op("dve", lambda E: E.tensor_tensor_scan(out=gr[:], data0=rhoT[g][:], data1=gir[:], initial=init[:, g, ip, 0:1],
                                                       op0=ALU.mult, op1=ALU.add),
                 reads=[Btab[g], bgir, Binit[g][ip]], writes=[bgr])
            C.op("dve", lambda E: E.tensor_tensor_scan(out=gi[:], data0=rhoT[g][:], data1=gii[:], initial=init[:, g, ip, 1:2],
                                                       op0=ALU.mult, op1=ALU.add),
                 reads=[Btab[g], bgii, Binit[g][ip]], writes=[bgi])
            cS, sS = stp[:, g, 0:1], stp[:, g, 1:2]
            nx = 1 - ip
            rdn = [bgr, bgi, Bstp[g], Binit[g][nx]]
            C.op("dve", lambda E: E.tensor_scalar(out=stp[:, g, 5:6], in0=gi[:, SG - 1:SG], scalar1=sS, scalar2=None, op0=ALU.mult),
                 reads=rdn, writes=[Bstp[g]])
            C.op("dve", lambda E: E.scalar_tensor_tensor(out=init[:, g, nx, 0:1], in0=gr[:, SG - 1:SG], scalar=cS, in1=stp[:, g, 5:6],
                                                         op0=ALU.mult, op1=ALU.subtract),
                 reads=rdn, writes=[Binit[g][nx]])
            C.op("dve", lambda E: E.tensor_scalar(out=stp[:, g, 6:7], in0=gr[:, SG - 1:SG], scalar1=sS, scalar2=None, op0=ALU.mult),
                 reads=rdn, writes=[Bstp[g]])
            C.op("dve", lambda E: E.scalar_tensor_tensor(out=init[:, g, nx, 1:2], in0=gi[:, SG - 1:SG], scalar=cS, in1=stp[:, g, 6:7],
                                                         op0=ALU.mult, op1=ALU.add),
                 reads=rdn, writes=[Binit[g][nx]])
            n1, bn1 = nexttmp(); n2, bn2 = nexttmp(); n3, bn3 = nexttmp(); n4, bn4 = nexttmp()
            C.op("pool", lambda E: E.tensor_tensor(out=n1[:], in0=gr[:], in1=cosT[g][:], op=ALU.mult), reads=[bgr, Btab[g]], writes=[bn1])
            C.op("pool", lambda E: E.tensor_tensor(out=n2[:], in0=gi[:], in1=sinT[g][:], op=ALU.mult), reads=[bgi, Btab[g]], writes=[bn2])
            C.op("pool", lambda E: E.tensor_tensor(out=n3[:], in0=gr[:], in1=sinT[g][:], op=ALU.mult), reads=[bgr, Btab[g]], writes=[bn3])
            C.op("pool", lambda E: E.tensor_tensor(out=n4[:], in0=gi[:], in1=cosT[g][:], op=ALU.mult), reads=[bgi, Btab[g]], writes=[bn4])
            (hr, bhr), (hi, bhi) = hbf[seg % 2][g]
            C.op("pool", lambda E: E.tensor_tensor(out=hr[:], in0=n1[:], in1=n2[:], op=ALU.subtract), reads=[bn1, bn2], writes=[bhr])
            C.op("pool", lambda E: E.tensor_tensor(out=hi[:], in0=n3[:], in1=n4[:], op=ALU.add), reads=[bn3, bn4], writes=[bhi])
        return f

    def step_y(seg):
        def f():
            t0 = seg * SG
            bu_ = Bu[t0 // 4096]
            for g in range(4):
                (hr, bhr), (hi, bhi) = hbf[seg % 2][g]
                C.op("pe", _mm(psY, CT[g][0][:, :], hr[:, :], g == 0, False), reads=[Bmat[g], bhr], writes=[BF[0]])
                C.op("pe", _mm(psY, CT[g][1][:, :], hi[:, :], False, g == 3), reads=[Bmat[g], bhi], writes=[BF[0]])
            yt, byt = ytmp[seg % 2][0]
            w, bw = ytmp[seg % 2][1]
            C.op("dve", lambda E: E.scalar_tensor_tensor(out=yt[:], in0=uT[:, t0:t0 + SG], scalar=dsk[:, 0:1], in1=psY, op0=ALU.mult, op1=ALU.add),
                 reads=[bu_, Bdsk, BF[0]], writes=[byt])
            C.op("pool", lambda E: E.tensor_tensor(out=w[:], in0=yt[:], in1=yt[:], op=ALU.mult), reads=[byt], writes=[bw])
            C.op("pool", lambda E: E.tensor_scalar(out=w[:], in0=w[:], scalar1=0.044715, scalar2=1.0, op0=ALU.mult, op1=ALU.add), reads=[bw], writes=[bw])
            C.op("pool", lambda E: E.tensor_tensor(out=w[:], in0=w[:], in1=yt[:], op=ALU.mult), reads=[bw, byt], writes=[bw])

            def part2():
                C.op("act", lambda E: E.activation(out=w[:], in_=w[:], func=ACT.Exp, scale=-1.5957691216057308), reads=[bw], writes=[bw])
                C.op("pool", lambda E: E.tensor_scalar(out=w[:], in0=w[:], scalar1=1.0, scalar2=None, op0=ALU.add), reads=[bw], writes=[bw])
                C.op("dve", lambda E: E.reciprocal(out=w[:], in_=w[:]), reads=[bw], writes=[bw])
                yo, byo = yob[seg % 2]
                C.op("pool", lambda E: E.tensor_tensor(out=yo[:], in0=yt[:], in1=w[:], op=ALU.mult), reads=[bw, byt], writes=[byo])
                dst = Buf()
                out_bufs.append(dst)
                C.dma("sp", d["yT"][:, t0:t0 + SG], yo[:], reads=[byo], writes=[dst])
            return [(5, part2)]
        return f

    steps = []
    nseg = SEQ // SG
    for seg in range(nseg):
        for g in range(4):
            steps.append(step_bu(seg, g))
            if g == 0 and seg > 0:
                steps.append(step_y(seg - 1))
    steps.append(step_y(nseg - 1))
    return steps


from concourse.bass_utils import run_bass_kernel_spmd
import ml_dtypes

NCORES = 8
_BF = ml_dtypes.bfloat16
_PROGS = {}


def _din(nc, name, shape, dt=F32):
    return nc.dram_tensor(name, list(shape), dt, kind="ExternalInput").ap()


def _dout(nc, name, shape, dt=F32):
    return nc.dram_tensor(name, list(shape), dt, kind="ExternalOutput").ap()


def _vec128(v):
    return np.ascontiguousarray(np.asarray(v, np.float32).reshape(-1, 128).T)


def build_A():
    nc = bass.Bass("TRN2", target_bir_lowering=False)
    d = dict(xT=_din(nc, "xT", [D, TOK]), w_in=_din(nc, "w_in", [D, 4096]), projT=_dout(nc, "projT", [4096, TOK], BF16))
    C = Ctx(nc)
    S = Shared(C)
    outs = []
    phase_A(C, S, d, outs)
    C.finish(outs, "sp")
    C.emit()
    return nc


def build_B():
    nc = bass.Bass("TRN2", target_bir_lowering=False)
    d = dict(qT=_din(nc, "qT", [128, SEQ], BF16), kT=_din(nc, "kT", [128, SEQ], BF16), vT=_din(nc, "vT", [128, SEQ], BF16),
             uT=_din(nc, "uT", [128, SEQ], BF16), ident=_din(nc, "ident", [128, 128]), lamv=_din(nc, "lamv", [128, 4, 64]),
             cvec=_din(nc, "cvec", [128, 4]), gattn=_din(nc, "gattn", [128, 1]),
             p_lr=_din(nc, "p_lr", [128, 4]), p_li=_din(nc, "p_li", [128, 4]), p_ldt=_din(nc, "p_ldt", [128, 4]),
             b_re=_din(nc, "b_re", [8, 64, 16]), b_im=_din(nc, "b_im", [8, 64, 16]), c_reT=_din(nc, "c_reT", [8, 64, 16]),
             c_imT=_din(nc, "c_imT", [8, 64, 16]), dsk=_din(nc, "dsk", [128, 1]),
             attnT=_dout(nc, "attnT", [128, SEQ], BF16), yT=_dout(nc, "yT", [128, SEQ], BF16))
    C = Ctx(nc)
    outs = []
    phase_B(C, d, outs)
    C.finish(outs, "sp")
    C.emit()
    return nc


def build_C(do_proj):
    nc = bass.Bass("TRN2", target_bir_lowering=False)
    d = dict(xres=_din(nc, "xres", [D, TOK]), attnT=_din(nc, "attnT", [1024, TOK], BF16), yT=_din(nc, "yT", [1024, TOK], BF16),
             glu_w=_din(nc, "glu_w", [1024, 1024]), glu_b=_din(nc, "glu_b", [128, 8]), ssm_g=_din(nc, "ssm_g", [128, 8]),
             w_out=_din(nc, "w_out", [D, D]), ln1_g=_din(nc, "ln1_g", [128, 16]), ln1_b=_din(nc, "ln1_b", [128, 16]),
             w_up=_din(nc, "w_up", [D, DFF]), w_down=_din(nc, "w_down", [DFF, D]), ln2_g=_din(nc, "ln2_g", [128, 16]),
             ln2_b=_din(nc, "ln2_b", [128, 16]), bd16=_din(nc, "bd16", [128, 128]), xout=_dout(nc, "xout", [D, TOK]))
    if do_proj:
        d["w_in"] = _din(nc, "w_in", [D, 4096])
        d["projT"] = _dout(nc, "projT", [4096, TOK], BF16)
    C = Ctx(nc)
    S = Shared(C)
    outs = []
    phase_C(C, S, d, do_proj, outs)
    C.finish(outs, "sp")
    C.emit()
    return nc


def _host_B_inputs(inp, l, projT_full, h):
    lam_init = 0.8 - 0.6 * math.exp(-0.3 * l)
    m = {}
    m["qT"] = np.ascontiguousarray(projT_full[h * 128:(h + 1) * 128])
    m["kT"] = np.ascontiguousarray(projT_full[1024 + h * 128:1024 + (h + 1) * 128])
    m["vT"] = np.ascontiguousarray(projT_full[2048 + h * 128:2048 + (h + 1) * 128])
    m["uT"] = np.ascontiguousarray(projT_full[3072 + h * 128:3072 + (h + 1) * 128])
    m["ident"] = np.eye(128, dtype=np.float32)
    lam4 = np.stack([inp["lambda_q1"][l], inp["lambda_k1"][l], inp["lambda_q2"][l], inp["lambda_k2"][l]])
    m["lamv"] = np.ascontiguousarray(np.broadcast_to(lam4[None], (128, 4, 64))).astype(np.float32)
    cv = np.zeros((128, 4), np.float32)
    cv[:, 0] = lam_init
    cv[:, 1] = 1.0 - lam_init
    m["cvec"] = cv
    m["gattn"] = np.ascontiguousarray(inp["attn_norm_g"][l][h * 128:(h + 1) * 128].reshape(128, 1)).astype(np.float32)
    gs = slice(8 * h, 8 * h + 8)

    def pcol(a):
        return np.ascontiguousarray(np.asarray(a).reshape(4, 128).T).astype(np.float32)
    m["p_lr"] = pcol(inp["ssm_lambda_re"][l][gs])
    m["p_li"] = pcol(inp["ssm_lambda_im"][l][gs])
    m["p_ldt"] = pcol(np.broadcast_to(inp["ssm_log_dt"][l][gs][:, None], (8, 64)))
    m["b_re"] = np.ascontiguousarray(inp["ssm_b_re"][l][gs]).astype(np.float32)
    m["b_im"] = np.ascontiguousarray(inp["ssm_b_im"][l][gs]).astype(np.float32)
    m["c_reT"] = np.ascontiguousarray(inp["ssm_c_re"][l][gs].transpose(0, 2, 1)).astype(np.float32)
    m["c_imT"] = np.ascontiguousarray(inp["ssm_c_im"][l][gs].transpose(0, 2, 1)).astype(np.float32)
    m["dsk"] = np.ascontiguousarray(inp["ssm_d"][l][h * 128:(h + 1) * 128].reshape(128, 1)).astype(np.float32)
    return m


def _host_C_inputs(inp, l, do_proj):
    m = dict(glu_w=np.ascontiguousarray(inp["glu_w"][l]), glu_b=_vec128(inp["glu_b"][l]), ssm_g=_vec128(inp["ssm_norm_g"][l]),
             w_out=np.ascontiguousarray(inp["w_out"][l]), ln1_g=_vec128(inp["ln1_g"][l]), ln1_b=_vec128(inp["ln1_b"][l]),
             w_up=np.ascontiguousarray(inp["w_up"][l]), w_down=np.ascontiguousarray(inp["w_down"][l]),
             ln2_g=_vec128(inp["ln2_g"][l]), ln2_b=_vec128(inp["ln2_b"][l]),
             bd16=np.kron(np.eye(8), np.ones((16, 16)) / 16).astype(np.float32))
    if do_proj:
        m["w_in"] = np.ascontiguousarray(inp["w_in"][l + 1])
    return m


def _prog(key, fn):
    if key not in _PROGS:
        _PROGS[key] = fn()
    return _PROGS[key]


def kernel(**inputs):
    inp = {k: np.asarray(v) for k, v in inputs.items()}
    cores = list(range(NCORES))
    x = inp["x"][0]
    xT = [np.ascontiguousarray(x[c * TOK:(c + 1) * TOK].T) for c in cores]
    res = run_bass_kernel_spmd(_prog("A", build_A), [dict(xT=xT[c], w_in=np.ascontiguousarray(inp["w_in"][0])) for c in cores],
                               core_ids=cores)
    projT = np.concatenate([res.results[c]["projT"] for c in cores], axis=1)
    for l in range(DEPTH):
        res = run_bass_kernel_spmd(_prog("B", build_B), [_host_B_inputs(inp, l, projT, h) for h in cores], core_ids=cores)
        attnT = np.concatenate([res.results[h]["attnT"] for h in cores], axis=0)
        yT = np.concatenate([res.results[h]["yT"] for h in cores], axis=0)
        do_proj = l + 1 < DEPTH
        common = _host_C_inputs(inp, l, do_proj)
        maps = []
        for c in cores:
            m = dict(common)
            m["xres"] = xT[c]
            m["attnT"] = np.ascontiguousarray(attnT[:, c * TOK:(c + 1) * TOK])
            m["yT"] = np.ascontiguousarray(yT[:, c * TOK:(c + 1) * TOK])
            maps.append(m)
        res = run_bass_kernel_spmd(_prog(("C", do_proj), lambda: build_C(do_proj)), maps, core_ids=cores)
        xT = [res.results[c]["xout"] for c in cores]
        if do_proj:
            projT = np.concatenate([res.results[c]["projT"] for c in cores], axis=1)
    out = np.concatenate([np.ascontiguousarray(xT[c].T) for c in cores], axis=0)[None]
    return out.astype(np.float32)
```
